# Optimizing a Trainium2 kernel written in Bass

```python
import math
import jax, jax.numpy as jnp
from jax import lax
import numpy as np

D_MODEL = 1024
BATCH = 8
SEQ = 2048
DEPTH = 2

RET_HEADS = 4
RET_DK = D_MODEL // RET_HEADS
RET_DV = 2 * RET_DK
RET_CHUNK = 128
ROPE_BASE = 10000.0
HGRN_HEADS = 8
HGRN_DK = D_MODEL // HGRN_HEADS
HGRN_DV = D_MODEL // HGRN_HEADS
HGRN_CHUNK = 64
LB_FLOOR = 1e-30
FNET_GROUPS = 4
FNET_WIDTH = D_MODEL
FNET_GROUP_DIM = FNET_WIDTH // FNET_GROUPS
D_FF = 2816
CONV_W = 3
N_BRANCH = 3
EPS = 1e-6

RET_QK_W = RET_HEADS * RET_DK
RET_V_W = RET_HEADS * RET_DV
HGRN_K_W = HGRN_HEADS * HGRN_DK
HGRN_V_W = HGRN_HEADS * HGRN_DV
D_IN = 2 * RET_QK_W + 2 * RET_V_W + 3 * HGRN_K_W + 2 * HGRN_V_W + FNET_WIDTH + N_BRANCH * D_MODEL

kernel_name = "hybrid_retention_hgrn2_fnet_encoder"


def _split_points():
    widths = (RET_QK_W, RET_QK_W, RET_V_W, RET_V_W,
              HGRN_K_W, HGRN_K_W, HGRN_K_W, HGRN_V_W, HGRN_V_W,
              FNET_WIDTH, N_BRANCH * D_MODEL)
    return tuple(int(c) for c in np.cumsum(widths)[:-1])


def _rms(x):
    xf = x.astype(jnp.float32)
    return xf * lax.rsqrt(jnp.mean(xf * xf, axis=-1, keepdims=True) + EPS)


def rms_norm(x, w):
    return (_rms(x) * w.astype(jnp.float32)).astype(x.dtype)


def rotary(x, positions):
    half = x.shape[-1] // 2
    inv_freq = ROPE_BASE ** (-jnp.arange(half, dtype=jnp.float32) / half)
    ang = positions.astype(jnp.float32)[..., None] * inv_freq
    cos = jnp.cos(ang)[:, :, None, :]
    sin = jnp.sin(ang)[:, :, None, :]
    x1 = x[..., :half].astype(jnp.float32)
    x2 = x[..., half:].astype(jnp.float32)
    return jnp.concatenate([x1 * cos - x2 * sin, x1 * sin + x2 * cos], axis=-1)


def retention_past(qc, kc, vc, log_gamma):
    B, H, nC, C, dk = qc.shape
    dv = vc.shape[-1]
    pos = jnp.arange(C, dtype=jnp.float32)
    q_dec = jnp.exp(log_gamma[:, None] * (pos + 1.0))[None, :, :, None]
    k_dec = jnp.exp(log_gamma[:, None] * (C - 1.0 - pos))[None, :, :, None]
    chunk_dec = jnp.exp(log_gamma * C)[None, :, None, None]

    def step(state, xs):
        q, k, v = xs
        out = jnp.einsum('bhcd,bhde->bhce', q * q_dec, state)
        state = state * chunk_dec + jnp.einsum('bhcd,bhce->bhde', k * k_dec, v)
        return state, out

    init = jnp.zeros((B, H, dk, dv), jnp.float32)
    xs = (jnp.moveaxis(qc, 2, 0), jnp.moveaxis(kc, 2, 0), jnp.moveaxis(vc, 2, 0))
    _, out = lax.scan(step, init, xs)
    return jnp.moveaxis(out, 0, 2)


def retention(q, k, v, log_gamma):
    B, S, H, _ = q.shape
    C = RET_CHUNK
    nC = S // C

    def chunk(t):
        return t.astype(jnp.float32).reshape(B, nC, C, H, -1).transpose(0, 3, 1, 2, 4)

    qc, kc, vc = chunk(q), chunk(k), chunk(v)
    pos = jnp.arange(C, dtype=jnp.float32)
    dist = jnp.abs(pos[:, None] - pos[None, :])
    decay = jnp.exp(log_gamma[:, None, None] * dist)
    scores = jnp.einsum('bhnid,bhnjd->bhnij', qc, kc) * decay[None, :, None]
    intra = jnp.einsum('bhnij,bhnje->bhnie', scores, vc)

    def flip(t):
        return t[:, :, ::-1, ::-1]

    past = retention_past(qc, kc, vc, log_gamma)
    future = flip(retention_past(flip(qc), flip(kc), flip(vc), log_gamma))
    out = intra + past + future
    return out.transpose(0, 2, 3, 1, 4).reshape(B, S, H, -1)


def hgrn2_scan(q, k, log_f, v):
    B, S, H, dk = q.shape
    dv = v.shape[-1]
    C = HGRN_CHUNK
    nC = S // C

    def chunk(t):
        return t.reshape(B, nC, C, H, -1).transpose(1, 0, 3, 2, 4)

    causal = jnp.tril(jnp.ones((C, C), dtype=bool))[None, None, :, :, None]

    def step(state, xs):
        qc, kc, gc, vc = xs
        b = jnp.cumsum(gc, axis=2)
        diff = b[:, :, :, None, :] - b[:, :, None, :, :]
        pair_dec = jnp.where(causal, jnp.exp(jnp.where(causal, diff, 0.0)), 0.0)
        attn = jnp.einsum('bhtsd,bhtd,bhsd->bhts', pair_dec, qc, kc)
        intra = jnp.einsum('bhts,bhse->bhte', attn, vc)
        inter = jnp.einsum('bhtd,bhde->bhte', qc * jnp.exp(b), state)
        b_last = b[:, :, -1:, :]
        state = (state * jnp.exp(b_last)[:, :, 0, :, None]
                 + jnp.einsum('bhsd,bhse->bhde', kc * jnp.exp(b_last - b), vc))
        return state, intra + inter

    init = jnp.zeros((B, H, dk, dv), jnp.float32)
    _, out = lax.scan(step, init, (chunk(q), chunk(k), chunk(log_f), chunk(v)))
    return out.transpose(1, 0, 3, 2, 4).reshape(B, S, H, dv)


def hgrn2_gate(z, lb):
    zf = z.astype(jnp.float32)
    log_lb = jnp.log(jnp.maximum(lb, LB_FLOOR))
    log_f = jnp.logaddexp(jax.nn.log_sigmoid(zf), log_lb + jax.nn.log_sigmoid(-zf))
    k = (1.0 - lb) * jax.nn.sigmoid(-zf)
    return log_f, k


def hgrn2_lower_bounds(lb_logits):
    p = jax.nn.softmax(lb_logits.astype(jnp.float32), axis=1)
    return jnp.cumsum(p, axis=1) - p[:, :1]


def fourier_mix(u):
    B, S, _ = u.shape
    ug = u.astype(jnp.float32).reshape(B, S, FNET_GROUPS, FNET_GROUP_DIM)
    y = jnp.fft.fft2(ug, axes=(1, 3), norm='ortho').real
    return y.reshape(B, S, FNET_WIDTH).astype(u.dtype)


def conv_ffn(x, w_up, conv_w, conv_b, w_down):
    h = x @ w_up
    pad = CONV_W // 2
    hp = jnp.pad(h, ((0, 0), (pad, pad), (0, 0)))
    S = h.shape[1]
    hc = conv_b
    for j in range(CONV_W):
        hc = hc + hp[:, j:j + S] * conv_w[j]
    gate, up = jnp.split(hc, 2, axis=-1)
    return (jax.nn.gelu(gate, approximate=True) * up) @ w_down


def setup_inputs(seed: int = 0) -> dict:
    key = jax.random.key(seed)
    ks = jax.random.split(key, 14)
    f32 = jnp.float32

    def nrm(k, shape, fan_in):
        return jax.random.normal(k, shape, f32) * (fan_in ** -0.5)

    x = jax.random.normal(ks[0], (BATCH, SEQ, D_MODEL), f32)
    positions = jnp.tile(jnp.arange(SEQ, dtype=jnp.int32)[None, :], (BATCH, 1))
    norm_w = 1.0 + 0.05 * jax.random.normal(ks[1], (DEPTH, 4, D_MODEL), f32)
    w_in = nrm(ks[2], (DEPTH, D_MODEL, D_IN), D_MODEL)
    hgrn_lb_logits = jax.random.normal(ks[3], (2, DEPTH, HGRN_K_W), f32)
    hgrn_norm_w = 1.0 + 0.05 * jax.random.normal(ks[4], (DEPTH, HGRN_DV), f32)
    w_ret_o = nrm(ks[5], (DEPTH, RET_V_W, D_MODEL), RET_V_W)
    w_hgrn_o = nrm(ks[6], (DEPTH, HGRN_V_W, D_MODEL), HGRN_V_W)
    w_fnet = nrm(ks[7], (DEPTH, FNET_WIDTH, D_MODEL), FNET_WIDTH)
    w_out = nrm(ks[8], (DEPTH, D_MODEL, D_MODEL), D_MODEL)
    w_up = nrm(ks[9], (DEPTH, D_MODEL, 2 * D_FF), D_MODEL)
    conv_w = nrm(ks[10], (DEPTH, CONV_W, 2 * D_FF), CONV_W)
    conv_b = 0.01 * jax.random.normal(ks[11], (DEPTH, 2 * D_FF), f32)
    w_down = nrm(ks[12], (DEPTH, D_FF, D_MODEL), D_FF)
    return {"x": x, "positions": positions, "norm_w": norm_w, "w_in": w_in,
            "hgrn_lb_logits": hgrn_lb_logits, "hgrn_norm_w": hgrn_norm_w,
            "w_ret_o": w_ret_o, "w_hgrn_o": w_hgrn_o, "w_fnet": w_fnet, "w_out": w_out,
            "w_up": w_up, "conv_w": conv_w, "conv_b": conv_b, "w_down": w_down}


def reference(x, positions, norm_w, w_in, hgrn_lb_logits, hgrn_norm_w,
              w_ret_o, w_hgrn_o, w_fnet, w_out, w_up, conv_w, conv_b, w_down):
    B, S, _ = x.shape
    dt = x.dtype
    split_points = _split_points()
    log_gamma = jnp.log(1.0 - 2.0 ** (-5.0 - jnp.arange(RET_HEADS, dtype=jnp.float32)))
    lower_bounds = hgrn2_lower_bounds(hgrn_lb_logits)
    hgrn_scale = HGRN_DK ** -0.5
    ret_scale = RET_DK ** -0.5

    def flip(t):
        return t[:, ::-1]

    for l in range(DEPTH):
        xn = rms_norm(x, norm_w[l, 0])
        u = xn @ w_in[l]
        (rq, rk, rv, rg, hq, hz_f, hz_b, hi, hg, fu, ga) = jnp.split(u, split_points, axis=-1)

        q = rotary(rq.reshape(B, S, RET_HEADS, RET_DK), positions)
        k = rotary(rk.reshape(B, S, RET_HEADS, RET_DK), positions) * ret_scale
        ro = retention(q, k, rv.reshape(B, S, RET_HEADS, RET_DV), log_gamma)
        ro = _rms(ro).reshape(B, S, RET_V_W).astype(dt) * jax.nn.silu(rg)
        y_ret = ro @ w_ret_o[l]

        hqh = jax.nn.silu(hq.astype(jnp.float32)).reshape(B, S, HGRN_HEADS, HGRN_DK) * hgrn_scale
        hih = hi.astype(jnp.float32).reshape(B, S, HGRN_HEADS, HGRN_DV)
        lf_f, k_f = hgrn2_gate(hz_f.reshape(B, S, HGRN_HEADS, HGRN_DK),
                               lower_bounds[0, l].reshape(HGRN_HEADS, HGRN_DK))
        lf_b, k_b = hgrn2_gate(hz_b.reshape(B, S, HGRN_HEADS, HGRN_DK),
                               lower_bounds[1, l].reshape(HGRN_HEADS, HGRN_DK))
        ho_f = hgrn2_scan(hqh, k_f, lf_f, hih)
        ho_b = flip(hgrn2_scan(flip(hqh), flip(k_b), flip(lf_b), flip(hih)))
        ho = _rms(ho_f + ho_b) * hgrn_norm_w[l].astype(jnp.float32)
        ho = ho.reshape(B, S, HGRN_V_W).astype(dt) * jax.nn.silu(hg)
        y_hgrn = ho @ w_hgrn_o[l]

        y_fft = fourier_mix(fu) @ w_fnet[l]

        g_ret, g_hgrn, g_fft = jnp.split(jax.nn.sigmoid(ga), N_BRANCH, axis=-1)
        mix = (g_ret * y_ret + g_hgrn * y_hgrn + g_fft * y_fft) @ w_out[l]
        x = x + rms_norm(mix, norm_w[l, 1])

        hn = rms_norm(x, norm_w[l, 2])
        x = x + rms_norm(conv_ffn(hn, w_up[l], conv_w[l], conv_b[l], w_down[l]), norm_w[l, 3])
    return x
```

```python
import numpy as np
import concourse.bass as bass
import concourse.mybir as mybir

F32 = mybir.dt.float32
BF16 = mybir.dt.bfloat16
I32 = mybir.dt.int32
U8 = mybir.dt.uint8
AF = mybir.ActivationFunctionType
ALU = mybir.AluOpType

ENGS = ("pe", "act", "dve", "pool", "sp")
BUCKET = 2048
EPOCH = 12000


class _Rec:
    __slots__ = ("lo", "hi", "iid", "eng", "alive")

    def __init__(self, lo, hi, iid, eng):
        self.lo, self.hi, self.iid, self.eng, self.alive = lo, hi, iid, eng, True


class _Space:
    def __init__(self):
        self.w = {}
        self.r = {}

    @staticmethod
    def _bk(lo, hi):
        return range(lo // BUCKET, (hi - 1) // BUCKET + 1)

    def query(self, table, lo, hi, out):
        for b in self._bk(lo, hi):
            lst = table.get(b)
            if not lst:
                continue
            for rec in lst:
                if rec.alive and rec.lo < hi and lo < rec.hi:
                    out.add(rec.iid)

    def kill_covered(self, table, lo, hi):
        for b in self._bk(lo, hi):
            lst = table.get(b)
            if not lst:
                continue
            keep = []
            for rec in lst:
                if not rec.alive:
                    continue
                if lo <= rec.lo and rec.hi <= hi:
                    rec.alive = False
                else:
                    keep.append(rec)
            table[b] = keep

    def insert(self, table, rec):
        for b in self._bk(rec.lo, rec.hi):
            table.setdefault(b, []).append(rec)

    def add_reader(self, lo, hi, iid, eng):
        b0 = lo // BUCKET
        lst = self.r.get(b0)
        if lst and eng != "dma":
            for rec in lst:
                if rec.alive and rec.lo == lo and rec.hi == hi and rec.eng == eng:
                    rec.iid = iid
                    return
        self.insert(self.r, _Rec(lo, hi, iid, eng))


class Acc:
    __slots__ = ("ap", "regs")

    def __init__(self, ap, regs):
        self.ap, self.regs = ap, regs


def _regions(shape, idx):
    if len(shape) == 1:
        return [(idx[0][0], idx[0][1])]
    inner = int(np.prod(shape[1:]))
    sub = _regions(shape[1:], idx[1:])
    if len(sub) == 1 and sub[0] == (0, inner):
        return [(idx[0][0] * inner, idx[0][1] * inner)]
    out = []
    for i in range(idx[0][0], idx[0][1]):
        for (l, h) in sub:
            out.append((i * inner + l, i * inner + h))
    return out


class View:
    def __init__(self, space, ap, free_shape, esize, base_bytes=0, track=True):
        self.space, self.ap, self.shape, self.esize, self.base, self.track = space, ap, tuple(free_shape), esize, base_bytes, track

    def __getitem__(self, key):
        if not isinstance(key, tuple):
            key = (key,)
        key = key + (slice(None),) * (1 + len(self.shape) - len(key))
        ap = self.ap[key]
        if not self.track:
            return Acc(ap, [])
        idx = []
        for k, n in zip(key[1:], self.shape):
            if isinstance(k, slice):
                assert k.step in (None, 1)
                idx.append((k.start or 0, n if k.stop is None else k.stop))
            else:
                idx.append((k, k + 1))
        regs = _regions(self.shape, idx)
        if len(regs) > 48:
            regs = [(regs[0][0], regs[-1][1])]
        out = [(self.space, self.base + l * self.esize, self.base + h * self.esize) for (l, h) in regs]
        if self.space == "ps":
            lo = min(r[1] for r in out) // 2048 * 2048
            hi = (max(r[2] for r in out) + 2047) // 2048 * 2048
            out = [("ps", lo, hi)]
        return Acc(ap, out)


class Ins:
    __slots__ = ("eng", "fn", "deps", "dma", "semkey", "semval", "ms", "needs_inc", "waits", "name")


class Prog:
    def __init__(self, nc):
        self.nc = nc
        self.ins = []
        self.streams = {e: [] for e in ENGS}
        self.spaces = {}
        self.dma_count = {}
        self.cur_group = None

    def _sp(self, name):
        s = self.spaces.get(name)
        if s is None:
            s = self.spaces[name] = _Space()
        return s

    def add(self, eng, fn, reads=(), writes=(), dma=False, semkey=None, extra_deps=(), name=""):
        iid = len(self.ins)
        deps = set(extra_deps)
        rregs = [r for a in reads for r in a.regs]
        wregs = [r for a in writes for r in a.regs]
        for (sp, lo, hi) in rregs:
            s = self._sp(sp)
            s.query(s.w, lo, hi, deps)
        for (sp, lo, hi) in wregs:
            s = self._sp(sp)
            s.query(s.w, lo, hi, deps)
            s.query(s.r, lo, hi, deps)
        for (sp, lo, hi) in wregs:
            s = self._sp(sp)
            s.kill_covered(s.w, lo, hi)
            s.kill_covered(s.r, lo, hi)
            s.insert(s.w, _Rec(lo, hi, iid, "dma" if dma else eng))
        for (sp, lo, hi) in rregs:
            s = self._sp(sp)
            s.add_reader(lo, hi, iid, "dma" if dma else eng)
        deps.discard(iid)
        I = Ins()
        I.eng, I.fn, I.deps, I.dma, I.semkey, I.name = eng, fn, deps, dma, semkey, name
        I.semval = None
        I.needs_inc = dma
        I.waits = []
        I.ms = None
        if dma:
            assert semkey is not None
            self.dma_count[semkey] = self.dma_count.get(semkey, 0) + 16
            I.semval = self.dma_count[semkey]
            if self.cur_group is not None:
                self.cur_group.append(I)
        self.ins.append(I)
        self.streams[eng].append(iid)
        return iid

    def begin_group(self):
        self.cur_group = []

    def end_group(self):
        g = self.cur_group
        self.cur_group = None
        if g:
            tot = max(i.semval for i in g)
            for i in g:
                i.semval = tot
        return g

    def finalize(self):
        ins = self.ins
        for I in ins:
            for d in I.deps:
                D = ins[d]
                if not D.dma:
                    if D.eng == "pe" and I.eng == "pe" and not I.dma:
                        continue
                    D.needs_inc = True
        for e in ENGS:
            g = 0
            for iid in self.streams[e]:
                I = ins[iid]
                if I.needs_inc and not I.dma:
                    I.ms = g
                    g += 1
        self.n_ms = {e: sum(1 for iid in self.streams[e] if ins[iid].ms is not None) for e in ENGS}
        waited = {e: {p: -1 for p in ENGS} for e in ENGS}
        waited_dma = {e: {} for e in ENGS}
        last_ms = {}
        for e in ENGS:
            cur = -1
            for iid in self.streams[e]:
                if ins[iid].ms is not None:
                    cur = ins[iid].ms
                last_ms[iid] = cur
        for I in ins:
            need = {}
            need_dma = {}
            for d in I.deps:
                D = ins[d]
                if D.dma:
                    need_dma[D.semkey] = max(need_dma.get(D.semkey, 0), D.semval)
                else:
                    if D.eng == "pe" and I.eng == "pe" and not I.dma:
                        continue
                    need[D.eng] = max(need.get(D.eng, -1), D.ms)
            for p, m in need.items():
                if waited[I.eng][p] >= m:
                    continue
                waited[I.eng][p] = m
                I.waits.append(("c", p, m))
            for k, v in need_dma.items():
                if waited_dma[I.eng].get(k, 0) >= v:
                    continue
                waited_dma[I.eng][k] = v
                I.waits.append(("d", k, v))

    def emit(self, sems_eng, sems_dma, block):
        nc = self.nc
        ins = self.ins

        def run(engname, eng):
            for iid in self.streams[engname]:
                I = ins[iid]
                for (kind, a, b) in I.waits:
                    if kind == "c":
                        eng.wait_ge(sems_eng[a][b // EPOCH], b % EPOCH + 1)
                    else:
                        eng.wait_ge(sems_dma[a], b)
                r = I.fn(eng)
                if I.dma:
                    r.then_inc(sems_dma[I.semkey], 16)
                elif I.ms is not None:
                    r.then_inc(sems_eng[engname][I.ms // EPOCH], 1)

        @block.tensor
        def _(e):
            run("pe", e)

        @block.scalar
        def _(e):
            run("act", e)

        @block.vector
        def _(e):
            run("dve", e)

        @block.gpsimd
        def _(e):
            run("pool", e)

        @block.sync
        def _(e):
            run("sp", e)

import contextlib, os
HG_STOP = os.environ.get('HG_STOP', '')
HG_S1 = int(os.environ.get('HG_S1', '9'))
from concourse.bass_utils import run_bass_kernel_spmd

D = 1024
S = 2048
DEPTH = 2
DFF = 2816
DIN = 15360
NT = 16
NG = 4
EPSV = 1e-6
BIGB = 212800
C_RQ, C_RK, C_RV, C_RG, C_HQ, C_HZF, C_HZB, C_HI, C_HG, C_FU, C_GA = 0, 1024, 2048, 4096, 6144, 7168, 8192, 9216, 10240, 11264, 12288
MAGIC = 12582912.0
TWO_PI = 6.28318
P_LBL = 0
P_HNW = 32
P_CW = 34
P_CB = 298
P_IF = 386
NPRM = 388


def build_program(debug_stage=None, nlayers=DEPTH):
    nc = bass.Bass("TRN2", target_bir_lowering=False)
    dx = nc.dram_tensor("x", [S, D], F32, kind="ExternalInput").ap()
    dpos = nc.dram_tensor("pos", [1, S], I32, kind="ExternalInput").ap()
    dnw = nc.dram_tensor("norm_w", [DEPTH * 4, D], F32, kind="ExternalInput").ap()
    dwin = nc.dram_tensor("w_in", [DEPTH, D, DIN], F32, kind="ExternalInput").ap()
    dwro = nc.dram_tensor("w_ret_o", [DEPTH, 2048, D], F32, kind="ExternalInput").ap()
    dwho = nc.dram_tensor("w_hgrn_o", [DEPTH, D, D], F32, kind="ExternalInput").ap()
    dwfn = nc.dram_tensor("w_fnet", [DEPTH, D, D], F32, kind="ExternalInput").ap()
    dwout = nc.dram_tensor("w_out", [DEPTH, D, D], F32, kind="ExternalInput").ap()
    dwup = nc.dram_tensor("w_up", [DEPTH, D, 2 * DFF], F32, kind="ExternalInput").ap()
    dwdn = nc.dram_tensor("w_down", [DEPTH, DFF, D], F32, kind="ExternalInput").ap()
    dprm = nc.dram_tensor("prm", [128, NPRM], F32, kind="ExternalInput").ap()
    dmask = nc.dram_tensor("rmask", [4, 128, 3968], F32, kind="ExternalInput").ap()
    dcg = nc.dram_tensor("cg", [2, 256, 256], F32, kind="ExternalInput").ap()
    ddft = nc.dram_tensor("dft", [2, S, S], F32, kind="ExternalInput").ap()
    dcst = nc.dram_tensor("cst", [128, 512], F32, kind="ExternalInput").ap()
    dout = nc.dram_tensor("out", [S, D], F32, kind="ExternalOutput").ap()
    dxs = nc.dram_tensor("xs", [S, D], F32, kind="Internal").ap()
    ddbg = None
    if debug_stage is not None:
        ddbg = nc.dram_tensor("dbg", [128, 8, S], F32, kind="ExternalOutput").ap()

    es = contextlib.ExitStack()
    big = es.enter_context(nc.sbuf_tensor("big", [128, BIGB // 4], F32))
    ps = es.enter_context(nc.psum_tensor("ps", [128, 8 * 512], F32))
    P = Prog(nc)

    def sview(offb, shape, dt):
        esz = 2 if dt == BF16 else 4
        n = int(np.prod(shape))
        assert offb % 4 == 0 and (n * esz) % 4 == 0, (offb, shape)
        assert offb + n * esz <= BIGB, ("SBUF overflow", offb, shape)
        ap = big[:, offb // 4: offb // 4 + n * esz // 4]
        if dt != F32:
            ap = ap.bitcast(dt)
        if len(shape) > 1:
            names = " ".join("d%d" % i for i in range(len(shape)))
            ap = ap.rearrange("p (%s) -> p %s" % (names, names), **{"d%d" % i: shape[i] for i in range(len(shape) - 1)})
        return View("sb", ap, shape, esz, offb)

    def bank(b, n=512):
        return View("ps", ps[:, b * 512: b * 512 + n], [n], 4, b * 2048)

    def bank3(b, a, n):
        return View("ps", ps[:, b * 512: b * 512 + a * n].rearrange("p (a n) -> p a n", a=a), [a, n], 4, b * 2048)

    psb = View("ps", ps[:, 7 * 512: 8 * 512].bitcast(BF16).rearrange("p (a b) -> p a b", a=8), [8, 128], 2, 7 * 2048)

    def dview(ap, name, track=True):
        return View(name, ap.rearrange("(t p) d -> p t d", p=128), [NT, D], 4, 0, track=track)

    vx = dview(dx, "d_x", track=False)
    vxs = dview(dxs, "d_xs")
    vout = dview(dout, "d_out")

    def mm(out, lhsT, rhs, start, stop):
        P.add("pe", lambda e: e.matmul(out.ap, lhsT=lhsT.ap, rhs=rhs.ap, start=start, stop=stop), reads=[lhsT, rhs], writes=[out])

    def tr(out, in_, ident):
        P.add("pe", lambda e: e.transpose(out=out.ap, in_=in_.ap, identity=ident.ap), reads=[in_, ident], writes=[out])

    def act(out, in_, func, scale=None, bias=None, accum=None, eng="act"):
        rd = [in_]
        kw = {}
        if scale is not None:
            if isinstance(scale, Acc):
                rd.append(scale); kw["scale"] = scale.ap
            else:
                kw["scale"] = scale
        if bias is not None:
            if isinstance(bias, Acc):
                rd.append(bias); kw["bias"] = bias.ap
            else:
                kw["bias"] = bias
        wr = [out]
        if accum is not None:
            wr.append(accum); kw["accum_out"] = accum.ap
        P.add("act", lambda e: e.activation(out=out.ap, in_=in_.ap, func=func, **kw), reads=rd, writes=wr)

    def tt(out, a, b, op, eng="dve"):
        P.add(eng, lambda e: e.tensor_tensor(out=out.ap, in0=a.ap, in1=b.ap, op=op), reads=[a, b], writes=[out])

    def ts(out, a, s1, s2, op0, op1=None, eng="dve"):
        rd = [a]
        v1 = s1.ap if isinstance(s1, Acc) else s1
        v2 = s2.ap if isinstance(s2, Acc) else s2
        if isinstance(s1, Acc): rd.append(s1)
        if isinstance(s2, Acc): rd.append(s2)
        if op1 is None:
            P.add(eng, lambda e: e.tensor_scalar(out=out.ap, in0=a.ap, scalar1=v1, scalar2=None, op0=op0), reads=rd, writes=[out])
        else:
            P.add(eng, lambda e: e.tensor_scalar(out=out.ap, in0=a.ap, scalar1=v1, scalar2=v2, op0=op0, op1=op1), reads=rd, writes=[out])

    def stt(out, a, s, b, op0, op1):
        rd = [a, b]
        v = s.ap if isinstance(s, Acc) else s
        if isinstance(s, Acc): rd.append(s)
        P.add("dve", lambda e: e.scalar_tensor_tensor(out=out.ap, in0=a.ap, scalar=v, in1=b.ap, op0=op0, op1=op1), reads=rd, writes=[out])

    def cpy(out, in_, eng="dve"):
        if eng == "act":
            act(out, in_, AF.Copy)
        else:
            P.add(eng, lambda e: e.tensor_copy(out=out.ap, in_=in_.ap), reads=[in_], writes=[out])

    def memset(out, v, eng="dve"):
        P.add(eng, lambda e: e.memset(out.ap, v), writes=[out])

    def recip(out, in_):
        P.add("dve", lambda e: e.reciprocal(out=out.ap, in_=in_.ap), reads=[in_], writes=[out])

    def scan(out, ones, data):
        P.add("dve", lambda e: e.tensor_tensor_scan(out=out.ap, data0=ones.ap, data1=data.ap, initial=0.0, op0=ALU.mult, op1=ALU.add), reads=[ones, data], writes=[out])

    def cpred(out, mask, data):
        P.add("dve", lambda e: e.copy_predicated(out=out.ap, mask=mask.ap, data=data.ap), reads=[mask, data, out], writes=[out])

    dma_n = [0]

    def dma(eng, out_acc, in_acc, out_ap=None, in_ap=None, key=None):
        i = dma_n[0]; dma_n[0] += 1
        k = key or ("q_%s_%d" % (eng, i % 12))
        oa = out_ap if out_ap is not None else out_acc.ap
        ia = in_ap if in_ap is not None else in_acc.ap
        rd = [in_acc] if in_acc is not None else []
        wr = [out_acc] if out_acc is not None else []
        prev = dma_last.get(k)
        iid = P.add(eng, lambda e: e.dma_start(out=oa, in_=ia), reads=rd, writes=wr, dma=True, semkey=k,
                    extra_deps=[prev] if prev is not None else [])
        dma_last[k] = iid
        return iid

    dma_last = {}

    CB = 0
    ident = sview(CB + 0, [128], BF16)
    onesb = sview(CB + 256, [128], BF16)
    maskf = sview(CB + 512, [128], I32)
    maskb = sview(CB + 1024, [128], I32)
    onesf = sview(CB + 1536, [128], F32)
    epsb = sview(CB + 2048, [1], F32)
    prm = sview(CB + 2112, [NPRM], F32)
    hp = sview(CB + 3680, [3, 2, 8], F32)
    cst32 = sview(CB + 3904, [512], F32)
    XNT_B = 6144
    xnT = sview(XNT_B, [8, S], BF16)
    MRG_B = XNT_B + 32768
    merged = sview(MRG_B, [8, S], F32)
    POOL_B = MRG_B + 65536
    POOL_SZ = BIGB - POOL_B

    class Bump:
        def __init__(self, base, size):
            self.base, self.size, self.cur = base, size, 0

        def reset(self, base=None, size=None):
            if base is not None:
                self.base, self.size = base, size
            self.cur = 0

        def __call__(self, shape, dt):
            esz = 2 if dt == BF16 else 4
            n = int(np.prod(shape)) * esz
            n = (n + 63) // 64 * 64
            assert self.cur + n <= self.size, ("pool overflow", self.cur, n, self.size)
            v = sview(self.base + self.cur, shape, dt)
            self.cur += n
            return v

    pool = Bump(POOL_B, POOL_SZ)

    class Arena:
        def __init__(self):
            self.base = self.size = self.cur = 0
            self.n = 0
            self.hist = []

        def set(self, base, size):
            self.base, self.size, self.cur = base, size, 0

        def load(self, src_ap, nch, ncols):
            nb = nch * ncols * 2
            nb = (nb + 63) // 64 * 64
            assert nb <= self.size, ("arena too small", nb, self.size)
            if self.cur + nb > self.size:
                self.cur = 0
            v = sview(self.base + self.cur, [nch, ncols], BF16)
            self.cur += nb
            k = "w%d" % (self.n % 8)
            self.n += 1
            prev = dma_last.get(k)
            iid = P.add("pool", lambda e: e.dma_start(out=v[:].ap, in_=src_ap), writes=[v[:]], dma=True, semkey=k,
                        extra_deps=[prev] if prev is not None else [])
            dma_last[k] = iid
            return v

    arena = Arena()

    def wcols(wl, c0, n):
        return wl.rearrange("(c p) n -> p c n", p=128)[:, :, c0:c0 + n]

    dma("sp", cst32[:], None, in_ap=dcst[:, :])
    cpy(ident[:], cst32[:, 0:128])
    cpy(onesb[:], cst32[:, 128:256])
    cpy(maskf[:], cst32[:, 256:384])
    cpy(maskb[:], cst32[:, 384:512])
    memset(onesf[:], 1.0)
    memset(epsb[:], EPSV)
    dma("sp", prm[:], None, in_ap=dprm[:, :])

    def norm_phase(l, ni, src):
        pool.reset()
        nwb = pool([D], F32)
        dma("sp", nwb[:], None, in_ap=dnw[l * 4 + ni: l * 4 + ni + 1, :].partition_broadcast(128))
        xb = [pool([D], F32) for _ in range(2)]
        junk = pool([D], F32)
        xnb = [pool([8, 128], BF16) for _ in range(2)]
        sc = pool([2, 4], F32)
        for t in range(NT):
            xt = xb[t % 2]
            dma("sp", xt[:], src[:, t, :])
            act(junk[:], xt[:], AF.Square, accum=sc[:, t % 2, 0:1])
            act(sc[:, t % 2, 1:2], sc[:, t % 2, 0:1], AF.Sqrt, scale=1.0 / D, bias=epsb[:, 0:1])
            recip(sc[:, t % 2, 2:3], sc[:, t % 2, 1:2])
            xn_flat = View("sb", xnb[t % 2].ap.rearrange("p a b -> p (a b)"), [D], 2, xnb[t % 2].base)
            stt(xn_flat[:], xt[:], sc[:, t % 2, 2:3], nwb[:], ALU.mult, ALU.mult)
            for c in range(8):
                tr(psb[:, c, :], xnb[t % 2][:, c, :], ident[:])
            cpy(xnT[:, :, t * 128:(t + 1) * 128], psb[:], eng="act")

    def residual_tile(t, halves, nwb, xb, yb, sc, src, dst):
        i = t % 2
        junk = yb[i]
        act(junk[:, 0:512], halves[0], AF.Square, accum=sc[:, i, 0:1])
        act(junk[:, 512:1024], halves[1], AF.Square, accum=sc[:, i, 1:2])
        tt(sc[:, i, 2:3], sc[:, i, 0:1], sc[:, i, 1:2], ALU.add)
        act(sc[:, i, 3:4], sc[:, i, 2:3], AF.Sqrt, scale=1.0 / D, bias=epsb[:, 0:1])
        recip(sc[:, i, 4:5], sc[:, i, 3:4])
        dma("sp", xb[i][:], src[:, t, :])
        for h in range(2):
            stt(yb[i][:, h * 512:(h + 1) * 512], halves[h], sc[:, i, 4:5], nwb[:, h * 512:(h + 1) * 512], ALU.mult, ALU.mult)
        tt(yb[i][:], yb[i][:], xb[i][:], ALU.add)
        return dma("sp", dst[:, t, :], yb[i][:])

    def dump_merged():
        iid = dma("sp", None, merged[:], out_ap=ddbg[:, :, :])
        P.add("sp", lambda e: e.nop(), extra_deps=[iid])

    out_dmas = []

    for l in range(nlayers):
        wl = dwin[l]
        src_a = vx if l == 0 else vxs
        norm_phase(l, 0, src_a)

        pool.reset()
        pi = pool([S], I32)
        pf = pool([S], F32)
        u1 = pool([S], F32)
        cosT = pool([S], F32)
        sinT = pool([S], F32)
        dma("sp", pi[:], None, in_ap=dpos[0:1, :].partition_broadcast(128))
        cpy(pf[:], pi[:])
        ts(u1[:], pf[:], prm[:, P_IF:P_IF + 1], 1.0 / (2 * np.pi), ALU.mult, ALU.mult)
        ts(pf[:], u1[:], MAGIC, MAGIC, ALU.add, ALU.subtract)
        tt(pf[:], u1[:], pf[:], ALU.subtract)
        act(sinT[:], pf[:], AF.Sin, scale=TWO_PI)
        ts(u1[:], u1[:], 0.25, None, ALU.add)
        ts(pf[:], u1[:], MAGIC, MAGIC, ALU.add, ALU.subtract)
        tt(pf[:], u1[:], pf[:], ALU.subtract)
        act(cosT[:], pf[:], AF.Sin, scale=TWO_PI)
        pool.cur = 0
        _ = pool([S], I32); _ = pool([S], F32); _ = pool([S], F32); _ = pool([S], F32); _ = pool([S], F32)
        pool.cur = 0
        qT = pool([2, S], BF16)
        kT = pool([2, S], BF16)
        tmp1 = pool([512], F32); tmp2 = pool([512], F32)
        sq = pool([4, 512], BF16)
        assert pool.cur <= 24576
        pool.cur = 40960
        V = pool([NT, 512], BF16)
        mask = pool([3968], F32)
        roT = pool([4, 512], BF16)
        PT = [pool([512], BF16) for _ in range(3)]
        rstd = pool([512], F32)
        sg = pool([512], F32)
        tq = pool([512], F32)
        arena.set(POOL_B + pool.cur, POOL_SZ - pool.cur)
        for h in range(4):
            wq = arena.load(wcols(wl, C_RQ + h * 256, 256), 8, 256)
            wk = arena.load(wcols(wl, C_RK + h * 256, 256), 8, 256)
            wv = arena.load(wcols(wl, C_RV + h * 512, 512), 8, 512)
            dma("pool", mask[:], None, in_ap=dmask[h, :, :])
            for (w_, dst) in ((wq, qT), (wk, kT)):
                for g in range(NG):
                    gs = slice(g * 512, (g + 1) * 512)
                    for dc in range(2):
                        for k in range(8):
                            mm(bank(4 + dc)[:], w_[:, k, dc * 128:(dc + 1) * 128], xnT[:, k, gs], k == 0, k == 7)
                    A, B = bank(4)[:], bank(5)[:]
                    tt(tmp1[:], A, cosT[:, gs], ALU.mult)
                    tt(tmp2[:], B, sinT[:, gs], ALU.mult)
                    tt(dst[:, 0, gs], tmp1[:], tmp2[:], ALU.subtract)
                    tt(tmp1[:], A, sinT[:, gs], ALU.mult)
                    tt(tmp2[:], B, cosT[:, gs], ALU.mult)
                    tt(dst[:, 1, gs], tmp1[:], tmp2[:], ALU.add)
            for t in range(NT):
                b = bank(4 + t % 2)
                for k in range(8):
                    mm(b[:], xnT[:, k, t * 128:(t + 1) * 128], wv[:, k, :], k == 0, k == 7)
                cpy(V[:, t, :], b[:], eng="act")
            wg = arena.load(wcols(wl, C_RG + h * 512, 512), 8, 512)
            wro = arena.load(dwro[l][h * 512:(h + 1) * 512, :].rearrange("(c p) n -> p c n", p=128), 4, D)
            for g in range(NG):
                gs = slice(g * 512, (g + 1) * 512)
                for j in range(NT):
                    scb = bank(4 + j % 2)
                    for dc in range(2):
                        mm(scb[:], kT[:, dc, j * 128:(j + 1) * 128], qT[:, dc, gs], dc == 0, dc == 1)
                    off = 1920 - 128 * j + 512 * g
                    pt = PT[j % 3]
                    tt(pt[:], scb[:], mask[:, off:off + 512], ALU.mult)
                    for e_ in range(4):
                        mm(bank(e_)[:], V[:, j, e_ * 128:(e_ + 1) * 128], pt[:], j == 0, j == NT - 1)
                for e_ in range(4):
                    act(sq[:, e_, :], bank(e_)[:], AF.Square)
                for e_ in range(4):
                    mm(bank(6)[:], onesb[:], sq[:, e_, :], e_ == 0, e_ == 3)
                act(rstd[:], bank(6)[:], AF.Sqrt, scale=1.0 / 512, bias=epsb[:, 0:1])
                recip(rstd[:], rstd[:])
                for e_ in range(4):
                    gb = bank(4 + e_ % 2)
                    for k in range(8):
                        mm(gb[:], wg[:, k, e_ * 128:(e_ + 1) * 128], xnT[:, k, gs], k == 0, k == 7)
                    act(sg[:], gb[:], AF.Silu)
                    tt(tq[:], bank(e_)[:], rstd[:], ALU.mult)
                    tt(roT[:, e_, :], tq[:], sg[:], ALU.mult)
                for c in range(8):
                    yb_ = bank(4 + c % 2)
                    for e_ in range(4):
                        mm(yb_[:], wro[:, e_, c * 128:(c + 1) * 128], roT[:, e_, :], e_ == 0, e_ == 3)
                    if h == 0:
                        cpy(merged[:, c, gs], yb_[:], eng="act")
                    else:
                        tt(merged[:, c, gs], merged[:, c, gs], yb_[:], ALU.add)
        for c in range(8):
            wga = arena.load(wcols(wl, C_GA + c * 128, 128), 8, 128)
            for g in range(NG):
                gs = slice(g * 512, (g + 1) * 512)
                gb = bank(4 + g % 2)
                for k in range(8):
                    mm(gb[:], wga[:, k, :], xnT[:, k, gs], k == 0, k == 7)
                act(sg[:], gb[:], AF.Sigmoid)
                tt(merged[:, c, gs], merged[:, c, gs], sg[:], ALU.mult)
        if debug_stage == "ret" and l == 0:
            dump_merged(); break

        pool.reset()
        if True:
            lb = hp[:, 0, :, :]
            e0 = pool([2, 8], F32); e1 = pool([2, 8], F32)
            lbl = View("sb", prm.ap[:, P_LBL:P_LBL + 32].rearrange("p (d l h) -> p d l h", d=2, l=2), [2, 2, 8], 4, prm.base + P_LBL * 4)
            if l == 0:
                memset(hp[:, 0, :, :], 0.0)
            else:
                act(e0[:], lbl[:, :, 0, :], AF.Exp)
                act(e1[:], lbl[:, :, 1, :], AF.Exp)
                tt(e0[:], e0[:], e1[:], ALU.add)
                recip(e0[:], e0[:])
                tt(hp[:, 0, :, :], e1[:], e0[:], ALU.mult)
            ts(hp[:, 1, :, :], hp[:, 0, :, :], -1.0, 1.0, ALU.mult, ALU.add)
            ts(hp[:, 2, :, :], hp[:, 1, :, :], -1.0, None, ALU.mult)
        hoT = pool([8, S], BF16)
        qk = {("q", 0): pool([S], BF16), ("k", 0): pool([S], BF16), ("q", 1): pool([S], BF16), ("k", 1): pool([S], BF16)}
        kTt = [pool([NT, 128], BF16) for _ in range(2)]
        Vh = pool([NT, 128], BF16)
        Sbf = [pool([NT, 128], BF16) for _ in range(2)]
        St = [pool([128], F32) for _ in range(2)]
        Esc = [pool([3, NT], F32) for _ in range(2)]
        q32 = pool([512], F32); SG = pool([512], F32); Kt = pool([512], F32); Bt = pool([512], F32); Dt = pool([512], F32)
        tdd = pool([4], F32)
        AT = [pool([128], BF16) for _ in range(4)]
        utmp = pool([128], F32)
        rstd = pool([512], F32); sgt = pool([512], F32); tq = pool([512], F32); sqb = pool([512], BF16)
        for a_ in AT:
            memset(a_[:], 0.0)
        arena.set(POOL_B + pool.cur, POOL_SZ - pool.cur)
        hscale = 128.0 ** -0.5
        for h in range(8):
            wq = arena.load(wcols(wl, C_HQ + h * 128, 128), 8, 128)
            wz = [arena.load(wcols(wl, C_HZF + h * 128, 128), 8, 128), arena.load(wcols(wl, C_HZB + h * 128, 128), 8, 128)]
            wi = arena.load(wcols(wl, C_HI + h * 128, 128), 8, 128)
            wgh = arena.load(wcols(wl, C_HG + h * 128, 128), 8, 128)
            for g in range(NG):
                gs = slice(g * 512, (g + 1) * 512)
                b0 = bank(g % 2)
                for k in range(8):
                    mm(b0[:], wq[:, k, :], xnT[:, k, gs], k == 0, k == 7)
                act(q32[:], b0[:], AF.Silu)
                for d_ in range(2):
                    bz = bank(2 + d_)
                    for k in range(8):
                        mm(bz[:], wz[d_][:, k, :], xnT[:, k, gs], k == 0, k == 7)
                    if HG_S1 < 2: continue
                    act(SG[:], bz[:], AF.Sigmoid)
                    ts(Kt[:], SG[:], hp[:, 2, d_, h:h + 1], hp[:, 1, d_, h:h + 1], ALU.mult, ALU.add)
                    act(SG[:], SG[:], AF.Ln, scale=hp[:, 1, d_, h:h + 1], bias=hp[:, 0, d_, h:h + 1])
                    if HG_S1 < 3: continue
                    for ck in range(4):
                        cs = slice(ck * 128, (ck + 1) * 128)
                        scan(Bt[:, cs], onesf[:], SG[:, cs])
                    if HG_S1 < 4: continue
                    B4 = View("sb", Bt.ap.rearrange("p (a b) -> p a b", a=4), [4, 128], 4, Bt.base)
                    c4 = slice(g * 4, (g + 1) * 4)
                    act(Esc[d_][:, 0, c4], B4[:, :, 127], AF.Exp)
                    act(Esc[d_][:, 1, c4], B4[:, :, 63], AF.Exp)
                    tt(tdd[:], B4[:, :, 127], B4[:, :, 63], ALU.subtract)
                    act(Esc[d_][:, 2, c4], tdd[:], AF.Exp)
                    if HG_S1 < 5: continue
                    if d_ == 0:
                        for ck in range(4):
                            cs = slice(ck * 128, (ck + 1) * 128)
                            ts(Dt[:, cs], Bt[:, cs], Bt[:, ck * 128 + 63: ck * 128 + 64], None, ALU.subtract)
                    else:
                        tt(Dt[:], SG[:], Bt[:], ALU.subtract)
                        for ck in range(4):
                            cs = slice(ck * 128, (ck + 1) * 128)
                            ts(Bt[:, cs], Dt[:, cs], Dt[:, ck * 128 + 64: ck * 128 + 65], None, ALU.subtract)
                    Dd = Dt if d_ == 0 else Bt
                    act(SG[:], Dd[:], AF.Exp)
                    stt(qk[("q", d_)][:, gs], q32[:], hscale, SG[:], ALU.mult, ALU.mult)
                    act(SG[:], Dd[:], AF.Exp, scale=-1.0)
                    tt(qk[("k", d_)][:, gs], Kt[:], SG[:], ALU.mult)
                if HG_S1 < 6: continue
                bv = bank3(4 + g % 2, 4, 128)
                for t4 in range(4):
                    t = g * 4 + t4
                    for k in range(8):
                        mm(bv[:, t4, :], xnT[:, k, t * 128:(t + 1) * 128], wi[:, k, :], k == 0, k == 7)
                cpy(Vh[:, g * 4:(g + 1) * 4, :], bv[:], eng="dve")
            if HG_STOP == 's1':
                break
            for d_ in range(2):
                kk = qk[("k", d_)]
                for half in range(2):
                    for i in range(8):
                        c = half * 8 + i
                        tr(psb[:, i, :], kk[:, c * 128:(c + 1) * 128], ident[:])
                    cpy(kTt[d_][:, half * 8:(half + 1) * 8, :], psb[:], eng="act")
                memset(St[d_][:], 0.0)
                order = range(NT) if d_ == 0 else range(NT - 1, -1, -1)
                ie_r, ie_tr = (1, 2) if d_ == 0 else (2, 1)
                for n_, c in enumerate(order):
                    ub = bank3(4 + n_ % 2, 4, 128)
                    mm(ub[:, 0, :], kTt[d_][:, c, :], Vh[:, c, :], True, True)
                    act(Sbf[d_][:, c, :], St[d_][:], AF.Identity, scale=Esc[d_][:, ie_r, c:c + 1])
                    ts(utmp[:], ub[:, 0, :], Esc[d_][:, ie_tr, c:c + 1], None, ALU.mult)
                    stt(St[d_][:], St[d_][:], Esc[d_][:, 0, c:c + 1], utmp[:], ALU.mult, ALU.add)
            if HG_STOP == 's2':
                break
            for g in range(NG):
                gs = slice(g * 512, (g + 1) * 512)
                ob = bank3(g % 2, 4, 128)
                for c4 in range(4):
                    c = g * 4 + c4
                    cs = slice(c * 128, (c + 1) * 128)
                    ab = bank3(2 + c4 % 2, 4, 128)
                    for d_ in range(2):
                        mm(ab[:, d_, :], qk[("k", d_)][:, cs], qk[("q", d_)][:, cs], True, True)
                        cpred(AT[(c4 % 2) * 2 + d_][:], (maskf if d_ == 0 else maskb)[:], ab[:, d_, :])
                    mm(ob[:, c4, :], Vh[:, c, :], AT[(c4 % 2) * 2 + 0][:], True, False)
                    mm(ob[:, c4, :], Vh[:, c, :], AT[(c4 % 2) * 2 + 1][:], False, False)
                    mm(ob[:, c4, :], Sbf[0][:, c, :], qk[("q", 0)][:, cs], False, False)
                    mm(ob[:, c4, :], Sbf[1][:, c, :], qk[("q", 1)][:, cs], False, True)
                obf = bank(g % 2)
                act(sqb[:], obf[:], AF.Square)
                mm(bank(6)[:], onesb[:], sqb[:], True, True)
                act(rstd[:], bank(6)[:], AF.Sqrt, scale=1.0 / 128, bias=epsb[:, 0:1])
                recip(rstd[:], rstd[:])
                gb = bank(4 + g % 2)
                for k in range(8):
                    mm(gb[:], wgh[:, k, :], xnT[:, k, gs], k == 0, k == 7)
                act(sgt[:], gb[:], AF.Silu)
                tt(tq[:], obf[:], rstd[:], ALU.mult)
                stt(hoT[:, h, gs], tq[:], prm[:, P_HNW + l:P_HNW + l + 1], sgt[:], ALU.mult, ALU.mult)
            if HG_STOP == 's3':
                break
        if HG_STOP:
            dump_merged(); break
        for c in range(8):
            who = arena.load(dwho[l].rearrange("(h p) n -> p h n", p=128)[:, :, c * 128:(c + 1) * 128], 8, 128)
            wga = arena.load(wcols(wl, C_GA + 1024 + c * 128, 128), 8, 128)
            for g in range(NG):
                gs = slice(g * 512, (g + 1) * 512)
                yb_ = bank(g % 2)
                for hh in range(8):
                    mm(yb_[:], who[:, hh, :], hoT[:, hh, gs], hh == 0, hh == 7)
                gb = bank(2 + g % 2)
                for k in range(8):
                    mm(gb[:], wga[:, k, :], xnT[:, k, gs], k == 0, k == 7)
                act(sgt[:], gb[:], AF.Sigmoid)
                tt(tq[:], yb_[:], sgt[:], ALU.mult)
                tt(merged[:, c, gs], merged[:, c, gs], tq[:], ALU.add)
        if debug_stage == "hgrn" and l == 0:
            dump_merged(); break

        pool.reset()
        Ap = pool([NT, D], BF16)
        Bp = pool([NT, D], BF16)
        Zc = pool([8, 256], BF16)
        Zs = pool([8, 256], BF16)
        fuT = pool([2, 256], BF16)
        cgt = pool([2, 256], BF16)
        sgt2 = pool([2, 256], BF16)
        wfn = pool([8, D], BF16)
        arena.set(POOL_B + pool.cur, POOL_SZ - pool.cur)
        dma("pool", cgt[:], None, in_ap=dcg[0].rearrange("(c p) n -> p c n", p=128))
        dma("pool", sgt2[:], None, in_ap=dcg[1].rearrange("(c p) n -> p c n", p=128))
        dma("pool", wfn[:], None, in_ap=dwfn[l].rearrange("(c p) n -> p c n", p=128))
        for tg in range(8):
            tgs = slice(tg * 256, (tg + 1) * 256)
            for G in range(4):
                wfu = arena.load(wcols(wl, C_FU + G * 256, 256), 8, 256)
                for cc in range(2):
                    b_ = bank(cc, 256)
                    for k in range(8):
                        mm(b_[:], wfu[:, k, cc * 128:(cc + 1) * 128], xnT[:, k, tgs], k == 0, k == 7)
                    cpy(fuT[:, cc, :], b_[:], eng="act")
                for (tab, Z, bb) in ((cgt, Zc, 2), (sgt2, Zs, 4)):
                    for c2 in range(2):
                        b_ = bank(bb + c2, 256)
                        for cc in range(2):
                            mm(b_[:], tab[:, cc, c2 * 128:(c2 + 1) * 128], fuT[:, cc, :], cc == 0, cc == 1)
                        cpy(Z[:, G * 2 + c2, :], b_[:])
            for t2 in range(2):
                t = tg * 2 + t2
                for (Z, dstp, bb) in ((Zc, Ap, 0), (Zs, Bp, 2)):
                    for half in range(2):
                        b_ = bank(bb + half)
                        for c_ in range(8):
                            mm(b_[:], Z[:, c_, t2 * 128:(t2 + 1) * 128], wfn[:, c_, half * 512:(half + 1) * 512], c_ == 0, c_ == 7)
                        cpy(dstp[:, t, half * 512:(half + 1) * 512], b_[:], eng="act" if half == 0 else "dve")
        pool.cur = 2 * NT * D * 2
        yst = pool([8, 512], F32)
        sgt = pool([512], F32)
        arena.set(POOL_B + pool.cur, POOL_SZ - pool.cur)
        for kg in range(NG):
            ks = slice(kg * 512, (kg + 1) * 512)
            for n in range(NT):
                ct = arena.load(ddft[0, n * 128:(n + 1) * 128, ks].rearrange("p (c n) -> p c n", c=1), 1, 512)
                st_ = arena.load(ddft[1, n * 128:(n + 1) * 128, ks].rearrange("p (c n) -> p c n", c=1), 1, 512)
                for c in range(8):
                    mm(bank(c)[:], Ap[:, n, c * 128:(c + 1) * 128], ct[:, 0, :], n == 0, False)
                    mm(bank(c)[:], Bp[:, n, c * 128:(c + 1) * 128], st_[:, 0, :], False, n == NT - 1)
            for c in range(8):
                cpy(yst[:, c, :], bank(c)[:], eng="act")
            for c in range(8):
                wga = arena.load(wcols(wl, C_GA + 2048 + c * 128, 128), 8, 128)
                gb = bank(c % 2)
                for k in range(8):
                    mm(gb[:], wga[:, k, :], xnT[:, k, ks], k == 0, k == 7)
                act(sgt[:], gb[:], AF.Sigmoid)
                tt(yst[:, c, :], yst[:, c, :], sgt[:], ALU.mult)
                tt(merged[:, c, ks], merged[:, c, ks], yst[:, c, :], ALU.add)
        if debug_stage == "fft" and l == 0:
            dump_merged(); break

        pool.reset()
        nwb = pool([D], F32)
        dma("sp", nwb[:], None, in_ap=dnw[l * 4 + 1: l * 4 + 2, :].partition_broadcast(128))
        xb = [pool([D], F32) for _ in range(2)]
        yb = [pool([D], F32) for _ in range(2)]
        sc = pool([2, 8], F32)
        mb = [pool([8, 128], BF16) for _ in range(2)]
        wout = pool([8, D], BF16)
        dma("pool", wout[:], None, in_ap=dwout[l].rearrange("(c p) n -> p c n", p=128))
        for t in range(NT):
            cpy(mb[t % 2][:], merged[:, :, t * 128:(t + 1) * 128], eng="act")
            hb = [bank((t % 2) * 2 + hh) for hh in range(2)]
            for hh in range(2):
                for c in range(8):
                    mm(hb[hh][:], mb[t % 2][:, c, :], wout[:, c, hh * 512:(hh + 1) * 512], c == 0, c == 7)
            residual_tile(t, [hb[0][:], hb[1][:]], nwb, xb, yb, sc, src_a, vxs)
        if debug_stage == "x1" and l == 0:
            break

        norm_phase(l, 2, vxs)
        ACT_B = MRG_B
        actT = sview(ACT_B, [22, S], BF16)
        fb = ACT_B + 22 * S * 2
        ffp = Bump(fb, BIGB - fb)
        wdn = ffp([22, D], BF16)
        hraw = [ffp([S + 2], F32) for _ in range(2)]
        hc = [ffp([S], F32) for _ in range(2)]
        for i in range(2):
            memset(hraw[i][:, 0:1], 0.0)
            memset(hraw[i][:, S + 1:S + 2], 0.0)
        arena.set(fb + ffp.cur, BIGB - fb - ffp.cur)
        dma("pool", wdn[:], None, in_ap=dwdn[l].rearrange("(f p) n -> p f n", p=128))
        wupl = dwup[l]
        for f in range(22):
            wgu = [arena.load(wcols(wupl, f * 128, 128), 8, 128), arena.load(wcols(wupl, DFF + f * 128, 128), 8, 128)]
            for i in range(2):
                for g in range(NG):
                    gs = slice(g * 512, (g + 1) * 512)
                    b_ = bank(i * 4 + g)
                    for k in range(8):
                        mm(b_[:], wgu[i][:, k, :], xnT[:, k, gs], k == 0, k == 7)
                ch = i * 22 + f
                cw = lambda j: prm[:, P_CW + (l * 3 + j) * 44 + ch: P_CW + (l * 3 + j) * 44 + ch + 1]
                cb_ = prm[:, P_CB + l * 44 + ch: P_CB + l * 44 + ch + 1]
                for g in range(NG):
                    gs = slice(g * 512, (g + 1) * 512)
                    b_ = bank(i * 4 + g)
                    cpy(hraw[i][:, 1 + g * 512: 1 + (g + 1) * 512], b_[:], eng="act")
                    act(hc[i][:, gs], b_[:], AF.Identity, scale=cw(1), bias=cb_)
                stt(hc[i][:], hraw[i][:, 0:S], cw(0), hc[i][:], ALU.mult, ALU.add)
                stt(hc[i][:], hraw[i][:, 2:S + 2], cw(2), hc[i][:], ALU.mult, ALU.add)
            act(hc[0][:], hc[0][:], AF.Gelu_apprx_tanh)
            tt(actT[:, f, :], hc[0][:], hc[1][:], ALU.mult)
        rp = Bump(fb + 22 * D * 2, BIGB - fb - 22 * D * 2)
        nwb = rp([D], F32)
        dma("sp", nwb[:], None, in_ap=dnw[l * 4 + 3: l * 4 + 4, :].partition_broadcast(128))
        xb = [rp([D], F32) for _ in range(2)]
        yb = [rp([D], F32) for _ in range(2)]
        sc = rp([2, 8], F32)
        last = (l == nlayers - 1)
        dst = vout if last else vxs
        for t in range(NT):
            hb = [bank((t % 2) * 2 + hh) for hh in range(2)]
            for hh in range(2):
                for f in range(22):
                    mm(hb[hh][:], actT[:, f, t * 128:(t + 1) * 128], wdn[:, f, hh * 512:(hh + 1) * 512], f == 0, f == 21)
            iid = residual_tile(t, [hb[0][:], hb[1][:]], nwb, xb, yb, sc, vxs, dst)
            if last:
                out_dmas.append(iid)
    if out_dmas:
        P.add("sp", lambda e: e.nop(), extra_deps=out_dmas)

    P.finalize()
    sems_eng = {e: [es.enter_context(nc.semaphore("s_%s_%d" % (e, i))) for i in range(P.n_ms[e] // EPOCH + 1)] for e in ENGS}
    sems_dma = {k: es.enter_context(nc.semaphore("d_%s" % k)) for k in P.dma_count}
    block = es.enter_context(nc.Block())
    P.emit(sems_eng, sems_dma, block)
    es.close()
    return nc, P


def _constants():
    cst = np.zeros((128, 512), np.float32)
    cst[:, 0:128] = np.eye(128, dtype=np.float32)
    cst[:, 128:256] = 1.0
    s_ = np.arange(128)[:, None]; t_ = np.arange(128)[None, :]
    cst[:, 256:384] = (s_ <= t_).astype(np.float32)
    cst[:, 384:512] = (s_ >= t_).astype(np.float32)
    gam = 1.0 - 2.0 ** (-5.0 - np.arange(4, dtype=np.float64))
    m_ = np.arange(128)[:, None]; c_ = np.arange(3968)[None, :]
    rmask = np.stack([(256.0 ** -0.5) * g ** np.abs(c_ - 1920 - m_) for g in gam]).astype(np.float32)
    i_ = np.arange(256, dtype=np.float64)
    ang = 2 * np.pi * np.outer(i_, i_) / 256.0
    cg = np.stack([np.cos(ang), np.sin(ang)]).astype(np.float32) / 16.0
    n_ = np.arange(S, dtype=np.int64)
    nk = (np.outer(n_, n_) % S).astype(np.float64)
    ang = 2 * np.pi * nk / S
    dft = np.stack([np.cos(ang), -np.sin(ang)]).astype(np.float32) / np.float32(np.sqrt(S))
    return cst, rmask, cg, dft


_CACHE = {}


def kernel(x, positions, norm_w, w_in, hgrn_lb_logits, hgrn_norm_w, w_ret_o, w_hgrn_o, w_fnet, w_out,
           w_up, conv_w, conv_b, w_down):
    if "nc" not in _CACHE:
        _CACHE["nc"] = build_program()[0]
        _CACHE["const"] = _constants()
    nc = _CACHE["nc"]
    cst, rmask, cg, dft = _CACHE["const"]
    f32 = np.float32
    prm = np.zeros((128, NPRM), f32)
    prm[:, P_LBL:P_LBL + 32] = np.asarray(hgrn_lb_logits, f32).reshape(2, DEPTH, 8, 128).transpose(3, 0, 1, 2).reshape(128, 32)
    prm[:, P_HNW:P_HNW + 2] = np.asarray(hgrn_norm_w, f32).T
    prm[:, P_CW:P_CW + 264] = np.asarray(conv_w, f32).reshape(DEPTH, 3, 44, 128).transpose(3, 0, 1, 2).reshape(128, 264)
    prm[:, P_CB:P_CB + 88] = np.asarray(conv_b, f32).reshape(DEPTH, 44, 128).transpose(2, 0, 1).reshape(128, 88)
    prm[:, P_IF] = (10000.0 ** (-np.arange(128, dtype=np.float32) / np.float32(128))).astype(f32)
    shared = dict(norm_w=np.ascontiguousarray(np.asarray(norm_w, f32).reshape(DEPTH * 4, D)),
                  w_in=np.asarray(w_in, f32), w_ret_o=np.asarray(w_ret_o, f32), w_hgrn_o=np.asarray(w_hgrn_o, f32),
                  w_fnet=np.asarray(w_fnet, f32), w_out=np.asarray(w_out, f32), w_up=np.asarray(w_up, f32),
                  w_down=np.asarray(w_down, f32), prm=prm, rmask=rmask, cg=cg, dft=dft, cst=cst)
    xs = np.asarray(x, f32)
    pos = np.asarray(positions, np.int32)
    in_maps = [dict(shared, x=np.ascontiguousarray(xs[b]), pos=np.ascontiguousarray(pos[b:b + 1])) for b in range(8)]
    res = run_bass_kernel_spmd(nc, in_maps, core_ids=list(range(8)))
    return np.stack([np.asarray(r["out"], f32) for r in res.results], axis=0)
```

```python
import numpy as np
import concourse.bass as bass
import concourse.mybir as mybir

F32 = mybir.dt.float32
BF16 = mybir.dt.bfloat16
I32 = mybir.dt.int32
U8 = mybir.dt.uint8
AF = mybir.ActivationFunctionType
ALU = mybir.AluOpType

ENGS = ("pe", "act", "dve", "pool", "sp")
BUCKET = 2048
EPOCH = 12000


class _Rec:
    __slots__ = ("lo", "hi", "iid", "eng", "alive")

    def __init__(self, lo, hi, iid, eng):
        self.lo, self.hi, self.iid, self.eng, self.alive = lo, hi, iid, eng, True


class _Space:
    def __init__(self):
        self.w = {}
        self.r = {}

    @staticmethod
    def _bk(lo, hi):
        return range(lo // BUCKET, (hi - 1) // BUCKET + 1)

    def query(self, table, lo, hi, out):
        for b in self._bk(lo, hi):
            lst = table.get(b)
            if not lst:
                continue
            for rec in lst:
                if rec.alive and rec.lo < hi and lo < rec.hi:
                    out.add(rec.iid)

    def kill_covered(self, table, lo, hi):
        for b in self._bk(lo, hi):
            lst = table.get(b)
            if not lst:
                continue
            keep = []
            for rec in lst:
                if not rec.alive:
                    continue
                if lo <= rec.lo and rec.hi <= hi:
                    rec.alive = False
                else:
                    keep.append(rec)
            table[b] = keep

    def insert(self, table, rec):
        for b in self._bk(rec.lo, rec.hi):
            table.setdefault(b, []).append(rec)

    def add_reader(self, lo, hi, iid, eng):
        b0 = lo // BUCKET
        lst = self.r.get(b0)
        if lst and eng != "dma":
            for rec in lst:
                if rec.alive and rec.lo == lo and rec.hi == hi and rec.eng == eng:
                    rec.iid = iid
                    return
        self.insert(self.r, _Rec(lo, hi, iid, eng))


class Acc:
    __slots__ = ("ap", "regs")

    def __init__(self, ap, regs):
        self.ap, self.regs = ap, regs


def _regions(shape, idx):
    if len(shape) == 1:
        return [(idx[0][0], idx[0][1])]
    inner = int(np.prod(shape[1:]))
    sub = _regions(shape[1:], idx[1:])
    if len(sub) == 1 and sub[0] == (0, inner):
        return [(idx[0][0] * inner, idx[0][1] * inner)]
    out = []
    for i in range(idx[0][0], idx[0][1]):
        for (l, h) in sub:
            out.append((i * inner + l, i * inner + h))
    return out


class View:
    def __init__(self, space, ap, free_shape, esize, base_bytes=0, track=True):
        self.space, self.ap, self.shape, self.esize, self.base, self.track = space, ap, tuple(free_shape), esize, base_bytes, track

    def __getitem__(self, key):
        if not isinstance(key, tuple):
            key = (key,)
        key = key + (slice(None),) * (1 + len(self.shape) - len(key))
        ap = self.ap[key]
        if not self.track:
            return Acc(ap, [])
        idx = []
        for k, n in zip(key[1:], self.shape):
            if isinstance(k, slice):
                assert k.step in (None, 1)
                idx.append((k.start or 0, n if k.stop is None else k.stop))
            else:
                idx.append((k, k + 1))
        regs = _regions(self.shape, idx)
        if len(regs) > 48:
            regs = [(regs[0][0], regs[-1][1])]
        out = [(self.space, self.base + l * self.esize, self.base + h * self.esize) for (l, h) in regs]
        if self.space == "ps":
            lo = min(r[1] for r in out) // 2048 * 2048
            hi = (max(r[2] for r in out) + 2047) // 2048 * 2048
            out = [("ps", lo, hi)]
        return Acc(ap, out)


class Ins:
    __slots__ = ("eng", "fn", "deps", "dma", "semkey", "semval", "ms", "needs_inc", "waits", "name")


class Prog:
    def __init__(self, nc):
        self.nc = nc
        self.ins = []
        self.streams = {e: [] for e in ENGS}
        self.spaces = {}
        self.dma_count = {}
        self.cur_group = None

    def _sp(self, name):
        s = self.spaces.get(name)
        if s is None:
            s = self.spaces[name] = _Space()
        return s

    def add(self, eng, fn, reads=(), writes=(), dma=False, semkey=None, extra_deps=(), name=""):
        iid = len(self.ins)
        deps = set(extra_deps)
        rregs = [r for a in reads for r in a.regs]
        wregs = [r for a in writes for r in a.regs]
        for (sp, lo, hi) in rregs:
            s = self._sp(sp)
            s.query(s.w, lo, hi, deps)
            if sp == "ps":
                tmp = set()
                s.query(s.r, lo, hi, tmp)
                for d in tmp:
                    if self.ins[d].eng != eng:
                        deps.add(d)
        for (sp, lo, hi) in wregs:
            s = self._sp(sp)
            s.query(s.w, lo, hi, deps)
            s.query(s.r, lo, hi, deps)
        for (sp, lo, hi) in wregs:
            s = self._sp(sp)
            s.kill_covered(s.w, lo, hi)
            s.kill_covered(s.r, lo, hi)
            s.insert(s.w, _Rec(lo, hi, iid, "dma" if dma else eng))
        for (sp, lo, hi) in rregs:
            s = self._sp(sp)
            s.add_reader(lo, hi, iid, "dma" if dma else eng)
        deps.discard(iid)
        I = Ins()
        I.eng, I.fn, I.deps, I.dma, I.semkey, I.name = eng, fn, deps, dma, semkey, name
        I.semval = None
        I.needs_inc = dma
        I.waits = []
        I.ms = None
        if dma:
            assert semkey is not None
            self.dma_count[semkey] = self.dma_count.get(semkey, 0) + 16
            I.semval = self.dma_count[semkey]
            if self.cur_group is not None:
                self.cur_group.append(I)
        self.ins.append(I)
        self.streams[eng].append(iid)
        return iid

    def begin_group(self):
        self.cur_group = []

    def end_group(self):
        g = self.cur_group
        self.cur_group = None
        if g:
            tot = max(i.semval for i in g)
            for i in g:
                i.semval = tot
        return g

    def finalize(self):
        ins = self.ins
        for I in ins:
            for d in I.deps:
                D = ins[d]
                if not D.dma:
                    if D.eng == "pe" and I.eng == "pe" and not I.dma:
                        continue
                    D.needs_inc = True
        for e in ENGS:
            g = 0
            for iid in self.streams[e]:
                I = ins[iid]
                if I.needs_inc and not I.dma:
                    I.ms = g
                    g += 1
        self.n_ms = {e: sum(1 for iid in self.streams[e] if ins[iid].ms is not None) for e in ENGS}
        waited = {e: {p: -1 for p in ENGS} for e in ENGS}
        waited_dma = {e: {} for e in ENGS}
        last_ms = {}
        for e in ENGS:
            cur = -1
            for iid in self.streams[e]:
                if ins[iid].ms is not None:
                    cur = ins[iid].ms
                last_ms[iid] = cur
        for I in ins:
            need = {}
            need_dma = {}
            for d in I.deps:
                D = ins[d]
                if D.dma:
                    need_dma[D.semkey] = max(need_dma.get(D.semkey, 0), D.semval)
                else:
                    if D.eng == "pe" and I.eng == "pe" and not I.dma:
                        continue
                    need[D.eng] = max(need.get(D.eng, -1), D.ms)
            for p, m in need.items():
                if waited[I.eng][p] >= m:
                    continue
                waited[I.eng][p] = m
                I.waits.append(("c", p, m))
            for k, v in need_dma.items():
                if waited_dma[I.eng].get(k, 0) >= v:
                    continue
                waited_dma[I.eng][k] = v
                I.waits.append(("d", k, v))

    def emit(self, sems_eng, sems_dma, block):
        nc = self.nc
        ins = self.ins

        def run(engname, eng):
            for iid in self.streams[engname]:
                I = ins[iid]
                for (kind, a, b) in I.waits:
                    if kind == "c":
                        eng.wait_ge(sems_eng[a][b // EPOCH], b % EPOCH + 1)
                    else:
                        eng.wait_ge(sems_dma[a], b)
                r = I.fn(eng)
                if I.dma:
                    r.then_inc(sems_dma[I.semkey], 16)
                elif I.ms is not None:
                    r.then_inc(sems_eng[engname][I.ms // EPOCH], 1)

        @block.tensor
        def _(e):
            run("pe", e)

        @block.scalar
        def _(e):
            run("act", e)

        @block.vector
        def _(e):
            run("dve", e)

        @block.gpsimd
        def _(e):
            run("pool", e)

        @block.sync
        def _(e):
            run("sp", e)

import contextlib, os
HG_STOP = os.environ.get('HG_STOP', '')
HG_S1 = int(os.environ.get('HG_S1', '9'))
from concourse.bass_utils import run_bass_kernel_spmd

D = 1024
S = 2048
DEPTH = 2
DFF = 2816
DIN = 15360
NT = 16
NG = 4
EPSV = 1e-6
BIGB = 212800
C_RQ, C_RK, C_RV, C_RG, C_HQ, C_HZF, C_HZB, C_HI, C_HG, C_FU, C_GA = 0, 1024, 2048, 4096, 6144, 7168, 8192, 9216, 10240, 11264, 12288
MAGIC = 12582912.0
TWO_PI = 6.28318
P_LBL = 0
P_HNW = 32
P_CW = 34
P_CB = 298
P_IF = 386
NPRM = 388


def build_program(debug_stage=None, nlayers=DEPTH):
    nc = bass.Bass("TRN2", target_bir_lowering=False)
    dx = nc.dram_tensor("x", [S, D], F32, kind="ExternalInput").ap()
    dpos = nc.dram_tensor("pos", [1, S], I32, kind="ExternalInput").ap()
    dnw = nc.dram_tensor("norm_w", [DEPTH * 4, D], F32, kind="ExternalInput").ap()
    dwin = nc.dram_tensor("w_in", [DEPTH, D, DIN], F32, kind="ExternalInput").ap()
    dwro = nc.dram_tensor("w_ret_o", [DEPTH, 2048, D], F32, kind="ExternalInput").ap()
    dwho = nc.dram_tensor("w_hgrn_o", [DEPTH, D, D], F32, kind="ExternalInput").ap()
    dwfn = nc.dram_tensor("w_fnet", [DEPTH, D, D], F32, kind="ExternalInput").ap()
    dwout = nc.dram_tensor("w_out", [DEPTH, D, D], F32, kind="ExternalInput").ap()
    dwup = nc.dram_tensor("w_up", [DEPTH, D, 2 * DFF], F32, kind="ExternalInput").ap()
    dwdn = nc.dram_tensor("w_down", [DEPTH, DFF, D], F32, kind="ExternalInput").ap()
    dprm = nc.dram_tensor("prm", [128, NPRM], F32, kind="ExternalInput").ap()
    dmask = nc.dram_tensor("rmask", [4, 128, 3968], F32, kind="ExternalInput").ap()
    dcg = nc.dram_tensor("cg", [2, 256, 256], F32, kind="ExternalInput").ap()
    ddft = nc.dram_tensor("dft", [2, S, S], F32, kind="ExternalInput").ap()
    dcst = nc.dram_tensor("cst", [128, 512], F32, kind="ExternalInput").ap()
    dout = nc.dram_tensor("out", [S, D], F32, kind="ExternalOutput").ap()
    dxs = nc.dram_tensor("xs", [S, D], F32, kind="Internal").ap()
    ddbg = None
    if debug_stage is not None:
        ddbg = nc.dram_tensor("dbg", [128, 8, S], F32, kind="ExternalOutput").ap()

    es = contextlib.ExitStack()
    big = es.enter_context(nc.sbuf_tensor("big", [128, BIGB // 4], F32))
    ps = es.enter_context(nc.psum_tensor("ps", [128, 8 * 512], F32))
    P = Prog(nc)

    def sview(offb, shape, dt):
        esz = 2 if dt == BF16 else 4
        n = int(np.prod(shape))
        assert offb % 4 == 0 and (n * esz) % 4 == 0, (offb, shape)
        assert offb + n * esz <= BIGB, ("SBUF overflow", offb, shape)
        ap = big[:, offb // 4: offb // 4 + n * esz // 4]
        if dt != F32:
            ap = ap.bitcast(dt)
        if len(shape) > 1:
            names = " ".join("d%d" % i for i in range(len(shape)))
            ap = ap.rearrange("p (%s) -> p %s" % (names, names), **{"d%d" % i: shape[i] for i in range(len(shape) - 1)})
        return View("sb", ap, shape, esz, offb)

    def bank(b, n=512):
        return View("ps", ps[:, b * 512: b * 512 + n], [n], 4, b * 2048)

    def bank3(b, a, n):
        return View("ps", ps[:, b * 512: b * 512 + a * n].rearrange("p (a n) -> p a n", a=a), [a, n], 4, b * 2048)

    psb = View("ps", ps[:, 7 * 512: 8 * 512].bitcast(BF16).rearrange("p (a b) -> p a b", a=8), [8, 128], 2, 7 * 2048)

    def dview(ap, name, track=True):
        return View(name, ap.rearrange("(t p) d -> p t d", p=128), [NT, D], 4, 0, track=track)

    vx = dview(dx, "d_x", track=False)
    vxs = dview(dxs, "d_xs")
    vout = dview(dout, "d_out")

    def mm(out, lhsT, rhs, start, stop):
        P.add("pe", lambda e: e.matmul(out.ap, lhsT=lhsT.ap, rhs=rhs.ap, start=start, stop=stop), reads=[lhsT, rhs], writes=[out])

    def tr(out, in_, ident):
        P.add("pe", lambda e: e.transpose(out=out.ap, in_=in_.ap, identity=ident.ap), reads=[in_, ident], writes=[out])

    def act(out, in_, func, scale=None, bias=None, accum=None, eng="act"):
        rd = [in_]
        kw = {}
        if scale is not None:
            if isinstance(scale, Acc):
                rd.append(scale); kw["scale"] = scale.ap
            else:
                kw["scale"] = scale
        if bias is not None:
            if isinstance(bias, Acc):
                rd.append(bias); kw["bias"] = bias.ap
            else:
                kw["bias"] = bias
        wr = [out]
        if accum is not None:
            wr.append(accum); kw["accum_out"] = accum.ap
        P.add("act", lambda e: e.activation(out=out.ap, in_=in_.ap, func=func, **kw), reads=rd, writes=wr)

    def tt(out, a, b, op, eng="dve"):
        P.add(eng, lambda e: e.tensor_tensor(out=out.ap, in0=a.ap, in1=b.ap, op=op), reads=[a, b], writes=[out])

    def ts(out, a, s1, s2, op0, op1=None, eng="dve"):
        rd = [a]
        v1 = s1.ap if isinstance(s1, Acc) else s1
        v2 = s2.ap if isinstance(s2, Acc) else s2
        if isinstance(s1, Acc): rd.append(s1)
        if isinstance(s2, Acc): rd.append(s2)
        if op1 is None:
            P.add(eng, lambda e: e.tensor_scalar(out=out.ap, in0=a.ap, scalar1=v1, scalar2=None, op0=op0), reads=rd, writes=[out])
        else:
            P.add(eng, lambda e: e.tensor_scalar(out=out.ap, in0=a.ap, scalar1=v1, scalar2=v2, op0=op0, op1=op1), reads=rd, writes=[out])

    def stt(out, a, s, b, op0, op1):
        rd = [a, b]
        v = s.ap if isinstance(s, Acc) else s
        if isinstance(s, Acc): rd.append(s)
        P.add("dve", lambda e: e.scalar_tensor_tensor(out=out.ap, in0=a.ap, scalar=v, in1=b.ap, op0=op0, op1=op1), reads=rd, writes=[out])

    def cpy(out, in_, eng="dve"):
        if eng == "act":
            act(out, in_, AF.Copy)
        else:
            P.add(eng, lambda e: e.tensor_copy(out=out.ap, in_=in_.ap), reads=[in_], writes=[out])

    def memset(out, v, eng="dve"):
        P.add(eng, lambda e: e.memset(out.ap, v), writes=[out])

    def recip(out, in_):
        P.add("dve", lambda e: e.reciprocal(out=out.ap, in_=in_.ap), reads=[in_], writes=[out])

    def scan(out, ones, data):
        P.add("dve", lambda e: e.tensor_tensor_scan(out=out.ap, data0=ones.ap, data1=data.ap, initial=0.0, op0=ALU.mult, op1=ALU.add), reads=[ones, data], writes=[out])

    def cpred(out, mask, data):
        P.add("dve", lambda e: e.copy_predicated(out=out.ap, mask=mask.ap, data=data.ap), reads=[mask, data, out], writes=[out])

    dma_n = [0]

    def dma(eng, out_acc, in_acc, out_ap=None, in_ap=None, key=None):
        i = dma_n[0]; dma_n[0] += 1
        k = key or ("q_%s_%d" % (eng, i % 12))
        oa = out_ap if out_ap is not None else out_acc.ap
        ia = in_ap if in_ap is not None else in_acc.ap
        rd = [in_acc] if in_acc is not None else []
        wr = [out_acc] if out_acc is not None else []
        prev = dma_last.get(k)
        iid = P.add(eng, lambda e: e.dma_start(out=oa, in_=ia), reads=rd, writes=wr, dma=True, semkey=k,
                    extra_deps=[prev] if prev is not None else [])
        dma_last[k] = iid
        return iid

    dma_last = {}

    CB = 0
    ident = sview(CB + 0, [128], BF16)
    onesb = sview(CB + 256, [128], BF16)
    maskf = sview(CB + 512, [128], I32)
    maskb = sview(CB + 1024, [128], I32)
    onesf = sview(CB + 1536, [128], F32)
    epsb = sview(CB + 2048, [1], F32)
    prm = sview(CB + 2112, [NPRM], F32)
    hp = sview(CB + 3680, [3, 2, 8], F32)
    cst32 = sview(CB + 3904, [512], F32)
    XNT_B = 6144
    xnT = sview(XNT_B, [8, S], BF16)
    MRG_B = XNT_B + 32768
    merged = sview(MRG_B, [8, S], F32)
    POOL_B = MRG_B + 65536
    POOL_SZ = BIGB - POOL_B

    class Bump:
        def __init__(self, base, size):
            self.base, self.size, self.cur = base, size, 0

        def reset(self, base=None, size=None):
            if base is not None:
                self.base, self.size = base, size
            self.cur = 0

        def __call__(self, shape, dt):
            esz = 2 if dt == BF16 else 4
            n = int(np.prod(shape)) * esz
            n = (n + 63) // 64 * 64
            assert self.cur + n <= self.size, ("pool overflow", self.cur, n, self.size)
            v = sview(self.base + self.cur, shape, dt)
            self.cur += n
            return v

    pool = Bump(POOL_B, POOL_SZ)

    class Arena:
        def __init__(self):
            self.base = self.size = self.cur = 0
            self.n = 0
            self.hist = []

        def set(self, base, size):
            self.base, self.size, self.cur = base, size, 0

        def load(self, src_ap, nch, ncols):
            nb = nch * ncols * 2
            nb = (nb + 63) // 64 * 64
            assert nb <= self.size, ("arena too small", nb, self.size)
            if self.cur + nb > self.size:
                self.cur = 0
            v = sview(self.base + self.cur, [nch, ncols], BF16)
            self.cur += nb
            k = "w%d" % (self.n % 8)
            self.n += 1
            prev = dma_last.get(k)
            iid = P.add("pool", lambda e: e.dma_start(out=v[:].ap, in_=src_ap), writes=[v[:]], dma=True, semkey=k,
                        extra_deps=[prev] if prev is not None else [])
            dma_last[k] = iid
            return v

    arena = Arena()

    def wcols(wl, c0, n):
        return wl.rearrange("(c p) n -> p c n", p=128)[:, :, c0:c0 + n]

    dma("sp", cst32[:], None, in_ap=dcst[:, :])
    cpy(ident[:], cst32[:, 0:128])
    cpy(onesb[:], cst32[:, 128:256])
    cpy(maskf[:], cst32[:, 256:384])
    cpy(maskb[:], cst32[:, 384:512])
    memset(onesf[:], 1.0)
    memset(epsb[:], EPSV)
    dma("sp", prm[:], None, in_ap=dprm[:, :])

    def norm_phase(l, ni, src):
        pool.reset()
        nwb = pool([D], F32)
        dma("sp", nwb[:], None, in_ap=dnw[l * 4 + ni: l * 4 + ni + 1, :].partition_broadcast(128))
        xb = [pool([D], F32) for _ in range(2)]
        junk = pool([D], F32)
        xnb = [pool([8, 128], BF16) for _ in range(2)]
        sc = pool([2, 4], F32)
        for t in range(NT):
            xt = xb[t % 2]
            dma("sp", xt[:], src[:, t, :])
            act(junk[:], xt[:], AF.Square, accum=sc[:, t % 2, 0:1])
            act(sc[:, t % 2, 1:2], sc[:, t % 2, 0:1], AF.Sqrt, scale=1.0 / D, bias=epsb[:, 0:1])
            recip(sc[:, t % 2, 2:3], sc[:, t % 2, 1:2])
            xn_flat = View("sb", xnb[t % 2].ap.rearrange("p a b -> p (a b)"), [D], 2, xnb[t % 2].base)
            stt(xn_flat[:], xt[:], sc[:, t % 2, 2:3], nwb[:], ALU.mult, ALU.mult)
            for c in range(8):
                tr(psb[:, c, :], xnb[t % 2][:, c, :], ident[:])
            cpy(xnT[:, :, t * 128:(t + 1) * 128], psb[:], eng="act")

    def residual_tile(t, halves, nwb, xb, yb, sc, src, dst):
        i = t % 2
        junk = yb[i]
        act(junk[:, 0:512], halves[0], AF.Square, accum=sc[:, i, 0:1])
        act(junk[:, 512:1024], halves[1], AF.Square, accum=sc[:, i, 1:2])
        tt(sc[:, i, 2:3], sc[:, i, 0:1], sc[:, i, 1:2], ALU.add)
        act(sc[:, i, 3:4], sc[:, i, 2:3], AF.Sqrt, scale=1.0 / D, bias=epsb[:, 0:1])
        recip(sc[:, i, 4:5], sc[:, i, 3:4])
        dma("sp", xb[i][:], src[:, t, :])
        for h in range(2):
            stt(yb[i][:, h * 512:(h + 1) * 512], halves[h], sc[:, i, 4:5], nwb[:, h * 512:(h + 1) * 512], ALU.mult, ALU.mult)
        tt(yb[i][:], yb[i][:], xb[i][:], ALU.add)
        return dma("sp", dst[:, t, :], yb[i][:])

    def dump_merged():
        iid = dma("sp", None, merged[:], out_ap=ddbg[:, :, :])
        P.add("sp", lambda e: e.nop(), extra_deps=[iid])

    out_dmas = []

    for l in range(nlayers):
        wl = dwin[l]
        src_a = vx if l == 0 else vxs
        norm_phase(l, 0, src_a)

        pool.reset()
        pi = pool([S], I32)
        pf = pool([S], F32)
        u1 = pool([S], F32)
        cosT = pool([S], F32)
        sinT = pool([S], F32)
        dma("sp", pi[:], None, in_ap=dpos[0:1, :].partition_broadcast(128))
        cpy(pf[:], pi[:])
        ts(u1[:], pf[:], prm[:, P_IF:P_IF + 1], 1.0 / (2 * np.pi), ALU.mult, ALU.mult)
        ts(pf[:], u1[:], MAGIC, MAGIC, ALU.add, ALU.subtract)
        tt(pf[:], u1[:], pf[:], ALU.subtract)
        act(sinT[:], pf[:], AF.Sin, scale=TWO_PI)
        ts(u1[:], u1[:], 0.25, None, ALU.add)
        ts(pf[:], u1[:], MAGIC, MAGIC, ALU.add, ALU.subtract)
        tt(pf[:], u1[:], pf[:], ALU.subtract)
        act(cosT[:], pf[:], AF.Sin, scale=TWO_PI)
        pool.cur = 0
        _ = pool([S], I32); _ = pool([S], F32); _ = pool([S], F32); _ = pool([S], F32); _ = pool([S], F32)
        pool.cur = 0
        qT = pool([2, S], BF16)
        kT = pool([2, S], BF16)
        tmp1 = pool([512], F32); tmp2 = pool([512], F32)
        sq = pool([4, 512], BF16)
        assert pool.cur <= 24576
        pool.cur = 40960
        V = pool([NT, 512], BF16)
        mask = pool([3968], BF16)
        roT = pool([4, 512], BF16)
        PT = [pool([512], BF16) for _ in range(3)]
        rstd = pool([512], F32)
        sg4 = pool([4, 512], F32)
        sg = pool([512], F32)
        tq = pool([512], F32)
        arena.set(POOL_B + pool.cur, POOL_SZ - pool.cur)
        for h in range(4):
            wq = arena.load(wcols(wl, C_RQ + h * 256, 256), 8, 256)
            wk = arena.load(wcols(wl, C_RK + h * 256, 256), 8, 256)
            wv = arena.load(wcols(wl, C_RV + h * 512, 512), 8, 512)
            dma("pool", mask[:], None, in_ap=dmask[h, :, :])
            for wi_, (w_, dst) in enumerate(((wq, qT), (wk, kT))):
                for g in range(NG):
                    gs = slice(g * 512, (g + 1) * 512)
                    pb = 2 * ((wi_ * NG + g) % 3)
                    for dc in range(2):
                        for k in range(8):
                            mm(bank(pb + dc)[:], w_[:, k, dc * 128:(dc + 1) * 128], xnT[:, k, gs], k == 0, k == 7)
                    A, B = bank(pb)[:], bank(pb + 1)[:]
                    tt(tmp1[:], A, cosT[:, gs], ALU.mult)
                    tt(tmp2[:], B, sinT[:, gs], ALU.mult)
                    tt(dst[:, 0, gs], tmp1[:], tmp2[:], ALU.subtract)
                    tt(tmp1[:], A, sinT[:, gs], ALU.mult)
                    tt(tmp2[:], B, cosT[:, gs], ALU.mult)
                    tt(dst[:, 1, gs], tmp1[:], tmp2[:], ALU.add)
            for t in range(NT):
                b = bank(4 + t % 2)
                for k in range(8):
                    mm(b[:], xnT[:, k, t * 128:(t + 1) * 128], wv[:, k, :], k == 0, k == 7)
                cpy(V[:, t, :], b[:], eng="act")
            wg = arena.load(wcols(wl, C_RG + h * 512, 512), 8, 512)
            wro = arena.load(dwro[l][h * 512:(h + 1) * 512, :].rearrange("(c p) n -> p c n", p=128), 4, D)
            for g in range(NG):
                gs = slice(g * 512, (g + 1) * 512)
                def QK(j):
                    scb = bank(4 + j % 3)
                    for dc in range(2):
                        mm(scb[:], kT[:, dc, j * 128:(j + 1) * 128], qT[:, dc, gs], dc == 0, dc == 1)
                LA = 2
                for j in range(LA):
                    QK(j)
                for j in range(NT):
                    if j + LA < NT:
                        QK(j + LA)
                    off = 1920 - 128 * j + 512 * g
                    pt = PT[j % 3]
                    tt(pt[:], bank(4 + j % 3)[:], mask[:, off:off + 512], ALU.mult)
                    for e_ in range(4):
                        mm(bank(e_)[:], V[:, j, e_ * 128:(e_ + 1) * 128], pt[:], j == 0, j == NT - 1)
                for e_ in range(4):
                    gb = bank(4 + e_ % 2)
                    for k in range(8):
                        mm(gb[:], wg[:, k, e_ * 128:(e_ + 1) * 128], xnT[:, k, gs], k == 0, k == 7)
                    act(sg4[:, e_, :], gb[:], AF.Silu)
                for e_ in range(4):
                    act(sq[:, e_, :], bank(e_)[:], AF.Square)
                for e_ in range(4):
                    mm(bank(7)[:], onesb[:], sq[:, e_, :], e_ == 0, e_ == 3)
                act(rstd[:], bank(7)[:], AF.Ln, scale=1.0 / 512, bias=epsb[:, 0:1])
                act(rstd[:], rstd[:], AF.Exp, scale=-0.5)
                for e_ in range(4):
                    tt(tq[:], bank(e_)[:], rstd[:], ALU.mult)
                    tt(roT[:, e_, :], tq[:], sg4[:, e_, :], ALU.mult)
                for c in range(8):
                    yb_ = bank(4 + c % 2)
                    for e_ in range(4):
                        mm(yb_[:], wro[:, e_, c * 128:(c + 1) * 128], roT[:, e_, :], e_ == 0, e_ == 3)
                    if h == 0:
                        cpy(merged[:, c, gs], yb_[:], eng="act")
                    else:
                        tt(merged[:, c, gs], merged[:, c, gs], yb_[:], ALU.add)
        for c in range(8):
            wga = arena.load(wcols(wl, C_GA + c * 128, 128), 8, 128)
            for g in range(NG):
                gs = slice(g * 512, (g + 1) * 512)
                gb = bank(4 + g % 2)
                for k in range(8):
                    mm(gb[:], wga[:, k, :], xnT[:, k, gs], k == 0, k == 7)
                act(sg[:], gb[:], AF.Sigmoid)
                tt(merged[:, c, gs], merged[:, c, gs], sg[:], ALU.mult)
        if debug_stage == "ret" and l == 0:
            dump_merged(); break

        pool.reset()
        e0 = pool([2, 8], F32); e1 = pool([2, 8], F32)
        lbl = View("sb", prm.ap[:, P_LBL:P_LBL + 32].rearrange("p (d l h) -> p d l h", d=2, l=2), [2, 2, 8], 4, prm.base + P_LBL * 4)
        if l == 0:
            memset(hp[:, 0, :, :], 0.0)
        else:
            act(e0[:], lbl[:, :, 0, :], AF.Exp)
            act(e1[:], lbl[:, :, 1, :], AF.Exp)
            tt(e0[:], e0[:], e1[:], ALU.add)
            recip(e0[:], e0[:])
            tt(hp[:, 0, :, :], e1[:], e0[:], ALU.mult)
        ts(hp[:, 1, :, :], hp[:, 0, :, :], -1.0, 1.0, ALU.mult, ALU.add)
        ts(hp[:, 2, :, :], hp[:, 1, :, :], -1.0, None, ALU.mult)
        hoT = pool([8, S], BF16)
        qk = {("q", 0): pool([S], BF16), ("k", 0): pool([S], BF16), ("q", 1): pool([S], BF16), ("k", 1): pool([S], BF16)}
        Vh = pool([NT, 128], BF16)
        Sbf = [pool([NT, 128], BF16) for _ in range(2)]
        gateT = pool([S], BF16)
        St = [pool([128], F32) for _ in range(2)]
        Esc = [pool([3, NT], F32) for _ in range(2)]
        AT = [pool([128], BF16) for _ in range(4)]
        utmp = [pool([128], F32) for _ in range(2)]
        rmk = pool([512], F32)
        memset(rmk[:], 1.0)
        for ck in range(4):
            memset(rmk[:, ck * 128:ck * 128 + 1], 0.0)
        for a_ in AT:
            memset(a_[:], 0.0)
        Tbase = pool.cur
        q32 = pool([512], F32); sgq = pool([512], F32)
        SG = [pool([512], F32) for _ in range(2)]
        Kt = [pool([512], F32) for _ in range(2)]
        Bt = [pool([512], F32) for _ in range(2)]
        Dt = [pool([512], F32) for _ in range(2)]
        tdd = [pool([4], F32) for _ in range(2)]
        Tend = pool.cur
        pool.cur = Tbase
        kTt = [pool([NT, 128], BF16) for _ in range(2)]
        rstd = pool([512], F32); tq = pool([512], F32); sqb = pool([512], BF16)
        pool.cur = max(Tend, pool.cur)
        arena.set(POOL_B + pool.cur, POOL_SZ - pool.cur)
        hscale = 128.0 ** -0.5

        def v4(t_):
            return View("sb", t_.ap.rearrange("p (a b) -> p a b", a=4), [4, 128], 4, t_.base)

        for h in range(8):
            wq = arena.load(wcols(wl, C_HQ + h * 128, 128), 8, 128)
            wz = [arena.load(wcols(wl, C_HZF + h * 128, 128), 8, 128), arena.load(wcols(wl, C_HZB + h * 128, 128), 8, 128)]
            wgh = arena.load(wcols(wl, C_HG + h * 128, 128), 8, 128)
            wi = arena.load(wcols(wl, C_HI + h * 128, 128), 8, 128)
            for g in range(NG):
                gs = slice(g * 512, (g + 1) * 512)
                bb = (g % 2) * 4
                for (w_, b_) in ((wq, bb), (wz[0], bb + 1), (wz[1], bb + 2), (wgh, bb + 3)):
                    for k in range(8):
                        mm(bank(b_)[:], w_[:, k, :], xnT[:, k, gs], k == 0, k == 7)
                act(sgq[:], bank(bb)[:], AF.Sigmoid)
                act(SG[0][:], bank(bb + 1)[:], AF.Sigmoid)
                act(SG[1][:], bank(bb + 2)[:], AF.Sigmoid)
                act(Dt[0][:], bank(bb + 3)[:], AF.Sigmoid)
                tt(q32[:], bank(bb)[:], sgq[:], ALU.mult)
                tt(gateT[:, gs], bank(bb + 3)[:], Dt[0][:], ALU.mult)
                for d_ in range(2):
                    ts(Kt[d_][:], SG[d_][:], hp[:, 2, d_, h:h + 1], hp[:, 1, d_, h:h + 1], ALU.mult, ALU.add)
                for d_ in range(2):
                    act(SG[d_][:], SG[d_][:], AF.Ln, scale=hp[:, 1, d_, h:h + 1], bias=hp[:, 0, d_, h:h + 1])
                for d_ in range(2):
                    scan(Bt[d_][:], rmk[:], SG[d_][:])
                c4 = slice(g * 4, (g + 1) * 4)
                for d_ in range(2):
                    act(Esc[d_][:, 0, c4], v4(Bt[d_])[:, :, 127], AF.Exp)
                    act(Esc[d_][:, 1, c4], v4(Bt[d_])[:, :, 63], AF.Exp)
                for d_ in range(2):
                    tt(tdd[d_][:], v4(Bt[d_])[:, :, 127], v4(Bt[d_])[:, :, 63], ALU.subtract)
                for d_ in range(2):
                    act(Esc[d_][:, 2, c4], tdd[d_][:], AF.Exp)
                b63 = v4(Bt[0])[:, :, 63:64]
                P.add("dve", lambda e, o=v4(Dt[0])[:], a=v4(Bt[0])[:], b=b63: e.tensor_tensor(out=o.ap, in0=a.ap, in1=b.ap.broadcast_to([128, 4, 128]), op=ALU.subtract),
                      reads=[v4(Bt[0])[:]], writes=[v4(Dt[0])[:]])
                tt(Dt[1][:], SG[1][:], Bt[1][:], ALU.subtract)
                c64 = v4(Dt[1])[:, :, 64:65]
                P.add("dve", lambda e, o=v4(Bt[1])[:], a=v4(Dt[1])[:], b=c64: e.tensor_tensor(out=o.ap, in0=a.ap, in1=b.ap.broadcast_to([128, 4, 128]), op=ALU.subtract),
                      reads=[v4(Dt[1])[:]], writes=[v4(Bt[1])[:]])
                Dd = [Dt[0], Bt[1]]
                for d_ in range(2):
                    act(SG[d_][:], Dd[d_][:], AF.Exp)
                for d_ in range(2):
                    stt(qk[("q", d_)][:, gs], q32[:], hscale, SG[d_][:], ALU.mult, ALU.mult)
                for d_ in range(2):
                    act(SG[d_][:], Dd[d_][:], AF.Exp, scale=-1.0)
                for d_ in range(2):
                    tt(qk[("k", d_)][:, gs], Kt[d_][:], SG[d_][:], ALU.mult)
            for g in range(NG):
                bv = bank3(g, 4, 128)
                for t4 in range(4):
                    t = g * 4 + t4
                    for k in range(8):
                        mm(bv[:, t4, :], xnT[:, k, t * 128:(t + 1) * 128], wi[:, k, :], k == 0, k == 7)
                cpy(Vh[:, g * 4:(g + 1) * 4, :], bv[:], eng="dve")
            for d_ in range(2):
                kk = qk[("k", d_)]
                for half in range(2):
                    for i in range(8):
                        c = half * 8 + i
                        tr(psb[:, i, :], kk[:, c * 128:(c + 1) * 128], ident[:])
                    cpy(kTt[d_][:, half * 8:(half + 1) * 8, :], psb[:], eng="act")
                memset(St[d_][:], 0.0)
            for n_ in range(NT):
                for d_ in range(2):
                    c = n_ if d_ == 0 else NT - 1 - n_
                    ie_r, ie_tr = (1, 2) if d_ == 0 else (2, 1)
                    ub = bank(2 + 2 * (n_ % 2) + d_, 128)
                    mm(ub[:], kTt[d_][:, c, :], Vh[:, c, :], True, True)
                    act(Sbf[d_][:, c, :], St[d_][:], AF.Identity, scale=Esc[d_][:, ie_r, c:c + 1])
                    act(utmp[d_][:], ub[:], AF.Identity, scale=Esc[d_][:, ie_tr, c:c + 1])
                    stt(St[d_][:], St[d_][:], Esc[d_][:, 0, c:c + 1], utmp[d_][:], ALU.mult, ALU.add)
            kf, qf, kb, qb = qk[("k", 0)], qk[("q", 0)], qk[("k", 1)], qk[("q", 1)]

            def Amm(c):
                c0 = c * 128
                ab = bank3(6 + c % 2, 4, 128)
                mm(ab[0:64, 0, :], kf[:, c0:c0 + 64], qf[:, c0:c0 + 128], True, True)
                mm(ab[64:128, 0, 64:128], kf[:, c0 + 64:c0 + 128], qf[:, c0 + 64:c0 + 128], True, True)
                mm(ab[0:64, 1, 0:64], kb[:, c0:c0 + 64], qb[:, c0:c0 + 64], True, True)
                mm(ab[64:128, 1, :], kb[:, c0 + 64:c0 + 128], qb[:, c0:c0 + 128], True, True)
                for d_ in range(2):
                    cpred(AT[(c % 2) * 2 + d_][:], (maskf if d_ == 0 else maskb)[:], ab[:, d_, :])

            Amm(0)
            for c in range(NT):
                g = c // 4
                c4_ = c % 4
                gs = slice(g * 512, (g + 1) * 512)
                cs = slice(c * 128, (c + 1) * 128)
                if c + 1 < NT:
                    Amm(c + 1)
                ob = bank3(g % 2, 4, 128)
                mm(ob[:, c4_, :], Vh[:, c, :], AT[(c % 2) * 2 + 0][:], True, False)
                mm(ob[:, c4_, :], Vh[:, c, :], AT[(c % 2) * 2 + 1][:], False, False)
                mm(ob[:, c4_, :], Sbf[0][:, c, :], qf[:, cs], False, False)
                mm(ob[:, c4_, :], Sbf[1][:, c, :], qb[:, cs], False, True)
                if c4_ == 3:
                    obf = bank(g % 2)
                    act(sqb[:], obf[:], AF.Square)
                    mm(bank(2 + g % 2)[:], onesb[:], sqb[:], True, True)
                    act(rstd[:], bank(2 + g % 2)[:], AF.Ln, scale=1.0 / 128, bias=epsb[:, 0:1])
                    act(rstd[:], rstd[:], AF.Exp, scale=-0.5)
                    tt(tq[:], obf[:], rstd[:], ALU.mult)
                    stt(hoT[:, h, gs], tq[:], prm[:, P_HNW + l:P_HNW + l + 1], gateT[:, gs], ALU.mult, ALU.mult)
        sgt = q32
        for c in range(8):
            who = arena.load(dwho[l].rearrange("(h p) n -> p h n", p=128)[:, :, c * 128:(c + 1) * 128], 8, 128)
            wga = arena.load(wcols(wl, C_GA + 1024 + c * 128, 128), 8, 128)
            for g in range(NG):
                gs = slice(g * 512, (g + 1) * 512)
                yb_ = bank(g % 2)
                for hh in range(8):
                    mm(yb_[:], who[:, hh, :], hoT[:, hh, gs], hh == 0, hh == 7)
                gb = bank(2 + g % 2)
                for k in range(8):
                    mm(gb[:], wga[:, k, :], xnT[:, k, gs], k == 0, k == 7)
                act(sgt[:], gb[:], AF.Sigmoid)
                tt(tq[:], yb_[:], sgt[:], ALU.mult)
                tt(merged[:, c, gs], merged[:, c, gs], tq[:], ALU.add)
        if debug_stage == "hgrn" and l == 0:
            dump_merged(); break

        pool.reset()
        Ap = pool([NT, D], BF16)
        Bp = pool([NT, D], BF16)
        Zc = pool([8, 256], BF16)
        Zs = pool([8, 256], BF16)
        fuT = pool([2, 256], BF16)
        cgt = pool([2, 256], BF16)
        sgt2 = pool([2, 256], BF16)
        wfn = pool([8, D], BF16)
        arena.set(POOL_B + pool.cur, POOL_SZ - pool.cur)
        dma("pool", cgt[:], None, in_ap=dcg[0].rearrange("(c p) n -> p c n", p=128))
        dma("pool", sgt2[:], None, in_ap=dcg[1].rearrange("(c p) n -> p c n", p=128))
        dma("pool", wfn[:], None, in_ap=dwfn[l].rearrange("(c p) n -> p c n", p=128))
        Zc2 = [Zc, pool([8, 256], BF16)]
        Zs2 = [Zs, pool([8, 256], BF16)]
        fu2 = [fuT, pool([2, 256], BF16)]
        arena.set(POOL_B + pool.cur, POOL_SZ - pool.cur)
        units = [(tg, G) for tg in range(8) for G in range(4)]

        def FU(i):
            tg, G = units[i]
            tgs = slice(tg * 256, (tg + 1) * 256)
            wfu = arena.load(wcols(wl, C_FU + G * 256, 256), 8, 256)
            for cc in range(2):
                b_ = bank((i % 2) * 2 + cc, 256)
                for k in range(8):
                    mm(b_[:], wfu[:, k, cc * 128:(cc + 1) * 128], xnT[:, k, tgs], k == 0, k == 7)
                cpy(fu2[i % 2][:, cc, :], b_[:], eng="act")

        def ZZ(i):
            tg, G = units[i]
            for (tab, Z, bb) in ((cgt, Zc2[tg % 2], 4), (sgt2, Zs2[tg % 2], 6)):
                for c2 in range(2):
                    b_ = bank(bb + c2, 256)
                    for cc in range(2):
                        mm(b_[:], tab[:, cc, c2 * 128:(c2 + 1) * 128], fu2[i % 2][:, cc, :], cc == 0, cc == 1)
                    cpy(Z[:, G * 2 + c2, :], b_[:])

        def AB(tg):
            n_ = 0
            for t2 in range(2):
                t = tg * 2 + t2
                for (Z, dstp) in ((Zc2[tg % 2], Ap), (Zs2[tg % 2], Bp)):
                    for half in range(2):
                        b_ = bank(4 + n_ % 4)
                        n_ += 1
                        for c_ in range(8):
                            mm(b_[:], Z[:, c_, t2 * 128:(t2 + 1) * 128], wfn[:, c_, half * 512:(half + 1) * 512], c_ == 0, c_ == 7)
                        cpy(dstp[:, t, half * 512:(half + 1) * 512], b_[:], eng="act" if half == 0 else "dve")

        FU(0)
        for i in range(len(units)):
            if i + 1 < len(units):
                FU(i + 1)
            ZZ(i)
            if units[i][1] == 3:
                AB(units[i][0])
        pool.cur = 2 * NT * D * 2
        yst = pool([8, 512], F32)
        sgt = pool([512], F32)
        arena.set(POOL_B + pool.cur, POOL_SZ - pool.cur)
        for kg in range(NG):
            ks = slice(kg * 512, (kg + 1) * 512)
            for n in range(NT):
                ct = arena.load(ddft[0, n * 128:(n + 1) * 128, ks].rearrange("p (c n) -> p c n", c=1), 1, 512)
                st_ = arena.load(ddft[1, n * 128:(n + 1) * 128, ks].rearrange("p (c n) -> p c n", c=1), 1, 512)
                for c in range(8):
                    mm(bank(c)[:], Ap[:, n, c * 128:(c + 1) * 128], ct[:, 0, :], n == 0, False)
                    mm(bank(c)[:], Bp[:, n, c * 128:(c + 1) * 128], st_[:, 0, :], False, n == NT - 1)
            for c in range(8):
                cpy(yst[:, c, :], bank(c)[:], eng="act")
            for c in range(8):
                wga = arena.load(wcols(wl, C_GA + 2048 + c * 128, 128), 8, 128)
                gb = bank(c % 2)
                for k in range(8):
                    mm(gb[:], wga[:, k, :], xnT[:, k, ks], k == 0, k == 7)
                act(sgt[:], gb[:], AF.Sigmoid)
                tt(yst[:, c, :], yst[:, c, :], sgt[:], ALU.mult)
                tt(merged[:, c, ks], merged[:, c, ks], yst[:, c, :], ALU.add)
        if debug_stage == "fft" and l == 0:
            dump_merged(); break

        pool.reset()
        nwb = pool([D], F32)
        dma("sp", nwb[:], None, in_ap=dnw[l * 4 + 1: l * 4 + 2, :].partition_broadcast(128))
        xb = [pool([D], F32) for _ in range(2)]
        yb = [pool([D], F32) for _ in range(2)]
        sc = pool([2, 8], F32)
        mb = [pool([8, 128], BF16) for _ in range(2)]
        wout = pool([8, D], BF16)
        dma("pool", wout[:], None, in_ap=dwout[l].rearrange("(c p) n -> p c n", p=128))
        for t in range(NT):
            cpy(mb[t % 2][:], merged[:, :, t * 128:(t + 1) * 128], eng="act")
            hb = [bank((t % 2) * 2 + hh) for hh in range(2)]
            for hh in range(2):
                for c in range(8):
                    mm(hb[hh][:], mb[t % 2][:, c, :], wout[:, c, hh * 512:(hh + 1) * 512], c == 0, c == 7)
            residual_tile(t, [hb[0][:], hb[1][:]], nwb, xb, yb, sc, src_a, vxs)
        if debug_stage == "x1" and l == 0:
            break

        norm_phase(l, 2, vxs)
        ACT_B = MRG_B
        actT = sview(ACT_B, [22, S], BF16)
        fb = ACT_B + 22 * S * 2
        ffp = Bump(fb, BIGB - fb)
        wdn = ffp([22, D], BF16)
        hraw = [ffp([S + 2], F32) for _ in range(2)]
        hc = [ffp([S], F32) for _ in range(2)]
        for i in range(2):
            memset(hraw[i][:, 0:1], 0.0)
            memset(hraw[i][:, S + 1:S + 2], 0.0)
        arena.set(fb + ffp.cur, BIGB - fb - ffp.cur)
        dma("pool", wdn[:], None, in_ap=dwdn[l].rearrange("(f p) n -> p f n", p=128))
        wupl = dwup[l]
        for f in range(22):
            wgu = [arena.load(wcols(wupl, f * 128, 128), 8, 128), arena.load(wcols(wupl, DFF + f * 128, 128), 8, 128)]
            for i in range(2):
                for g in range(NG):
                    gs = slice(g * 512, (g + 1) * 512)
                    b_ = bank(i * 4 + g)
                    for k in range(8):
                        mm(b_[:], wgu[i][:, k, :], xnT[:, k, gs], k == 0, k == 7)
                ch = i * 22 + f
                cw = lambda j: prm[:, P_CW + (l * 3 + j) * 44 + ch: P_CW + (l * 3 + j) * 44 + ch + 1]
                cb_ = prm[:, P_CB + l * 44 + ch: P_CB + l * 44 + ch + 1]
                for g in range(NG):
                    gs = slice(g * 512, (g + 1) * 512)
                    b_ = bank(i * 4 + g)
                    cpy(hraw[i][:, 1 + g * 512: 1 + (g + 1) * 512], b_[:], eng="act")
                    act(hc[i][:, gs], b_[:], AF.Identity, scale=cw(1), bias=cb_)
                stt(hc[i][:], hraw[i][:, 0:S], cw(0), hc[i][:], ALU.mult, ALU.add)
                stt(hc[i][:], hraw[i][:, 2:S + 2], cw(2), hc[i][:], ALU.mult, ALU.add)
            act(hc[0][:], hc[0][:], AF.Gelu_apprx_tanh)
            tt(actT[:, f, :], hc[0][:], hc[1][:], ALU.mult)
        rp = Bump(fb + 22 * D * 2, BIGB - fb - 22 * D * 2)
        nwb = rp([D], F32)
        dma("sp", nwb[:], None, in_ap=dnw[l * 4 + 3: l * 4 + 4, :].partition_broadcast(128))
        xb = [rp([D], F32) for _ in range(2)]
        yb = [rp([D], F32) for _ in range(2)]
        sc = rp([2, 8], F32)
        last = (l == nlayers - 1)
        dst = vout if last else vxs
        for t in range(NT):
            hb = [bank((t % 2) * 2 + hh) for hh in range(2)]
            for hh in range(2):
                for f in range(22):
                    mm(hb[hh][:], actT[:, f, t * 128:(t + 1) * 128], wdn[:, f, hh * 512:(hh + 1) * 512], f == 0, f == 21)
            iid = residual_tile(t, [hb[0][:], hb[1][:]], nwb, xb, yb, sc, vxs, dst)
            if last:
                out_dmas.append(iid)
    if out_dmas:
        P.add("sp", lambda e: e.nop(), extra_deps=out_dmas)

    P.finalize()
    sems_eng = {e: [es.enter_context(nc.semaphore("s_%s_%d" % (e, i))) for i in range(P.n_ms[e] // EPOCH + 1)] for e in ENGS}
    sems_dma = {k: es.enter_context(nc.semaphore("d_%s" % k)) for k in P.dma_count}
    block = es.enter_context(nc.Block())
    P.emit(sems_eng, sems_dma, block)
    es.close()
    return nc, P


def _constants():
    cst = np.zeros((128, 512), np.float32)
    cst[:, 0:128] = np.eye(128, dtype=np.float32)
    cst[:, 128:256] = 1.0
    s_ = np.arange(128)[:, None]; t_ = np.arange(128)[None, :]
    cst[:, 256:384] = (s_ <= t_).astype(np.float32)
    cst[:, 384:512] = (s_ >= t_).astype(np.float32)
    gam = 1.0 - 2.0 ** (-5.0 - np.arange(4, dtype=np.float64))
    m_ = np.arange(128)[:, None]; c_ = np.arange(3968)[None, :]
    rmask = np.stack([(256.0 ** -0.5) * g ** np.abs(c_ - 1920 - m_) for g in gam]).astype(np.float32)
    i_ = np.arange(256, dtype=np.float64)
    ang = 2 * np.pi * np.outer(i_, i_) / 256.0
    cg = np.stack([np.cos(ang), np.sin(ang)]).astype(np.float32) / 16.0
    n_ = np.arange(S, dtype=np.int64)
    nk = (np.outer(n_, n_) % S).astype(np.float64)
    ang = 2 * np.pi * nk / S
    dft = np.stack([np.cos(ang), -np.sin(ang)]).astype(np.float32) / np.float32(np.sqrt(S))
    return cst, rmask, cg, dft


_CACHE = {}


def kernel(x, positions, norm_w, w_in, hgrn_lb_logits, hgrn_norm_w, w_ret_o, w_hgrn_o, w_fnet, w_out,
           w_up, conv_w, conv_b, w_down):
    if "nc" not in _CACHE:
        _CACHE["nc"] = build_program()[0]
        _CACHE["const"] = _constants()
    nc = _CACHE["nc"]
    cst, rmask, cg, dft = _CACHE["const"]
    f32 = np.float32
    prm = np.zeros((128, NPRM), f32)
    prm[:, P_LBL:P_LBL + 32] = np.asarray(hgrn_lb_logits, f32).reshape(2, DEPTH, 8, 128).transpose(3, 0, 1, 2).reshape(128, 32)
    prm[:, P_HNW:P_HNW + 2] = np.asarray(hgrn_norm_w, f32).T
    prm[:, P_CW:P_CW + 264] = np.asarray(conv_w, f32).reshape(DEPTH, 3, 44, 128).transpose(3, 0, 1, 2).reshape(128, 264)
    prm[:, P_CB:P_CB + 88] = np.asarray(conv_b, f32).reshape(DEPTH, 44, 128).transpose(2, 0, 1).reshape(128, 88)
    prm[:, P_IF] = (10000.0 ** (-np.arange(128, dtype=np.float32) / np.float32(128))).astype(f32)
    shared = dict(norm_w=np.ascontiguousarray(np.asarray(norm_w, f32).reshape(DEPTH * 4, D)),
                  w_in=np.asarray(w_in, f32), w_ret_o=np.asarray(w_ret_o, f32), w_hgrn_o=np.asarray(w_hgrn_o, f32),
                  w_fnet=np.asarray(w_fnet, f32), w_out=np.asarray(w_out, f32), w_up=np.asarray(w_up, f32),
                  w_down=np.asarray(w_down, f32), prm=prm, rmask=rmask, cg=cg, dft=dft, cst=cst)
    xs = np.asarray(x, f32)
    pos = np.asarray(positions, np.int32)
    in_maps = [dict(shared, x=np.ascontiguousarray(xs[b]), pos=np.ascontiguousarray(pos[b:b + 1])) for b in range(8)]
    res = run_bass_kernel_spmd(nc, in_maps, core_ids=list(range(8)))
    return np.stack([np.asarray(r["out"], f32) for r in res.results], axis=0)
```

```python
import numpy as np
import concourse.bass as bass
import concourse.mybir as mybir

F32 = mybir.dt.float32
BF16 = mybir.dt.bfloat16
I32 = mybir.dt.int32
U8 = mybir.dt.uint8
AF = mybir.ActivationFunctionType
ALU = mybir.AluOpType

ENGS = ("pe", "act", "dve", "pool", "sp")
BUCKET = 2048
EPOCH = 12000


class _Rec:
    __slots__ = ("lo", "hi", "iid", "eng", "alive")

    def __init__(self, lo, hi, iid, eng):
        self.lo, self.hi, self.iid, self.eng, self.alive = lo, hi, iid, eng, True


class _Space:
    def __init__(self):
        self.w = {}
        self.r = {}

    @staticmethod
    def _bk(lo, hi):
        return range(lo // BUCKET, (hi - 1) // BUCKET + 1)

    def query(self, table, lo, hi, out):
        for b in self._bk(lo, hi):
            lst = table.get(b)
            if not lst:
                continue
            for rec in lst:
                if rec.alive and rec.lo < hi and lo < rec.hi:
                    out.add(rec.iid)

    def kill_covered(self, table, lo, hi):
        for b in self._bk(lo, hi):
            lst = table.get(b)
            if not lst:
                continue
            keep = []
            for rec in lst:
                if not rec.alive:
                    continue
                if lo <= rec.lo and rec.hi <= hi:
                    rec.alive = False
                else:
                    keep.append(rec)
            table[b] = keep

    def insert(self, table, rec):
        for b in self._bk(rec.lo, rec.hi):
            table.setdefault(b, []).append(rec)

    def add_reader(self, lo, hi, iid, eng):
        b0 = lo // BUCKET
        lst = self.r.get(b0)
        if lst and eng != "dma":
            for rec in lst:
                if rec.alive and rec.lo == lo and rec.hi == hi and rec.eng == eng:
                    rec.iid = iid
                    return
        self.insert(self.r, _Rec(lo, hi, iid, eng))


class Acc:
    __slots__ = ("ap", "regs")

    def __init__(self, ap, regs):
        self.ap, self.regs = ap, regs


def _regions(shape, idx):
    if len(shape) == 1:
        return [(idx[0][0], idx[0][1])]
    inner = int(np.prod(shape[1:]))
    sub = _regions(shape[1:], idx[1:])
    if len(sub) == 1 and sub[0] == (0, inner):
        return [(idx[0][0] * inner, idx[0][1] * inner)]
    out = []
    for i in range(idx[0][0], idx[0][1]):
        for (l, h) in sub:
            out.append((i * inner + l, i * inner + h))
    return out


class View:
    def __init__(self, space, ap, free_shape, esize, base_bytes=0, track=True):
        self.space, self.ap, self.shape, self.esize, self.base, self.track = space, ap, tuple(free_shape), esize, base_bytes, track

    def __getitem__(self, key):
        if not isinstance(key, tuple):
            key = (key,)
        key = key + (slice(None),) * (1 + len(self.shape) - len(key))
        ap = self.ap[key]
        if not self.track:
            return Acc(ap, [])
        idx = []
        for k, n in zip(key[1:], self.shape):
            if isinstance(k, slice):
                assert k.step in (None, 1)
                idx.append((k.start or 0, n if k.stop is None else k.stop))
            else:
                idx.append((k, k + 1))
        regs = _regions(self.shape, idx)
        if len(regs) > 48:
            regs = [(regs[0][0], regs[-1][1])]
        out = [(self.space, self.base + l * self.esize, self.base + h * self.esize) for (l, h) in regs]
        if self.space == "ps":
            lo = min(r[1] for r in out) // 2048 * 2048
            hi = (max(r[2] for r in out) + 2047) // 2048 * 2048
            out = [("ps", lo, hi)]
        return Acc(ap, out)


class Ins:
    __slots__ = ("eng", "fn", "deps", "dma", "semkey", "semval", "ms", "needs_inc", "waits", "name")


class Prog:
    def __init__(self, nc):
        self.nc = nc
        self.ins = []
        self.streams = {e: [] for e in ENGS}
        self.spaces = {}
        self.dma_count = {}
        self.cur_group = None

    def _sp(self, name):
        s = self.spaces.get(name)
        if s is None:
            s = self.spaces[name] = _Space()
        return s

    def add(self, eng, fn, reads=(), writes=(), dma=False, semkey=None, extra_deps=(), name=""):
        iid = len(self.ins)
        deps = set(extra_deps)
        rregs = [r for a in reads for r in a.regs]
        wregs = [r for a in writes for r in a.regs]
        for (sp, lo, hi) in rregs:
            s = self._sp(sp)
            s.query(s.w, lo, hi, deps)
            if sp == "ps":
                tmp = set()
                s.query(s.r, lo, hi, tmp)
                for d in tmp:
                    if self.ins[d].eng != eng:
                        deps.add(d)
        for (sp, lo, hi) in wregs:
            s = self._sp(sp)
            s.query(s.w, lo, hi, deps)
            s.query(s.r, lo, hi, deps)
        for (sp, lo, hi) in wregs:
            s = self._sp(sp)
            s.kill_covered(s.w, lo, hi)
            s.kill_covered(s.r, lo, hi)
            s.insert(s.w, _Rec(lo, hi, iid, "dma" if dma else eng))
        for (sp, lo, hi) in rregs:
            s = self._sp(sp)
            s.add_reader(lo, hi, iid, "dma" if dma else eng)
        deps.discard(iid)
        I = Ins()
        I.eng, I.fn, I.deps, I.dma, I.semkey, I.name = eng, fn, deps, dma, semkey, name
        I.semval = None
        I.needs_inc = dma
        I.waits = []
        I.ms = None
        if dma:
            assert semkey is not None
            self.dma_count[semkey] = self.dma_count.get(semkey, 0) + 16
            I.semval = self.dma_count[semkey]
            if self.cur_group is not None:
                self.cur_group.append(I)
        self.ins.append(I)
        self.streams[eng].append(iid)
        return iid

    def begin_group(self):
        self.cur_group = []

    def end_group(self):
        g = self.cur_group
        self.cur_group = None
        if g:
            tot = max(i.semval for i in g)
            for i in g:
                i.semval = tot
        return g

    def finalize(self):
        ins = self.ins
        for I in ins:
            for d in I.deps:
                D = ins[d]
                if not D.dma:
                    if D.eng == "pe" and I.eng == "pe" and not I.dma:
                        continue
                    D.needs_inc = True
        for e in ENGS:
            g = 0
            for iid in self.streams[e]:
                I = ins[iid]
                if I.needs_inc and not I.dma:
                    I.ms = g
                    g += 1
        self.n_ms = {e: sum(1 for iid in self.streams[e] if ins[iid].ms is not None) for e in ENGS}
        waited = {e: {p: -1 for p in ENGS} for e in ENGS}
        waited_dma = {e: {} for e in ENGS}
        last_ms = {}
        for e in ENGS:
            cur = -1
            for iid in self.streams[e]:
                if ins[iid].ms is not None:
                    cur = ins[iid].ms
                last_ms[iid] = cur
        for I in ins:
            need = {}
            need_dma = {}
            for d in I.deps:
                D = ins[d]
                if D.dma:
                    need_dma[D.semkey] = max(need_dma.get(D.semkey, 0), D.semval)
                else:
                    if D.eng == "pe" and I.eng == "pe" and not I.dma:
                        continue
                    need[D.eng] = max(need.get(D.eng, -1), D.ms)
            for p, m in need.items():
                if waited[I.eng][p] >= m:
                    continue
                waited[I.eng][p] = m
                I.waits.append(("c", p, m))
            for k, v in need_dma.items():
                if waited_dma[I.eng].get(k, 0) >= v:
                    continue
                waited_dma[I.eng][k] = v
                I.waits.append(("d", k, v))

    def emit(self, sems_eng, sems_dma, block):
        nc = self.nc
        ins = self.ins

        def run(engname, eng):
            for iid in self.streams[engname]:
                I = ins[iid]
                for (kind, a, b) in I.waits:
                    if kind == "c":
                        eng.wait_ge(sems_eng[a][b // EPOCH], b % EPOCH + 1)
                    else:
                        eng.wait_ge(sems_dma[a], b)
                r = I.fn(eng)
                if I.dma:
                    r.then_inc(sems_dma[I.semkey], 16)
                elif I.ms is not None:
                    r.then_inc(sems_eng[engname][I.ms // EPOCH], 1)

        @block.tensor
        def _(e):
            run("pe", e)

        @block.scalar
        def _(e):
            run("act", e)

        @block.vector
        def _(e):
            run("dve", e)

        @block.gpsimd
        def _(e):
            run("pool", e)

        @block.sync
        def _(e):
            run("sp", e)

import contextlib, os
HG_STOP = os.environ.get('HG_STOP', '')
HG_S1 = int(os.environ.get('HG_S1', '9'))
from concourse.bass_utils import run_bass_kernel_spmd

D = 1024
S = 2048
DEPTH = 2
DFF = 2816
DIN = 15360
NT = 16
NG = 4
EPSV = 1e-6
BIGB = 212800
C_RQ, C_RK, C_RV, C_RG, C_HQ, C_HZF, C_HZB, C_HI, C_HG, C_FU, C_GA = 0, 1024, 2048, 4096, 6144, 7168, 8192, 9216, 10240, 11264, 12288
MAGIC = 12582912.0
TWO_PI = 6.28318
P_LBL = 0
P_HNW = 32
P_CW = 34
P_CB = 298
P_IF = 386
NPRM = 388


def build_program(debug_stage=None, nlayers=DEPTH):
    nc = bass.Bass("TRN2", target_bir_lowering=False)
    dx = nc.dram_tensor("x", [S, D], F32, kind="ExternalInput").ap()
    dpos = nc.dram_tensor("pos", [1, S], I32, kind="ExternalInput").ap()
    dnw = nc.dram_tensor("norm_w", [DEPTH * 4, D], F32, kind="ExternalInput").ap()
    dwin = nc.dram_tensor("w_in", [DEPTH, D, DIN], F32, kind="ExternalInput").ap()
    dwro = nc.dram_tensor("w_ret_o", [DEPTH, 2048, D], F32, kind="ExternalInput").ap()
    dwho = nc.dram_tensor("w_hgrn_o", [DEPTH, D, D], F32, kind="ExternalInput").ap()
    dwfn = nc.dram_tensor("w_fnet", [DEPTH, D, D], F32, kind="ExternalInput").ap()
    dwout = nc.dram_tensor("w_out", [DEPTH, D, D], F32, kind="ExternalInput").ap()
    dwup = nc.dram_tensor("w_up", [DEPTH, D, 2 * DFF], F32, kind="ExternalInput").ap()
    dwdn = nc.dram_tensor("w_down", [DEPTH, DFF, D], F32, kind="ExternalInput").ap()
    dprm = nc.dram_tensor("prm", [128, NPRM], F32, kind="ExternalInput").ap()
    dmask = nc.dram_tensor("rmask", [4, 128, 3968], F32, kind="ExternalInput").ap()
    dcg = nc.dram_tensor("cg", [2, 256, 256], F32, kind="ExternalInput").ap()
    ddft = nc.dram_tensor("dft", [2, S, S], F32, kind="ExternalInput").ap()
    dcst = nc.dram_tensor("cst", [128, 512], F32, kind="ExternalInput").ap()
    dout = nc.dram_tensor("out", [S, D], F32, kind="ExternalOutput").ap()
    dxs = nc.dram_tensor("xs", [S, D], F32, kind="Internal").ap()
    ddbg = None
    if debug_stage is not None:
        ddbg = nc.dram_tensor("dbg", [128, 8, S], F32, kind="ExternalOutput").ap()

    es = contextlib.ExitStack()
    big = es.enter_context(nc.sbuf_tensor("big", [128, BIGB // 4], F32))
    ps = es.enter_context(nc.psum_tensor("ps", [128, 8 * 512], F32))
    P = Prog(nc)

    def sview(offb, shape, dt):
        esz = 2 if dt == BF16 else 4
        n = int(np.prod(shape))
        assert offb % 4 == 0 and (n * esz) % 4 == 0, (offb, shape)
        assert offb + n * esz <= BIGB, ("SBUF overflow", offb, shape)
        ap = big[:, offb // 4: offb // 4 + n * esz // 4]
        if dt != F32:
            ap = ap.bitcast(dt)
        if len(shape) > 1:
            names = " ".join("d%d" % i for i in range(len(shape)))
            ap = ap.rearrange("p (%s) -> p %s" % (names, names), **{"d%d" % i: shape[i] for i in range(len(shape) - 1)})
        return View("sb", ap, shape, esz, offb)

    def bank(b, n=512):
        return View("ps", ps[:, b * 512: b * 512 + n], [n], 4, b * 2048)

    def bank3(b, a, n):
        return View("ps", ps[:, b * 512: b * 512 + a * n].rearrange("p (a n) -> p a n", a=a), [a, n], 4, b * 2048)

    psb = View("ps", ps[:, 7 * 512: 8 * 512].bitcast(BF16).rearrange("p (a b) -> p a b", a=8), [8, 128], 2, 7 * 2048)

    def dview(ap, name, track=True):
        return View(name, ap.rearrange("(t p) d -> p t d", p=128), [NT, D], 4, 0, track=track)

    vx = dview(dx, "d_x", track=False)
    vxs = dview(dxs, "d_xs")
    vout = dview(dout, "d_out")

    def mm(out, lhsT, rhs, start, stop):
        P.add("pe", lambda e: e.matmul(out.ap, lhsT=lhsT.ap, rhs=rhs.ap, start=start, stop=stop), reads=[lhsT, rhs], writes=[out])

    def tr(out, in_, ident):
        P.add("pe", lambda e: e.transpose(out=out.ap, in_=in_.ap, identity=ident.ap), reads=[in_, ident], writes=[out])

    def act(out, in_, func, scale=None, bias=None, accum=None, eng="act"):
        rd = [in_]
        kw = {}
        if scale is not None:
            if isinstance(scale, Acc):
                rd.append(scale); kw["scale"] = scale.ap
            else:
                kw["scale"] = scale
        if bias is not None:
            if isinstance(bias, Acc):
                rd.append(bias); kw["bias"] = bias.ap
            else:
                kw["bias"] = bias
        wr = [out]
        if accum is not None:
            wr.append(accum); kw["accum_out"] = accum.ap
        P.add("act", lambda e: e.activation(out=out.ap, in_=in_.ap, func=func, **kw), reads=rd, writes=wr)

    def tt(out, a, b, op, eng="dve"):
        P.add(eng, lambda e: e.tensor_tensor(out=out.ap, in0=a.ap, in1=b.ap, op=op), reads=[a, b], writes=[out])

    def ts(out, a, s1, s2, op0, op1=None, eng="dve"):
        rd = [a]
        v1 = s1.ap if isinstance(s1, Acc) else s1
        v2 = s2.ap if isinstance(s2, Acc) else s2
        if isinstance(s1, Acc): rd.append(s1)
        if isinstance(s2, Acc): rd.append(s2)
        if op1 is None:
            P.add(eng, lambda e: e.tensor_scalar(out=out.ap, in0=a.ap, scalar1=v1, scalar2=None, op0=op0), reads=rd, writes=[out])
        else:
            P.add(eng, lambda e: e.tensor_scalar(out=out.ap, in0=a.ap, scalar1=v1, scalar2=v2, op0=op0, op1=op1), reads=rd, writes=[out])

    def stt(out, a, s, b, op0, op1):
        rd = [a, b]
        v = s.ap if isinstance(s, Acc) else s
        if isinstance(s, Acc): rd.append(s)
        P.add("dve", lambda e: e.scalar_tensor_tensor(out=out.ap, in0=a.ap, scalar=v, in1=b.ap, op0=op0, op1=op1), reads=rd, writes=[out])

    def cpy(out, in_, eng="dve"):
        if eng == "act":
            act(out, in_, AF.Copy)
        else:
            P.add(eng, lambda e: e.tensor_copy(out=out.ap, in_=in_.ap), reads=[in_], writes=[out])

    def memset(out, v, eng="dve"):
        P.add(eng, lambda e: e.memset(out.ap, v), writes=[out])

    def recip(out, in_):
        P.add("dve", lambda e: e.reciprocal(out=out.ap, in_=in_.ap), reads=[in_], writes=[out])

    def scan(out, ones, data):
        P.add("dve", lambda e: e.tensor_tensor_scan(out=out.ap, data0=ones.ap, data1=data.ap, initial=0.0, op0=ALU.mult, op1=ALU.add), reads=[ones, data], writes=[out])

    def cpred(out, mask, data):
        P.add("dve", lambda e: e.copy_predicated(out=out.ap, mask=mask.ap, data=data.ap), reads=[mask, data, out], writes=[out])

    dma_n = [0]

    def dma(eng, out_acc, in_acc, out_ap=None, in_ap=None, key=None):
        i = dma_n[0]; dma_n[0] += 1
        k = key or ("q_%s_%d" % (eng, i % 12))
        oa = out_ap if out_ap is not None else out_acc.ap
        ia = in_ap if in_ap is not None else in_acc.ap
        rd = [in_acc] if in_acc is not None else []
        wr = [out_acc] if out_acc is not None else []
        prev = dma_last.get(k)
        iid = P.add(eng, lambda e: e.dma_start(out=oa, in_=ia), reads=rd, writes=wr, dma=True, semkey=k,
                    extra_deps=[prev] if prev is not None else [])
        dma_last[k] = iid
        return iid

    dma_last = {}

    CB = 0
    ident = sview(CB + 0, [128], BF16)
    onesb = sview(CB + 256, [128], BF16)
    maskf = sview(CB + 512, [128], I32)
    maskb = sview(CB + 1024, [128], I32)
    onesf = sview(CB + 1536, [128], F32)
    epsb = sview(CB + 2048, [1], F32)
    prm = sview(CB + 2112, [NPRM], F32)
    hp = sview(CB + 3680, [3, 2, 8], F32)
    cst32 = sview(CB + 3904, [512], F32)
    XNT_B = 6144
    xnT = sview(XNT_B, [8, S], BF16)
    MRG_B = XNT_B + 32768
    merged = sview(MRG_B, [8, S], F32)
    POOL_B = MRG_B + 65536
    POOL_SZ = BIGB - POOL_B

    class Bump:
        def __init__(self, base, size):
            self.base, self.size, self.cur = base, size, 0

        def reset(self, base=None, size=None):
            if base is not None:
                self.base, self.size = base, size
            self.cur = 0

        def __call__(self, shape, dt):
            esz = 2 if dt == BF16 else 4
            n = int(np.prod(shape)) * esz
            n = (n + 63) // 64 * 64
            assert self.cur + n <= self.size, ("pool overflow", self.cur, n, self.size)
            v = sview(self.base + self.cur, shape, dt)
            self.cur += n
            return v

    pool = Bump(POOL_B, POOL_SZ)

    class Arena:
        def __init__(self):
            self.base = self.size = self.cur = 0
            self.n = 0
            self.hist = []

        def set(self, base, size):
            self.base, self.size, self.cur = base, size, 0

        def load(self, src_ap, nch, ncols):
            nb = nch * ncols * 2
            nb = (nb + 63) // 64 * 64
            assert nb <= self.size, ("arena too small", nb, self.size)
            if self.cur + nb > self.size:
                self.cur = 0
            v = sview(self.base + self.cur, [nch, ncols], BF16)
            self.cur += nb
            k = "w%d" % (self.n % 8)
            self.n += 1
            prev = dma_last.get(k)
            iid = P.add("pool", lambda e: e.dma_start(out=v[:].ap, in_=src_ap), writes=[v[:]], dma=True, semkey=k,
                        extra_deps=[prev] if prev is not None else [])
            dma_last[k] = iid
            return v

    arena = Arena()

    def wcols(wl, c0, n):
        return wl.rearrange("(c p) n -> p c n", p=128)[:, :, c0:c0 + n]

    dma("sp", cst32[:], None, in_ap=dcst[:, :])
    cpy(ident[:], cst32[:, 0:128])
    cpy(onesb[:], cst32[:, 128:256])
    cpy(maskf[:], cst32[:, 256:384])
    cpy(maskb[:], cst32[:, 384:512])
    memset(onesf[:], 1.0)
    memset(epsb[:], EPSV)
    dma("sp", prm[:], None, in_ap=dprm[:, :])

    def norm_phase(l, ni, src):
        pool.reset()
        nwb = pool([D], F32)
        dma("sp", nwb[:], None, in_ap=dnw[l * 4 + ni: l * 4 + ni + 1, :].partition_broadcast(128))
        NB = 4
        xb = [pool([D], F32) for _ in range(NB)]
        junk = pool([D], F32)
        xnb = [pool([8, 128], BF16) for _ in range(NB)]
        sc = pool([NB, 4], F32)
        def stA(t):
            i = t % NB
            xt = xb[i]
            dma("sp", xt[:], src[:, t, :])
            act(junk[:], xt[:], AF.Square, accum=sc[:, i, 0:1])
            act(sc[:, i, 1:2], sc[:, i, 0:1], AF.Sqrt, scale=1.0 / D, bias=epsb[:, 0:1])
            recip(sc[:, i, 2:3], sc[:, i, 1:2])
            xn_flat = View("sb", xnb[i].ap.rearrange("p a b -> p (a b)"), [D], 2, xnb[i].base)
            stt(xn_flat[:], xt[:], sc[:, i, 2:3], nwb[:], ALU.mult, ALU.mult)

        def stB(t):
            i = t % NB
            for c in range(8):
                tr(psb[:, c, :], xnb[i][:, c, :], ident[:])
            cpy(xnT[:, :, t * 128:(t + 1) * 128], psb[:], eng="act")

        stA(0)
        stA(1)
        for t in range(NT):
            if t + 2 < NT:
                stA(t + 2)
            stB(t)

    def residual_tile(t, halves, nwb, xb, yb, sc, src, dst):
        i = t % len(xb)
        junk = yb[i]
        act(junk[:, 0:512], halves[0], AF.Square, accum=sc[:, i, 0:1])
        act(junk[:, 512:1024], halves[1], AF.Square, accum=sc[:, i, 1:2])
        tt(sc[:, i, 2:3], sc[:, i, 0:1], sc[:, i, 1:2], ALU.add)
        act(sc[:, i, 3:4], sc[:, i, 2:3], AF.Sqrt, scale=1.0 / D, bias=epsb[:, 0:1])
        recip(sc[:, i, 4:5], sc[:, i, 3:4])
        dma("sp", xb[i][:], src[:, t, :])
        for h in range(2):
            stt(yb[i][:, h * 512:(h + 1) * 512], halves[h], sc[:, i, 4:5], nwb[:, h * 512:(h + 1) * 512], ALU.mult, ALU.mult)
        tt(yb[i][:], yb[i][:], xb[i][:], ALU.add)
        return dma("sp", dst[:, t, :], yb[i][:])

    def dump_merged():
        iid = dma("sp", None, merged[:], out_ap=ddbg[:, :, :])
        P.add("sp", lambda e: e.nop(), extra_deps=[iid])

    out_dmas = []

    for l in range(nlayers):
        wl = dwin[l]
        src_a = vx if l == 0 else vxs
        norm_phase(l, 0, src_a)

        pool.reset()
        pi = pool([S], I32)
        pf = pool([S], F32)
        u1 = pool([S], F32)
        cosT = pool([S], F32)
        sinT = pool([S], F32)
        dma("sp", pi[:], None, in_ap=dpos[0:1, :].partition_broadcast(128))
        cpy(pf[:], pi[:])
        ts(u1[:], pf[:], prm[:, P_IF:P_IF + 1], 1.0 / (2 * np.pi), ALU.mult, ALU.mult)
        ts(pf[:], u1[:], MAGIC, MAGIC, ALU.add, ALU.subtract)
        tt(pf[:], u1[:], pf[:], ALU.subtract)
        act(sinT[:], pf[:], AF.Sin, scale=TWO_PI)
        ts(u1[:], u1[:], 0.25, None, ALU.add)
        ts(pf[:], u1[:], MAGIC, MAGIC, ALU.add, ALU.subtract)
        tt(pf[:], u1[:], pf[:], ALU.subtract)
        act(cosT[:], pf[:], AF.Sin, scale=TWO_PI)
        pool.cur = 0
        _ = pool([S], I32); _ = pool([S], F32); _ = pool([S], F32); _ = pool([S], F32); _ = pool([S], F32)
        pool.cur = 0
        qT = pool([2, S], BF16)
        kT = pool([2, S], BF16)
        tmp1 = pool([512], F32); tmp2 = pool([512], F32)
        sq = pool([4, 512], BF16)
        assert pool.cur <= 24576
        pool.cur = 40960
        V = pool([NT, 512], BF16)
        mask = pool([3968], BF16)
        roT = pool([4, 512], BF16)
        PT = [pool([512], BF16) for _ in range(3)]
        rstd = pool([512], F32)
        sg4 = pool([4, 512], F32)
        sg = pool([512], F32)
        tq = pool([512], F32)
        arena.set(POOL_B + pool.cur, POOL_SZ - pool.cur)
        for h in range(4):
            wq = arena.load(wcols(wl, C_RQ + h * 256, 256), 8, 256)
            wk = arena.load(wcols(wl, C_RK + h * 256, 256), 8, 256)
            wv = arena.load(wcols(wl, C_RV + h * 512, 512), 8, 512)
            dma("pool", mask[:], None, in_ap=dmask[h, :, :])
            for wi_, (w_, dst) in enumerate(((wq, qT), (wk, kT))):
                for g in range(NG):
                    gs = slice(g * 512, (g + 1) * 512)
                    pb = 2 * ((wi_ * NG + g) % 3)
                    for dc in range(2):
                        for k in range(8):
                            mm(bank(pb + dc)[:], w_[:, k, dc * 128:(dc + 1) * 128], xnT[:, k, gs], k == 0, k == 7)
                    A, B = bank(pb)[:], bank(pb + 1)[:]
                    tt(tmp1[:], A, cosT[:, gs], ALU.mult)
                    tt(tmp2[:], B, sinT[:, gs], ALU.mult)
                    tt(dst[:, 0, gs], tmp1[:], tmp2[:], ALU.subtract)
                    tt(tmp1[:], A, sinT[:, gs], ALU.mult)
                    tt(tmp2[:], B, cosT[:, gs], ALU.mult)
                    tt(dst[:, 1, gs], tmp1[:], tmp2[:], ALU.add)
            for t in range(NT):
                b = bank(4 + t % 2)
                for k in range(8):
                    mm(b[:], xnT[:, k, t * 128:(t + 1) * 128], wv[:, k, :], k == 0, k == 7)
                cpy(V[:, t, :], b[:], eng="act")
            wg = arena.load(wcols(wl, C_RG + h * 512, 512), 8, 512)
            wro = arena.load(dwro[l][h * 512:(h + 1) * 512, :].rearrange("(c p) n -> p c n", p=128), 4, D)
            for g in range(NG):
                gs = slice(g * 512, (g + 1) * 512)
                def QK(j):
                    scb = bank(4 + j % 3)
                    for dc in range(2):
                        mm(scb[:], kT[:, dc, j * 128:(j + 1) * 128], qT[:, dc, gs], dc == 0, dc == 1)
                LA = 2
                for j in range(LA):
                    QK(j)
                for j in range(NT):
                    if j + LA < NT:
                        QK(j + LA)
                    off = 1920 - 128 * j + 512 * g
                    pt = PT[j % 3]
                    tt(pt[:], bank(4 + j % 3)[:], mask[:, off:off + 512], ALU.mult)
                    for e_ in range(4):
                        mm(bank(e_)[:], V[:, j, e_ * 128:(e_ + 1) * 128], pt[:], j == 0, j == NT - 1)
                for e_ in range(4):
                    gb = bank(4 + e_ % 2)
                    for k in range(8):
                        mm(gb[:], wg[:, k, e_ * 128:(e_ + 1) * 128], xnT[:, k, gs], k == 0, k == 7)
                    act(sg4[:, e_, :], gb[:], AF.Silu)
                for e_ in range(4):
                    act(sq[:, e_, :], bank(e_)[:], AF.Square)
                for e_ in range(4):
                    mm(bank(7)[:], onesb[:], sq[:, e_, :], e_ == 0, e_ == 3)
                act(rstd[:], bank(7)[:], AF.Ln, scale=1.0 / 512, bias=epsb[:, 0:1])
                act(rstd[:], rstd[:], AF.Exp, scale=-0.5)
                for e_ in range(4):
                    tt(tq[:], bank(e_)[:], rstd[:], ALU.mult)
                    tt(roT[:, e_, :], tq[:], sg4[:, e_, :], ALU.mult)
                for c in range(8):
                    yb_ = bank(4 + c % 2)
                    for e_ in range(4):
                        mm(yb_[:], wro[:, e_, c * 128:(c + 1) * 128], roT[:, e_, :], e_ == 0, e_ == 3)
                    if h == 0:
                        cpy(merged[:, c, gs], yb_[:], eng="act")
                    else:
                        tt(merged[:, c, gs], merged[:, c, gs], yb_[:], ALU.add)
        for c in range(8):
            wga = arena.load(wcols(wl, C_GA + c * 128, 128), 8, 128)
            for g in range(NG):
                gs = slice(g * 512, (g + 1) * 512)
                gb = bank(4 + g % 2)
                for k in range(8):
                    mm(gb[:], wga[:, k, :], xnT[:, k, gs], k == 0, k == 7)
                act(sg[:], gb[:], AF.Sigmoid)
                tt(merged[:, c, gs], merged[:, c, gs], sg[:], ALU.mult)
        if debug_stage == "ret" and l == 0:
            dump_merged(); break

        pool.reset()
        e0 = pool([2, 8], F32); e1 = pool([2, 8], F32)
        lbl = View("sb", prm.ap[:, P_LBL:P_LBL + 32].rearrange("p (d l h) -> p d l h", d=2, l=2), [2, 2, 8], 4, prm.base + P_LBL * 4)
        if l == 0:
            memset(hp[:, 0, :, :], 0.0)
        else:
            act(e0[:], lbl[:, :, 0, :], AF.Exp)
            act(e1[:], lbl[:, :, 1, :], AF.Exp)
            tt(e0[:], e0[:], e1[:], ALU.add)
            recip(e0[:], e0[:])
            tt(hp[:, 0, :, :], e1[:], e0[:], ALU.mult)
        ts(hp[:, 1, :, :], hp[:, 0, :, :], -1.0, 1.0, ALU.mult, ALU.add)
        ts(hp[:, 2, :, :], hp[:, 1, :, :], -1.0, None, ALU.mult)
        hoT = pool([8, S], BF16)
        qk = {("q", 0): pool([S], BF16), ("k", 0): pool([S], BF16), ("q", 1): pool([S], BF16), ("k", 1): pool([S], BF16)}
        Vh = pool([NT, 128], BF16)
        Sbf = [pool([NT, 128], BF16) for _ in range(2)]
        gateT = pool([S], BF16)
        St = [pool([128], F32) for _ in range(2)]
        Esc = [pool([3, NT], F32) for _ in range(2)]
        AT = [pool([128], BF16) for _ in range(4)]
        utmp = [pool([128], F32) for _ in range(2)]
        rmk = pool([512], F32)
        memset(rmk[:], 1.0)
        for ck in range(4):
            memset(rmk[:, ck * 128:ck * 128 + 1], 0.0)
        for a_ in AT:
            memset(a_[:], 0.0)
        Tbase = pool.cur
        q32 = pool([512], F32); sgq = pool([512], F32)
        SG = [pool([512], F32) for _ in range(2)]
        Kt = [pool([512], F32) for _ in range(2)]
        Bt = [pool([512], F32) for _ in range(2)]
        Dt = [pool([512], F32) for _ in range(2)]
        tdd = [pool([4], F32) for _ in range(2)]
        Tend = pool.cur
        pool.cur = Tbase
        kTt = [pool([NT, 128], BF16) for _ in range(2)]
        rstd = pool([512], F32); tq = pool([512], F32); sqb = pool([512], BF16)
        pool.cur = max(Tend, pool.cur)
        arena.set(POOL_B + pool.cur, POOL_SZ - pool.cur)
        hscale = 128.0 ** -0.5

        def v4(t_):
            return View("sb", t_.ap.rearrange("p (a b) -> p a b", a=4), [4, 128], 4, t_.base)

        for h in range(8):
            wq = arena.load(wcols(wl, C_HQ + h * 128, 128), 8, 128)
            wz = [arena.load(wcols(wl, C_HZF + h * 128, 128), 8, 128), arena.load(wcols(wl, C_HZB + h * 128, 128), 8, 128)]
            wgh = arena.load(wcols(wl, C_HG + h * 128, 128), 8, 128)
            wi = arena.load(wcols(wl, C_HI + h * 128, 128), 8, 128)
            for g in range(NG):
                gs = slice(g * 512, (g + 1) * 512)
                bb = (g % 2) * 4
                for (w_, b_) in ((wq, bb), (wz[0], bb + 1), (wz[1], bb + 2), (wgh, bb + 3)):
                    for k in range(8):
                        mm(bank(b_)[:], w_[:, k, :], xnT[:, k, gs], k == 0, k == 7)
                act(sgq[:], bank(bb)[:], AF.Sigmoid)
                act(SG[0][:], bank(bb + 1)[:], AF.Sigmoid)
                act(SG[1][:], bank(bb + 2)[:], AF.Sigmoid)
                act(Dt[0][:], bank(bb + 3)[:], AF.Sigmoid)
                tt(q32[:], bank(bb)[:], sgq[:], ALU.mult)
                tt(gateT[:, gs], bank(bb + 3)[:], Dt[0][:], ALU.mult)
                for d_ in range(2):
                    ts(Kt[d_][:], SG[d_][:], hp[:, 2, d_, h:h + 1], hp[:, 1, d_, h:h + 1], ALU.mult, ALU.add)
                for d_ in range(2):
                    act(SG[d_][:], SG[d_][:], AF.Ln, scale=hp[:, 1, d_, h:h + 1], bias=hp[:, 0, d_, h:h + 1])
                for d_ in range(2):
                    scan(Bt[d_][:], rmk[:], SG[d_][:])
                c4 = slice(g * 4, (g + 1) * 4)
                for d_ in range(2):
                    act(Esc[d_][:, 0, c4], v4(Bt[d_])[:, :, 127], AF.Exp)
                    act(Esc[d_][:, 1, c4], v4(Bt[d_])[:, :, 63], AF.Exp)
                for d_ in range(2):
                    tt(tdd[d_][:], v4(Bt[d_])[:, :, 127], v4(Bt[d_])[:, :, 63], ALU.subtract)
                for d_ in range(2):
                    act(Esc[d_][:, 2, c4], tdd[d_][:], AF.Exp)
                b63 = v4(Bt[0])[:, :, 63:64]
                P.add("dve", lambda e, o=v4(Dt[0])[:], a=v4(Bt[0])[:], b=b63: e.tensor_tensor(out=o.ap, in0=a.ap, in1=b.ap.broadcast_to([128, 4, 128]), op=ALU.subtract),
                      reads=[v4(Bt[0])[:]], writes=[v4(Dt[0])[:]])
                tt(Dt[1][:], SG[1][:], Bt[1][:], ALU.subtract)
                c64 = v4(Dt[1])[:, :, 64:65]
                P.add("dve", lambda e, o=v4(Bt[1])[:], a=v4(Dt[1])[:], b=c64: e.tensor_tensor(out=o.ap, in0=a.ap, in1=b.ap.broadcast_to([128, 4, 128]), op=ALU.subtract),
                      reads=[v4(Dt[1])[:]], writes=[v4(Bt[1])[:]])
                Dd = [Dt[0], Bt[1]]
                for d_ in range(2):
                    act(SG[d_][:], Dd[d_][:], AF.Exp)
                for d_ in range(2):
                    stt(qk[("q", d_)][:, gs], q32[:], hscale, SG[d_][:], ALU.mult, ALU.mult)
                for d_ in range(2):
                    act(SG[d_][:], Dd[d_][:], AF.Exp, scale=-1.0)
                for d_ in range(2):
                    tt(qk[("k", d_)][:, gs], Kt[d_][:], SG[d_][:], ALU.mult)
            for g in range(NG):
                bv = bank3(g, 4, 128)
                for t4 in range(4):
                    t = g * 4 + t4
                    for k in range(8):
                        mm(bv[:, t4, :], xnT[:, k, t * 128:(t + 1) * 128], wi[:, k, :], k == 0, k == 7)
                cpy(Vh[:, g * 4:(g + 1) * 4, :], bv[:], eng="dve")
            for d_ in range(2):
                kk = qk[("k", d_)]
                for half in range(2):
                    for i in range(8):
                        c = half * 8 + i
                        tr(psb[:, i, :], kk[:, c * 128:(c + 1) * 128], ident[:])
                    cpy(kTt[d_][:, half * 8:(half + 1) * 8, :], psb[:], eng="act")
                memset(St[d_][:], 0.0)
            for n_ in range(NT):
                for d_ in range(2):
                    c = n_ if d_ == 0 else NT - 1 - n_
                    ie_r, ie_tr = (1, 2) if d_ == 0 else (2, 1)
                    ub = bank(2 + 2 * (n_ % 2) + d_, 128)
                    mm(ub[:], kTt[d_][:, c, :], Vh[:, c, :], True, True)
                    act(Sbf[d_][:, c, :], St[d_][:], AF.Identity, scale=Esc[d_][:, ie_r, c:c + 1])
                    act(utmp[d_][:], ub[:], AF.Identity, scale=Esc[d_][:, ie_tr, c:c + 1])
                    stt(St[d_][:], St[d_][:], Esc[d_][:, 0, c:c + 1], utmp[d_][:], ALU.mult, ALU.add)
            kf, qf, kb, qb = qk[("k", 0)], qk[("q", 0)], qk[("k", 1)], qk[("q", 1)]

            def Amm(c):
                c0 = c * 128
                ab = bank3(6 + c % 2, 4, 128)
                mm(ab[0:64, 0, :], kf[:, c0:c0 + 64], qf[:, c0:c0 + 128], True, True)
                mm(ab[64:128, 0, 64:128], kf[:, c0 + 64:c0 + 128], qf[:, c0 + 64:c0 + 128], True, True)
                mm(ab[0:64, 1, 0:64], kb[:, c0:c0 + 64], qb[:, c0:c0 + 64], True, True)
                mm(ab[64:128, 1, :], kb[:, c0 + 64:c0 + 128], qb[:, c0:c0 + 128], True, True)
                for d_ in range(2):
                    cpred(AT[(c % 2) * 2 + d_][:], (maskf if d_ == 0 else maskb)[:], ab[:, d_, :])

            Amm(0)
            for c in range(NT):
                g = c // 4
                c4_ = c % 4
                gs = slice(g * 512, (g + 1) * 512)
                cs = slice(c * 128, (c + 1) * 128)
                if c + 1 < NT:
                    Amm(c + 1)
                ob = bank3(g % 2, 4, 128)
                mm(ob[:, c4_, :], Vh[:, c, :], AT[(c % 2) * 2 + 0][:], True, False)
                mm(ob[:, c4_, :], Vh[:, c, :], AT[(c % 2) * 2 + 1][:], False, False)
                mm(ob[:, c4_, :], Sbf[0][:, c, :], qf[:, cs], False, False)
                mm(ob[:, c4_, :], Sbf[1][:, c, :], qb[:, cs], False, True)
                if c4_ == 3:
                    obf = bank(g % 2)
                    act(sqb[:], obf[:], AF.Square)
                    mm(bank(2 + g % 2)[:], onesb[:], sqb[:], True, True)
                    act(rstd[:], bank(2 + g % 2)[:], AF.Ln, scale=1.0 / 128, bias=epsb[:, 0:1])
                    act(rstd[:], rstd[:], AF.Exp, scale=-0.5)
                    tt(tq[:], obf[:], rstd[:], ALU.mult)
                    stt(hoT[:, h, gs], tq[:], prm[:, P_HNW + l:P_HNW + l + 1], gateT[:, gs], ALU.mult, ALU.mult)
        sgt = q32
        for c in range(8):
            who = arena.load(dwho[l].rearrange("(h p) n -> p h n", p=128)[:, :, c * 128:(c + 1) * 128], 8, 128)
            wga = arena.load(wcols(wl, C_GA + 1024 + c * 128, 128), 8, 128)
            for g in range(NG):
                gs = slice(g * 512, (g + 1) * 512)
                yb_ = bank(g % 2)
                for hh in range(8):
                    mm(yb_[:], who[:, hh, :], hoT[:, hh, gs], hh == 0, hh == 7)
                gb = bank(2 + g % 2)
                for k in range(8):
                    mm(gb[:], wga[:, k, :], xnT[:, k, gs], k == 0, k == 7)
                act(sgt[:], gb[:], AF.Sigmoid)
                tt(tq[:], yb_[:], sgt[:], ALU.mult)
                tt(merged[:, c, gs], merged[:, c, gs], tq[:], ALU.add)
        if debug_stage == "hgrn" and l == 0:
            dump_merged(); break

        pool.reset()
        Ap = pool([NT, D], BF16)
        Bp = pool([NT, D], BF16)
        Zc = pool([8, 256], BF16)
        Zs = pool([8, 256], BF16)
        fuT = pool([2, 256], BF16)
        cgt = pool([2, 256], BF16)
        sgt2 = pool([2, 256], BF16)
        wfn = pool([8, D], BF16)
        arena.set(POOL_B + pool.cur, POOL_SZ - pool.cur)
        dma("pool", cgt[:], None, in_ap=dcg[0].rearrange("(c p) n -> p c n", p=128))
        dma("pool", sgt2[:], None, in_ap=dcg[1].rearrange("(c p) n -> p c n", p=128))
        dma("pool", wfn[:], None, in_ap=dwfn[l].rearrange("(c p) n -> p c n", p=128))
        Zc2 = [Zc, Zc]
        Zs2 = [Zs, Zs]
        fu2 = [fuT, pool([2, 256], BF16)]
        arena.set(POOL_B + pool.cur, POOL_SZ - pool.cur)
        units = [(tg, G) for tg in range(8) for G in range(4)]

        def FU(i):
            tg, G = units[i]
            tgs = slice(tg * 256, (tg + 1) * 256)
            wfu = arena.load(wcols(wl, C_FU + G * 256, 256), 8, 256)
            for cc in range(2):
                b_ = bank((i % 2) * 2 + cc, 256)
                for k in range(8):
                    mm(b_[:], wfu[:, k, cc * 128:(cc + 1) * 128], xnT[:, k, tgs], k == 0, k == 7)
                cpy(fu2[i % 2][:, cc, :], b_[:], eng="act")

        def ZZ(i):
            tg, G = units[i]
            for (tab, Z, bb) in ((cgt, Zc2[tg % 2], 4), (sgt2, Zs2[tg % 2], 6)):
                for c2 in range(2):
                    b_ = bank(bb + c2, 256)
                    for cc in range(2):
                        mm(b_[:], tab[:, cc, c2 * 128:(c2 + 1) * 128], fu2[i % 2][:, cc, :], cc == 0, cc == 1)
                    cpy(Z[:, G * 2 + c2, :], b_[:])

        def AB(tg):
            n_ = 0
            for t2 in range(2):
                t = tg * 2 + t2
                for (Z, dstp) in ((Zc2[tg % 2], Ap), (Zs2[tg % 2], Bp)):
                    for half in range(2):
                        b_ = bank(4 + n_ % 4)
                        n_ += 1
                        for c_ in range(8):
                            mm(b_[:], Z[:, c_, t2 * 128:(t2 + 1) * 128], wfn[:, c_, half * 512:(half + 1) * 512], c_ == 0, c_ == 7)
                        cpy(dstp[:, t, half * 512:(half + 1) * 512], b_[:], eng="act" if half == 0 else "dve")

        FU(0)
        for i in range(len(units)):
            if i + 1 < len(units):
                FU(i + 1)
            ZZ(i)
            if units[i][1] == 3:
                AB(units[i][0])
        pool.cur = 2 * NT * D * 2
        yst = pool([8, 512], F32)
        sgt = pool([512], F32)
        arena.set(POOL_B + pool.cur, POOL_SZ - pool.cur)
        for kg in range(NG):
            ks = slice(kg * 512, (kg + 1) * 512)
            for n in range(NT):
                ct = arena.load(ddft[0, n * 128:(n + 1) * 128, ks].rearrange("p (c n) -> p c n", c=1), 1, 512)
                st_ = arena.load(ddft[1, n * 128:(n + 1) * 128, ks].rearrange("p (c n) -> p c n", c=1), 1, 512)
                for c in range(8):
                    mm(bank(c)[:], Ap[:, n, c * 128:(c + 1) * 128], ct[:, 0, :], n == 0, False)
                    mm(bank(c)[:], Bp[:, n, c * 128:(c + 1) * 128], st_[:, 0, :], False, n == NT - 1)
            for c in range(8):
                cpy(yst[:, c, :], bank(c)[:], eng="act")
            for c in range(8):
                wga = arena.load(wcols(wl, C_GA + 2048 + c * 128, 128), 8, 128)
                gb = bank(c % 2)
                for k in range(8):
                    mm(gb[:], wga[:, k, :], xnT[:, k, ks], k == 0, k == 7)
                act(sgt[:], gb[:], AF.Sigmoid)
                tt(yst[:, c, :], yst[:, c, :], sgt[:], ALU.mult)
                tt(merged[:, c, ks], merged[:, c, ks], yst[:, c, :], ALU.add)
        if debug_stage == "fft" and l == 0:
            dump_merged(); break

        pool.reset()
        nwb = pool([D], F32)
        dma("sp", nwb[:], None, in_ap=dnw[l * 4 + 1: l * 4 + 2, :].partition_broadcast(128))
        xb = [pool([D], F32) for _ in range(4)]
        yb = [pool([D], F32) for _ in range(4)]
        sc = pool([4, 8], F32)
        mb = [pool([8, 128], BF16) for _ in range(2)]
        wout = pool([8, D], BF16)
        dma("pool", wout[:], None, in_ap=dwout[l].rearrange("(c p) n -> p c n", p=128))
        def WO(t):
            cpy(mb[t % 2][:], merged[:, :, t * 128:(t + 1) * 128], eng="act")
            for hh in range(2):
                for c in range(8):
                    mm(bank((t % 2) * 2 + hh)[:], mb[t % 2][:, c, :], wout[:, c, hh * 512:(hh + 1) * 512], c == 0, c == 7)

        WO(0)
        for t in range(NT):
            if t + 1 < NT:
                WO(t + 1)
            residual_tile(t, [bank((t % 2) * 2)[:], bank((t % 2) * 2 + 1)[:]], nwb, xb, yb, sc, src_a, vxs)
        if debug_stage == "x1" and l == 0:
            break

        norm_phase(l, 2, vxs)
        ACT_B = MRG_B
        actT = sview(ACT_B, [22, S], BF16)
        fb = ACT_B + 22 * S * 2
        ffp = Bump(fb, BIGB - fb)
        wdn = ffp([22, D], BF16)
        hraw = [ffp([S + 2], F32) for _ in range(2)]
        hc = [ffp([S], F32) for _ in range(2)]
        for i in range(2):
            memset(hraw[i][:, 0:1], 0.0)
            memset(hraw[i][:, S + 1:S + 2], 0.0)
        arena.set(fb + ffp.cur, BIGB - fb - ffp.cur)
        dma("pool", wdn[:], None, in_ap=dwdn[l].rearrange("(f p) n -> p f n", p=128))
        wupl = dwup[l]
        for f in range(22):
            wgu = [arena.load(wcols(wupl, f * 128, 128), 8, 128), arena.load(wcols(wupl, DFF + f * 128, 128), 8, 128)]
            for i in range(2):
                for g in range(NG):
                    gs = slice(g * 512, (g + 1) * 512)
                    b_ = bank(i * 4 + g)
                    for k in range(8):
                        mm(b_[:], wgu[i][:, k, :], xnT[:, k, gs], k == 0, k == 7)
                ch = i * 22 + f
                cw = lambda j: prm[:, P_CW + (l * 3 + j) * 44 + ch: P_CW + (l * 3 + j) * 44 + ch + 1]
                cb_ = prm[:, P_CB + l * 44 + ch: P_CB + l * 44 + ch + 1]
                for g in range(NG):
                    gs = slice(g * 512, (g + 1) * 512)
                    b_ = bank(i * 4 + g)
                    cpy(hraw[i][:, 1 + g * 512: 1 + (g + 1) * 512], b_[:], eng="act")
                    act(hc[i][:, gs], b_[:], AF.Identity, scale=cw(1), bias=cb_)
                stt(hc[i][:], hraw[i][:, 0:S], cw(0), hc[i][:], ALU.mult, ALU.add)
                stt(hc[i][:], hraw[i][:, 2:S + 2], cw(2), hc[i][:], ALU.mult, ALU.add)
            act(hc[0][:], hc[0][:], AF.Gelu_apprx_tanh)
            tt(actT[:, f, :], hc[0][:], hc[1][:], ALU.mult)
        rp = Bump(fb + 22 * D * 2, BIGB - fb - 22 * D * 2)
        nwb = rp([D], F32)
        dma("sp", nwb[:], None, in_ap=dnw[l * 4 + 3: l * 4 + 4, :].partition_broadcast(128))
        xb = [rp([D], F32) for _ in range(3)]
        yb = [rp([D], F32) for _ in range(3)]
        sc = rp([3, 8], F32)
        last = (l == nlayers - 1)
        dst = vout if last else vxs
        def DN(t):
            for hh in range(2):
                for f in range(22):
                    mm(bank((t % 2) * 2 + hh)[:], actT[:, f, t * 128:(t + 1) * 128], wdn[:, f, hh * 512:(hh + 1) * 512], f == 0, f == 21)

        DN(0)
        for t in range(NT):
            if t + 1 < NT:
                DN(t + 1)
            iid = residual_tile(t, [bank((t % 2) * 2)[:], bank((t % 2) * 2 + 1)[:]], nwb, xb, yb, sc, vxs, dst)
            if last:
                out_dmas.append(iid)
    if out_dmas:
        P.add("sp", lambda e: e.nop(), extra_deps=out_dmas)

    P.finalize()
    sems_eng = {e: [es.enter_context(nc.semaphore("s_%s_%d" % (e, i))) for i in range(P.n_ms[e] // EPOCH + 1)] for e in ENGS}
    sems_dma = {k: es.enter_context(nc.semaphore("d_%s" % k)) for k in P.dma_count}
    block = es.enter_context(nc.Block())
    P.emit(sems_eng, sems_dma, block)
    es.close()
    return nc, P


def _constants():
    cst = np.zeros((128, 512), np.float32)
    cst[:, 0:128] = np.eye(128, dtype=np.float32)
    cst[:, 128:256] = 1.0
    s_ = np.arange(128)[:, None]; t_ = np.arange(128)[None, :]
    cst[:, 256:384] = (s_ <= t_).astype(np.float32)
    cst[:, 384:512] = (s_ >= t_).astype(np.float32)
    gam = 1.0 - 2.0 ** (-5.0 - np.arange(4, dtype=np.float64))
    m_ = np.arange(128)[:, None]; c_ = np.arange(3968)[None, :]
    rmask = np.stack([(256.0 ** -0.5) * g ** np.abs(c_ - 1920 - m_) for g in gam]).astype(np.float32)
    i_ = np.arange(256, dtype=np.float64)
    ang = 2 * np.pi * np.outer(i_, i_) / 256.0
    cg = np.stack([np.cos(ang), np.sin(ang)]).astype(np.float32) / 16.0
    n_ = np.arange(S, dtype=np.int64)
    nk = (np.outer(n_, n_) % S).astype(np.float64)
    ang = 2 * np.pi * nk / S
    dft = np.stack([np.cos(ang), -np.sin(ang)]).astype(np.float32) / np.float32(np.sqrt(S))
    return cst, rmask, cg, dft


_CACHE = {}


def kernel(x, positions, norm_w, w_in, hgrn_lb_logits, hgrn_norm_w, w_ret_o, w_hgrn_o, w_fnet, w_out,
           w_up, conv_w, conv_b, w_down):
    if "nc" not in _CACHE:
        _CACHE["nc"] = build_program()[0]
        _CACHE["const"] = _constants()
    nc = _CACHE["nc"]
    cst, rmask, cg, dft = _CACHE["const"]
    f32 = np.float32
    prm = np.zeros((128, NPRM), f32)
    prm[:, P_LBL:P_LBL + 32] = np.asarray(hgrn_lb_logits, f32).reshape(2, DEPTH, 8, 128).transpose(3, 0, 1, 2).reshape(128, 32)
    prm[:, P_HNW:P_HNW + 2] = np.asarray(hgrn_norm_w, f32).T
    prm[:, P_CW:P_CW + 264] = np.asarray(conv_w, f32).reshape(DEPTH, 3, 44, 128).transpose(3, 0, 1, 2).reshape(128, 264)
    prm[:, P_CB:P_CB + 88] = np.asarray(conv_b, f32).reshape(DEPTH, 44, 128).transpose(2, 0, 1).reshape(128, 88)
    prm[:, P_IF] = (10000.0 ** (-np.arange(128, dtype=np.float32) / np.float32(128))).astype(f32)
    shared = dict(norm_w=np.ascontiguousarray(np.asarray(norm_w, f32).reshape(DEPTH * 4, D)),
                  w_in=np.asarray(w_in, f32), w_ret_o=np.asarray(w_ret_o, f32), w_hgrn_o=np.asarray(w_hgrn_o, f32),
                  w_fnet=np.asarray(w_fnet, f32), w_out=np.asarray(w_out, f32), w_up=np.asarray(w_up, f32),
                  w_down=np.asarray(w_down, f32), prm=prm, rmask=rmask, cg=cg, dft=dft, cst=cst)
    xs = np.asarray(x, f32)
    pos = np.asarray(positions, np.int32)
    in_maps = [dict(shared, x=np.ascontiguousarray(xs[b]), pos=np.ascontiguousarray(pos[b:b + 1])) for b in range(8)]
    res = run_bass_kernel_spmd(nc, in_maps, core_ids=list(range(8)))
    return np.stack([np.asarray(r["out"], f32) for r in res.results], axis=0)
```

```python
import numpy as np
import concourse.bass as bass
import concourse.mybir as mybir

F32 = mybir.dt.float32
BF16 = mybir.dt.bfloat16
I32 = mybir.dt.int32
U8 = mybir.dt.uint8
AF = mybir.ActivationFunctionType
ALU = mybir.AluOpType

ENGS = ("pe", "act", "dve", "pool", "sp")
BUCKET = 2048
EPOCH = 12000


class _Rec:
    __slots__ = ("lo", "hi", "iid", "eng", "alive")

    def __init__(self, lo, hi, iid, eng):
        self.lo, self.hi, self.iid, self.eng, self.alive = lo, hi, iid, eng, True


class _Space:
    def __init__(self):
        self.w = {}
        self.r = {}

    @staticmethod
    def _bk(lo, hi):
        return range(lo // BUCKET, (hi - 1) // BUCKET + 1)

    def query(self, table, lo, hi, out):
        for b in self._bk(lo, hi):
            lst = table.get(b)
            if not lst:
                continue
            for rec in lst:
                if rec.alive and rec.lo < hi and lo < rec.hi:
                    out.add(rec.iid)

    def kill_covered(self, table, lo, hi):
        for b in self._bk(lo, hi):
            lst = table.get(b)
            if not lst:
                continue
            keep = []
            for rec in lst:
                if not rec.alive:
                    continue
                if lo <= rec.lo and rec.hi <= hi:
                    rec.alive = False
                else:
                    keep.append(rec)
            table[b] = keep

    def insert(self, table, rec):
        for b in self._bk(rec.lo, rec.hi):
            table.setdefault(b, []).append(rec)

    def add_reader(self, lo, hi, iid, eng):
        b0 = lo // BUCKET
        lst = self.r.get(b0)
        if lst and eng != "dma":
            for rec in lst:
                if rec.alive and rec.lo == lo and rec.hi == hi and rec.eng == eng:
                    rec.iid = iid
                    return
        self.insert(self.r, _Rec(lo, hi, iid, eng))


class Acc:
    __slots__ = ("ap", "regs")

    def __init__(self, ap, regs):
        self.ap, self.regs = ap, regs


def _regions(shape, idx):
    if len(shape) == 1:
        return [(idx[0][0], idx[0][1])]
    inner = int(np.prod(shape[1:]))
    sub = _regions(shape[1:], idx[1:])
    if len(sub) == 1 and sub[0] == (0, inner):
        return [(idx[0][0] * inner, idx[0][1] * inner)]
    out = []
    for i in range(idx[0][0], idx[0][1]):
        for (l, h) in sub:
            out.append((i * inner + l, i * inner + h))
    return out


class View:
    def __init__(self, space, ap, free_shape, esize, base_bytes=0, track=True):
        self.space, self.ap, self.shape, self.esize, self.base, self.track = space, ap, tuple(free_shape), esize, base_bytes, track

    def __getitem__(self, key):
        if not isinstance(key, tuple):
            key = (key,)
        key = key + (slice(None),) * (1 + len(self.shape) - len(key))
        ap = self.ap[key]
        if not self.track:
            return Acc(ap, [])
        idx = []
        for k, n in zip(key[1:], self.shape):
            if isinstance(k, slice):
                assert k.step in (None, 1)
                idx.append((k.start or 0, n if k.stop is None else k.stop))
            else:
                idx.append((k, k + 1))
        regs = _regions(self.shape, idx)
        if len(regs) > 48:
            regs = [(regs[0][0], regs[-1][1])]
        out = [(self.space, self.base + l * self.esize, self.base + h * self.esize) for (l, h) in regs]
        if self.space == "ps":
            lo = min(r[1] for r in out) // 2048 * 2048
            hi = (max(r[2] for r in out) + 2047) // 2048 * 2048
            out = [("ps", lo, hi)]
        return Acc(ap, out)


class Ins:
    __slots__ = ("eng", "fn", "deps", "dma", "semkey", "semval", "ms", "needs_inc", "waits", "name")


class Prog:
    def __init__(self, nc):
        self.nc = nc
        self.ins = []
        self.streams = {e: [] for e in ENGS}
        self.spaces = {}
        self.dma_count = {}
        self.cur_group = None

    def _sp(self, name):
        s = self.spaces.get(name)
        if s is None:
            s = self.spaces[name] = _Space()
        return s

    def add(self, eng, fn, reads=(), writes=(), dma=False, semkey=None, extra_deps=(), name=""):
        iid = len(self.ins)
        deps = set(extra_deps)
        rregs = [r for a in reads for r in a.regs]
        wregs = [r for a in writes for r in a.regs]
        for (sp, lo, hi) in rregs:
            s = self._sp(sp)
            s.query(s.w, lo, hi, deps)
            if sp == "ps":
                tmp = set()
                s.query(s.r, lo, hi, tmp)
                for d in tmp:
                    if self.ins[d].eng != eng:
                        deps.add(d)
        for (sp, lo, hi) in wregs:
            s = self._sp(sp)
            s.query(s.w, lo, hi, deps)
            s.query(s.r, lo, hi, deps)
        for (sp, lo, hi) in wregs:
            s = self._sp(sp)
            s.kill_covered(s.w, lo, hi)
            s.kill_covered(s.r, lo, hi)
            s.insert(s.w, _Rec(lo, hi, iid, "dma" if dma else eng))
        for (sp, lo, hi) in rregs:
            s = self._sp(sp)
            s.add_reader(lo, hi, iid, "dma" if dma else eng)
        deps.discard(iid)
        I = Ins()
        I.eng, I.fn, I.deps, I.dma, I.semkey, I.name = eng, fn, deps, dma, semkey, name
        I.semval = None
        I.needs_inc = dma
        I.waits = []
        I.ms = None
        if dma:
            assert semkey is not None
            self.dma_count[semkey] = self.dma_count.get(semkey, 0) + 16
            I.semval = self.dma_count[semkey]
            if self.cur_group is not None:
                self.cur_group.append(I)
        self.ins.append(I)
        self.streams[eng].append(iid)
        return iid

    def begin_group(self):
        self.cur_group = []

    def end_group(self):
        g = self.cur_group
        self.cur_group = None
        if g:
            tot = max(i.semval for i in g)
            for i in g:
                i.semval = tot
        return g

    def finalize(self):
        ins = self.ins
        for I in ins:
            for d in I.deps:
                D = ins[d]
                if not D.dma:
                    if D.eng == "pe" and I.eng == "pe" and not I.dma:
                        continue
                    D.needs_inc = True
        for e in ENGS:
            g = 0
            for iid in self.streams[e]:
                I = ins[iid]
                if I.needs_inc and not I.dma:
                    I.ms = g
                    g += 1
        self.n_ms = {e: sum(1 for iid in self.streams[e] if ins[iid].ms is not None) for e in ENGS}
        waited = {e: {p: -1 for p in ENGS} for e in ENGS}
        waited_dma = {e: {} for e in ENGS}
        last_ms = {}
        for e in ENGS:
            cur = -1
            for iid in self.streams[e]:
                if ins[iid].ms is not None:
                    cur = ins[iid].ms
                last_ms[iid] = cur
        for I in ins:
            need = {}
            need_dma = {}
            for d in I.deps:
                D = ins[d]
                if D.dma:
                    need_dma[D.semkey] = max(need_dma.get(D.semkey, 0), D.semval)
                else:
                    if D.eng == "pe" and I.eng == "pe" and not I.dma:
                        continue
                    need[D.eng] = max(need.get(D.eng, -1), D.ms)
            for p, m in need.items():
                if waited[I.eng][p] >= m:
                    continue
                waited[I.eng][p] = m
                I.waits.append(("c", p, m))
            for k, v in need_dma.items():
                if waited_dma[I.eng].get(k, 0) >= v:
                    continue
                waited_dma[I.eng][k] = v
                I.waits.append(("d", k, v))

    def emit(self, sems_eng, sems_dma, block):
        nc = self.nc
        ins = self.ins

        def run(engname, eng):
            for iid in self.streams[engname]:
                I = ins[iid]
                for (kind, a, b) in I.waits:
                    if kind == "c":
                        eng.wait_ge(sems_eng[a][b // EPOCH], b % EPOCH + 1)
                    else:
                        eng.wait_ge(sems_dma[a], b)
                r = I.fn(eng)
                if I.dma:
                    r.then_inc(sems_dma[I.semkey], 16)
                elif I.ms is not None:
                    r.then_inc(sems_eng[engname][I.ms // EPOCH], 1)

        @block.tensor
        def _(e):
            run("pe", e)

        @block.scalar
        def _(e):
            run("act", e)

        @block.vector
        def _(e):
            run("dve", e)

        @block.gpsimd
        def _(e):
            run("pool", e)

        @block.sync
        def _(e):
            run("sp", e)

import contextlib, os
HG_STOP = os.environ.get('HG_STOP', '')
HG_S1 = int(os.environ.get('HG_S1', '9'))
from concourse.bass_utils import run_bass_kernel_spmd

D = 1024
S = 2048
DEPTH = 2
DFF = 2816
DIN = 15360
NT = 16
NG = 4
EPSV = 1e-6
BIGB = 212800
C_RQ, C_RK, C_RV, C_RG, C_HQ, C_HZF, C_HZB, C_HI, C_HG, C_FU, C_GA = 0, 1024, 2048, 4096, 6144, 7168, 8192, 9216, 10240, 11264, 12288
MAGIC = 12582912.0
TWO_PI = 6.28318
P_LBL = 0
P_HNW = 32
P_CW = 34
P_CB = 298
P_IF = 386
NPRM = 388


def build_program(debug_stage=None, nlayers=DEPTH):
    nc = bass.Bass("TRN2", target_bir_lowering=False)
    dx = nc.dram_tensor("x", [S, D], F32, kind="ExternalInput").ap()
    dpos = nc.dram_tensor("pos", [1, S], I32, kind="ExternalInput").ap()
    dnw = nc.dram_tensor("norm_w", [DEPTH * 4, D], F32, kind="ExternalInput").ap()
    dwin = nc.dram_tensor("w_in", [DEPTH, D, DIN], F32, kind="ExternalInput").ap()
    dwro = nc.dram_tensor("w_ret_o", [DEPTH, 2048, D], F32, kind="ExternalInput").ap()
    dwho = nc.dram_tensor("w_hgrn_o", [DEPTH, D, D], F32, kind="ExternalInput").ap()
    dwfn = nc.dram_tensor("w_fnet", [DEPTH, D, D], F32, kind="ExternalInput").ap()
    dwout = nc.dram_tensor("w_out", [DEPTH, D, D], F32, kind="ExternalInput").ap()
    dwup = nc.dram_tensor("w_up", [DEPTH, D, 2 * DFF], F32, kind="ExternalInput").ap()
    dwdn = nc.dram_tensor("w_down", [DEPTH, DFF, D], F32, kind="ExternalInput").ap()
    dprm = nc.dram_tensor("prm", [128, NPRM], F32, kind="ExternalInput").ap()
    dmask = nc.dram_tensor("rmask", [4, 128, 3968], F32, kind="ExternalInput").ap()
    dcg = nc.dram_tensor("cg", [2, 256, 256], F32, kind="ExternalInput").ap()
    ddft = nc.dram_tensor("dft", [2, S, S], F32, kind="ExternalInput").ap()
    dcol = nc.dram_tensor("dcol", [128, 2, NT], F32, kind="ExternalInput").ap()
    dcst = nc.dram_tensor("cst", [128, 512], F32, kind="ExternalInput").ap()
    dout = nc.dram_tensor("out", [S, D], F32, kind="ExternalOutput").ap()
    dxs = nc.dram_tensor("xs", [S, D], F32, kind="Internal").ap()
    ddbg = None
    if debug_stage is not None:
        ddbg = nc.dram_tensor("dbg", [128, 8, S], F32, kind="ExternalOutput").ap()

    es = contextlib.ExitStack()
    big = es.enter_context(nc.sbuf_tensor("big", [128, BIGB // 4], F32))
    ps = es.enter_context(nc.psum_tensor("ps", [128, 8 * 512], F32))
    P = Prog(nc)

    def sview(offb, shape, dt):
        esz = 2 if dt == BF16 else 4
        n = int(np.prod(shape))
        assert offb % 4 == 0 and (n * esz) % 4 == 0, (offb, shape)
        assert offb + n * esz <= BIGB, ("SBUF overflow", offb, shape)
        ap = big[:, offb // 4: offb // 4 + n * esz // 4]
        if dt != F32:
            ap = ap.bitcast(dt)
        if len(shape) > 1:
            names = " ".join("d%d" % i for i in range(len(shape)))
            ap = ap.rearrange("p (%s) -> p %s" % (names, names), **{"d%d" % i: shape[i] for i in range(len(shape) - 1)})
        return View("sb", ap, shape, esz, offb)

    def bank(b, n=512):
        return View("ps", ps[:, b * 512: b * 512 + n], [n], 4, b * 2048)

    def bank3(b, a, n):
        return View("ps", ps[:, b * 512: b * 512 + a * n].rearrange("p (a n) -> p a n", a=a), [a, n], 4, b * 2048)

    psb = View("ps", ps[:, 7 * 512: 8 * 512].bitcast(BF16).rearrange("p (a b) -> p a b", a=8), [8, 128], 2, 7 * 2048)

    def dview(ap, name, track=True):
        return View(name, ap.rearrange("(t p) d -> p t d", p=128), [NT, D], 4, 0, track=track)

    vx = dview(dx, "d_x", track=False)
    vxs = dview(dxs, "d_xs")
    vout = dview(dout, "d_out")

    def mm(out, lhsT, rhs, start, stop):
        P.add("pe", lambda e: e.matmul(out.ap, lhsT=lhsT.ap, rhs=rhs.ap, start=start, stop=stop), reads=[lhsT, rhs], writes=[out])

    def tr(out, in_, ident):
        P.add("pe", lambda e: e.transpose(out=out.ap, in_=in_.ap, identity=ident.ap), reads=[in_, ident], writes=[out])

    def act(out, in_, func, scale=None, bias=None, accum=None, eng="act"):
        rd = [in_]
        kw = {}
        if scale is not None:
            if isinstance(scale, Acc):
                rd.append(scale); kw["scale"] = scale.ap
            else:
                kw["scale"] = scale
        if bias is not None:
            if isinstance(bias, Acc):
                rd.append(bias); kw["bias"] = bias.ap
            else:
                kw["bias"] = bias
        wr = [out]
        if accum is not None:
            wr.append(accum); kw["accum_out"] = accum.ap
        P.add("act", lambda e: e.activation(out=out.ap, in_=in_.ap, func=func, **kw), reads=rd, writes=wr)

    def tt(out, a, b, op, eng="dve"):
        P.add(eng, lambda e: e.tensor_tensor(out=out.ap, in0=a.ap, in1=b.ap, op=op), reads=[a, b], writes=[out])

    def ts(out, a, s1, s2, op0, op1=None, eng="dve"):
        rd = [a]
        v1 = s1.ap if isinstance(s1, Acc) else s1
        v2 = s2.ap if isinstance(s2, Acc) else s2
        if isinstance(s1, Acc): rd.append(s1)
        if isinstance(s2, Acc): rd.append(s2)
        if op1 is None:
            P.add(eng, lambda e: e.tensor_scalar(out=out.ap, in0=a.ap, scalar1=v1, scalar2=None, op0=op0), reads=rd, writes=[out])
        else:
            P.add(eng, lambda e: e.tensor_scalar(out=out.ap, in0=a.ap, scalar1=v1, scalar2=v2, op0=op0, op1=op1), reads=rd, writes=[out])

    def stt(out, a, s, b, op0, op1):
        rd = [a, b]
        v = s.ap if isinstance(s, Acc) else s
        if isinstance(s, Acc): rd.append(s)
        P.add("dve", lambda e: e.scalar_tensor_tensor(out=out.ap, in0=a.ap, scalar=v, in1=b.ap, op0=op0, op1=op1), reads=rd, writes=[out])

    def cpy(out, in_, eng="dve"):
        if eng == "act":
            act(out, in_, AF.Copy)
        else:
            P.add(eng, lambda e: e.tensor_copy(out=out.ap, in_=in_.ap), reads=[in_], writes=[out])

    def memset(out, v, eng="dve"):
        P.add(eng, lambda e: e.memset(out.ap, v), writes=[out])

    def recip(out, in_):
        P.add("dve", lambda e: e.reciprocal(out=out.ap, in_=in_.ap), reads=[in_], writes=[out])

    def scan(out, ones, data):
        P.add("dve", lambda e: e.tensor_tensor_scan(out=out.ap, data0=ones.ap, data1=data.ap, initial=0.0, op0=ALU.mult, op1=ALU.add), reads=[ones, data], writes=[out])

    def cpred(out, mask, data):
        P.add("dve", lambda e: e.copy_predicated(out=out.ap, mask=mask.ap, data=data.ap), reads=[mask, data, out], writes=[out])

    dma_n = [0]

    def dma(eng, out_acc, in_acc, out_ap=None, in_ap=None, key=None):
        i = dma_n[0]; dma_n[0] += 1
        k = key or ("q_%s_%d" % (eng, i % 12))
        oa = out_ap if out_ap is not None else out_acc.ap
        ia = in_ap if in_ap is not None else in_acc.ap
        rd = [in_acc] if in_acc is not None else []
        wr = [out_acc] if out_acc is not None else []
        prev = dma_last.get(k)
        iid = P.add(eng, lambda e: e.dma_start(out=oa, in_=ia), reads=rd, writes=wr, dma=True, semkey=k,
                    extra_deps=[prev] if prev is not None else [])
        dma_last[k] = iid
        return iid

    dma_last = {}

    CB = 0
    ident = sview(CB + 0, [128], BF16)
    onesb = sview(CB + 256, [128], BF16)
    maskf = sview(CB + 512, [128], I32)
    maskb = sview(CB + 1024, [128], I32)
    onesf = sview(CB + 1536, [128], F32)
    epsb = sview(CB + 2048, [1], F32)
    prm = sview(CB + 2112, [NPRM], F32)
    hp = sview(CB + 3680, [3, 2, 8], F32)
    cst32 = sview(CB + 3904, [512], F32)
    XNT_B = 6144
    xnT = sview(XNT_B, [8, S], BF16)
    MRG_B = XNT_B + 32768
    merged = sview(MRG_B, [8, S], F32)
    POOL_B = MRG_B + 65536
    POOL_SZ = BIGB - POOL_B

    class Bump:
        def __init__(self, base, size):
            self.base, self.size, self.cur = base, size, 0

        def reset(self, base=None, size=None):
            if base is not None:
                self.base, self.size = base, size
            self.cur = 0

        def __call__(self, shape, dt):
            esz = 2 if dt == BF16 else 4
            n = int(np.prod(shape)) * esz
            n = (n + 63) // 64 * 64
            assert self.cur + n <= self.size, ("pool overflow", self.cur, n, self.size)
            v = sview(self.base + self.cur, shape, dt)
            self.cur += n
            return v

    pool = Bump(POOL_B, POOL_SZ)

    class Arena:
        def __init__(self):
            self.base = self.size = self.cur = 0
            self.n = 0
            self.hist = []

        def set(self, base, size):
            self.base, self.size, self.cur = base, size, 0

        def load(self, src_ap, nch, ncols):
            nb = nch * ncols * 2
            nb = (nb + 63) // 64 * 64
            assert nb <= self.size, ("arena too small", nb, self.size)
            if self.cur + nb > self.size:
                self.cur = 0
            v = sview(self.base + self.cur, [nch, ncols], BF16)
            self.cur += nb
            k = "w%d" % (self.n % 8)
            self.n += 1
            prev = dma_last.get(k)
            iid = P.add("pool", lambda e: e.dma_start(out=v[:].ap, in_=src_ap), writes=[v[:]], dma=True, semkey=k,
                        extra_deps=[prev] if prev is not None else [])
            dma_last[k] = iid
            return v

    arena = Arena()

    def wcols(wl, c0, n):
        return wl.rearrange("(c p) n -> p c n", p=128)[:, :, c0:c0 + n]

    dma("sp", cst32[:], None, in_ap=dcst[:, :])
    cpy(ident[:], cst32[:, 0:128])
    cpy(onesb[:], cst32[:, 128:256])
    cpy(maskf[:], cst32[:, 256:384])
    cpy(maskb[:], cst32[:, 384:512])
    memset(onesf[:], 1.0)
    memset(epsb[:], EPSV)
    dma("sp", prm[:], None, in_ap=dprm[:, :])

    def norm_phase(l, ni, src):
        pool.reset()
        nwb = pool([D], F32)
        dma("sp", nwb[:], None, in_ap=dnw[l * 4 + ni: l * 4 + ni + 1, :].partition_broadcast(128))
        NB = 4
        xb = [pool([D], F32) for _ in range(NB)]
        junk = pool([D], F32)
        xnb = [pool([8, 128], BF16) for _ in range(NB)]
        sc = pool([NB, 4], F32)
        def stA(t):
            i = t % NB
            xt = xb[i]
            dma("sp", xt[:], src[:, t, :])
            act(junk[:], xt[:], AF.Square, accum=sc[:, i, 0:1])
            act(sc[:, i, 1:2], sc[:, i, 0:1], AF.Sqrt, scale=1.0 / D, bias=epsb[:, 0:1])
            recip(sc[:, i, 2:3], sc[:, i, 1:2])
            xn_flat = View("sb", xnb[i].ap.rearrange("p a b -> p (a b)"), [D], 2, xnb[i].base)
            stt(xn_flat[:], xt[:], sc[:, i, 2:3], nwb[:], ALU.mult, ALU.mult)

        def stB(t):
            i = t % NB
            for c in range(8):
                tr(psb[:, c, :], xnb[i][:, c, :], ident[:])
            cpy(xnT[:, :, t * 128:(t + 1) * 128], psb[:], eng="act")

        stA(0)
        stA(1)
        for t in range(NT):
            if t + 2 < NT:
                stA(t + 2)
            stB(t)

    def residual_tile(t, halves, nwb, xb, yb, sc, src, dst):
        i = t % len(xb)
        junk = yb[i]
        act(junk[:, 0:512], halves[0], AF.Square, accum=sc[:, i, 0:1])
        act(junk[:, 512:1024], halves[1], AF.Square, accum=sc[:, i, 1:2])
        tt(sc[:, i, 2:3], sc[:, i, 0:1], sc[:, i, 1:2], ALU.add)
        act(sc[:, i, 3:4], sc[:, i, 2:3], AF.Sqrt, scale=1.0 / D, bias=epsb[:, 0:1])
        recip(sc[:, i, 4:5], sc[:, i, 3:4])
        dma("sp", xb[i][:], src[:, t, :])
        for h in range(2):
            stt(yb[i][:, h * 512:(h + 1) * 512], halves[h], sc[:, i, 4:5], nwb[:, h * 512:(h + 1) * 512], ALU.mult, ALU.mult)
        tt(yb[i][:], yb[i][:], xb[i][:], ALU.add)
        return dma("sp", dst[:, t, :], yb[i][:])

    def dump_merged():
        iid = dma("sp", None, merged[:], out_ap=ddbg[:, :, :])
        P.add("sp", lambda e: e.nop(), extra_deps=[iid])

    out_dmas = []

    for l in range(nlayers):
        wl = dwin[l]
        src_a = vx if l == 0 else vxs
        norm_phase(l, 0, src_a)

        pool.reset()
        pi = pool([S], I32)
        pf = pool([S], F32)
        u1 = pool([S], F32)
        cosT = pool([S], F32)
        sinT = pool([S], F32)
        dma("sp", pi[:], None, in_ap=dpos[0:1, :].partition_broadcast(128))
        cpy(pf[:], pi[:])
        ts(u1[:], pf[:], prm[:, P_IF:P_IF + 1], 1.0 / (2 * np.pi), ALU.mult, ALU.mult)
        ts(pf[:], u1[:], MAGIC, MAGIC, ALU.add, ALU.subtract)
        tt(pf[:], u1[:], pf[:], ALU.subtract)
        act(sinT[:], pf[:], AF.Sin, scale=TWO_PI)
        ts(u1[:], u1[:], 0.25, None, ALU.add)
        ts(pf[:], u1[:], MAGIC, MAGIC, ALU.add, ALU.subtract)
        tt(pf[:], u1[:], pf[:], ALU.subtract)
        act(cosT[:], pf[:], AF.Sin, scale=TWO_PI)
        pool.cur = 0
        _ = pool([S], I32); _ = pool([S], F32); _ = pool([S], F32); _ = pool([S], F32); _ = pool([S], F32)
        pool.cur = 0
        qT = pool([2, S], BF16)
        kT = pool([2, S], BF16)
        tmp1 = pool([512], F32); tmp2 = pool([512], F32)
        sq = pool([4, 512], BF16)
        assert pool.cur <= 24576
        pool.cur = 40960
        V = pool([NT, 512], BF16)
        mask = pool([3968], BF16)
        roT = pool([4, 512], BF16)
        PT = [pool([512], BF16) for _ in range(3)]
        rstd = pool([512], F32)
        sg4 = pool([4, 512], F32)
        sg = pool([512], F32)
        tq = pool([512], F32)
        arena.set(POOL_B + pool.cur, POOL_SZ - pool.cur)
        for h in range(4):
            wq = arena.load(wcols(wl, C_RQ + h * 256, 256), 8, 256)
            wk = arena.load(wcols(wl, C_RK + h * 256, 256), 8, 256)
            wv = arena.load(wcols(wl, C_RV + h * 512, 512), 8, 512)
            dma("pool", mask[:], None, in_ap=dmask[h, :, :])
            def Vproj(t):
                b = bank(6 + t % 2)
                for k in range(8):
                    mm(b[:], xnT[:, k, t * 128:(t + 1) * 128], wv[:, k, :], k == 0, k == 7)
                cpy(V[:, t, :], b[:], eng="act")

            for wi_, (w_, dst) in enumerate(((wq, qT), (wk, kT))):
                for g in range(NG):
                    gs = slice(g * 512, (g + 1) * 512)
                    u_ = wi_ * NG + g
                    pb = 2 * (u_ % 3)
                    for dc in range(2):
                        for k in range(8):
                            mm(bank(pb + dc)[:], w_[:, k, dc * 128:(dc + 1) * 128], xnT[:, k, gs], k == 0, k == 7)
                    A, B = bank(pb)[:], bank(pb + 1)[:]
                    tt(tmp1[:], A, cosT[:, gs], ALU.mult)
                    tt(tmp2[:], B, sinT[:, gs], ALU.mult)
                    tt(dst[:, 0, gs], tmp1[:], tmp2[:], ALU.subtract)
                    tt(tmp1[:], A, sinT[:, gs], ALU.mult)
                    tt(tmp2[:], B, cosT[:, gs], ALU.mult)
                    tt(dst[:, 1, gs], tmp1[:], tmp2[:], ALU.add)
                    Vproj(2 * u_)
                    Vproj(2 * u_ + 1)
            wg = arena.load(wcols(wl, C_RG + h * 512, 512), 8, 512)
            wro = arena.load(dwro[l][h * 512:(h + 1) * 512, :].rearrange("(c p) n -> p c n", p=128), 4, D)
            LA = 2

            def QK(g, j):
                scb = bank(4 + j % 3)
                for dc in range(2):
                    mm(scb[:], kT[:, dc, j * 128:(j + 1) * 128], qT[:, dc, g * 512:(g + 1) * 512], dc == 0, dc == 1)

            for j in range(LA):
                QK(0, j)
            for g in range(NG):
                gs = slice(g * 512, (g + 1) * 512)
                for j in range(NT):
                    if j + LA < NT:
                        QK(g, j + LA)
                    off = 1920 - 128 * j + 512 * g
                    pt = PT[j % 3]
                    tt(pt[:], bank(4 + j % 3)[:], mask[:, off:off + 512], ALU.mult)
                    for e_ in range(4):
                        mm(bank(e_)[:], V[:, j, e_ * 128:(e_ + 1) * 128], pt[:], j == 0, j == NT - 1)
                for e_ in range(4):
                    act(sq[:, e_, :], bank(e_)[:], AF.Square)

                def RG(e_):
                    gb = bank(6 + e_ % 2)
                    for k in range(8):
                        mm(gb[:], wg[:, k, e_ * 128:(e_ + 1) * 128], xnT[:, k, gs], k == 0, k == 7)
                    act(sg4[:, e_, :], gb[:], AF.Silu)

                RG(0)
                for e_ in range(4):
                    mm(bank(5)[:], onesb[:], sq[:, e_, :], e_ == 0, e_ == 3)
                act(rstd[:], bank(5)[:], AF.Ln, scale=1.0 / 512, bias=epsb[:, 0:1])
                act(rstd[:], rstd[:], AF.Exp, scale=-0.5)
                RG(1)
                RG(2)
                RG(3)
                for e_ in range(4):
                    tt(tq[:], bank(e_)[:], rstd[:], ALU.mult)
                    tt(roT[:, e_, :], tq[:], sg4[:, e_, :], ALU.mult)
                if g + 1 < NG:
                    for j in range(LA):
                        QK(g + 1, j)
                for c in range(8):
                    yb_ = bank(6 + c % 2)
                    for e_ in range(4):
                        mm(yb_[:], wro[:, e_, c * 128:(c + 1) * 128], roT[:, e_, :], e_ == 0, e_ == 3)
                    if h == 0:
                        cpy(merged[:, c, gs], yb_[:], eng="act")
                    else:
                        tt(merged[:, c, gs], merged[:, c, gs], yb_[:], ALU.add)
        for c in range(8):
            wga = arena.load(wcols(wl, C_GA + c * 128, 128), 8, 128)
            for g in range(NG):
                gs = slice(g * 512, (g + 1) * 512)
                gb = bank(4 + g % 2)
                for k in range(8):
                    mm(gb[:], wga[:, k, :], xnT[:, k, gs], k == 0, k == 7)
                act(sg[:], gb[:], AF.Sigmoid)
                tt(merged[:, c, gs], merged[:, c, gs], sg[:], ALU.mult)
        if debug_stage == "ret" and l == 0:
            dump_merged(); break

        pool.reset()
        e0 = pool([2, 8], F32); e1 = pool([2, 8], F32)
        lbl = View("sb", prm.ap[:, P_LBL:P_LBL + 32].rearrange("p (d l h) -> p d l h", d=2, l=2), [2, 2, 8], 4, prm.base + P_LBL * 4)
        if l == 0:
            memset(hp[:, 0, :, :], 0.0)
        else:
            act(e0[:], lbl[:, :, 0, :], AF.Exp)
            act(e1[:], lbl[:, :, 1, :], AF.Exp)
            tt(e0[:], e0[:], e1[:], ALU.add)
            recip(e0[:], e0[:])
            tt(hp[:, 0, :, :], e1[:], e0[:], ALU.mult)
        ts(hp[:, 1, :, :], hp[:, 0, :, :], -1.0, 1.0, ALU.mult, ALU.add)
        ts(hp[:, 2, :, :], hp[:, 1, :, :], -1.0, None, ALU.mult)
        hoT = pool([8, S], BF16)
        qk = {("q", 0): pool([S], BF16), ("k", 0): pool([S], BF16), ("q", 1): pool([S], BF16), ("k", 1): pool([S], BF16)}
        Vh = pool([NT, 128], BF16)
        Sbf = [pool([NT, 128], BF16) for _ in range(2)]
        gateT = pool([S], BF16)
        St = [pool([128], F32) for _ in range(2)]
        Esc = [pool([3, NT], F32) for _ in range(2)]
        AT = [pool([128], BF16) for _ in range(4)]
        utmp = [pool([128], F32) for _ in range(2)]
        rmk = pool([512], F32)
        memset(rmk[:], 1.0)
        for ck in range(4):
            memset(rmk[:, ck * 128:ck * 128 + 1], 0.0)
        for a_ in AT:
            memset(a_[:], 0.0)
        Tbase = pool.cur
        q32 = pool([512], F32); sgq = pool([512], F32)
        SG = [pool([512], F32) for _ in range(2)]
        Kt = [pool([512], F32) for _ in range(2)]
        Bt = [pool([512], F32) for _ in range(2)]
        Dt = [pool([512], F32) for _ in range(2)]
        tdd = [pool([4], F32) for _ in range(2)]
        Tend = pool.cur
        pool.cur = Tbase
        kTt = [pool([NT, 128], BF16) for _ in range(2)]
        rstd = pool([512], F32); tq = pool([512], F32); sqb = pool([512], BF16)
        pool.cur = max(Tend, pool.cur)
        arena.set(POOL_B + pool.cur, POOL_SZ - pool.cur)
        hscale = 128.0 ** -0.5

        def v4(t_):
            return View("sb", t_.ap.rearrange("p (a b) -> p a b", a=4), [4, 128], 4, t_.base)

        for h in range(8):
            wq = arena.load(wcols(wl, C_HQ + h * 128, 128), 8, 128)
            wz = [arena.load(wcols(wl, C_HZF + h * 128, 128), 8, 128), arena.load(wcols(wl, C_HZB + h * 128, 128), 8, 128)]
            wgh = arena.load(wcols(wl, C_HG + h * 128, 128), 8, 128)
            wi = arena.load(wcols(wl, C_HI + h * 128, 128), 8, 128)
            for g in range(NG):
                gs = slice(g * 512, (g + 1) * 512)
                bb = (g % 2) * 4
                for (w_, b_) in ((wq, bb), (wz[0], bb + 1), (wz[1], bb + 2), (wgh, bb + 3)):
                    for k in range(8):
                        mm(bank(b_)[:], w_[:, k, :], xnT[:, k, gs], k == 0, k == 7)
                act(sgq[:], bank(bb)[:], AF.Sigmoid)
                act(SG[0][:], bank(bb + 1)[:], AF.Sigmoid)
                act(SG[1][:], bank(bb + 2)[:], AF.Sigmoid)
                act(Dt[0][:], bank(bb + 3)[:], AF.Sigmoid)
                tt(q32[:], bank(bb)[:], sgq[:], ALU.mult)
                tt(gateT[:, gs], bank(bb + 3)[:], Dt[0][:], ALU.mult)
                for d_ in range(2):
                    ts(Kt[d_][:], SG[d_][:], hp[:, 2, d_, h:h + 1], hp[:, 1, d_, h:h + 1], ALU.mult, ALU.add)
                for d_ in range(2):
                    act(SG[d_][:], SG[d_][:], AF.Ln, scale=hp[:, 1, d_, h:h + 1], bias=hp[:, 0, d_, h:h + 1])
                for d_ in range(2):
                    scan(Bt[d_][:], rmk[:], SG[d_][:])
                c4 = slice(g * 4, (g + 1) * 4)
                for d_ in range(2):
                    act(Esc[d_][:, 0, c4], v4(Bt[d_])[:, :, 127], AF.Exp)
                    act(Esc[d_][:, 1, c4], v4(Bt[d_])[:, :, 63], AF.Exp)
                for d_ in range(2):
                    tt(tdd[d_][:], v4(Bt[d_])[:, :, 127], v4(Bt[d_])[:, :, 63], ALU.subtract)
                for d_ in range(2):
                    act(Esc[d_][:, 2, c4], tdd[d_][:], AF.Exp)
                b63 = v4(Bt[0])[:, :, 63:64]
                P.add("dve", lambda e, o=v4(Dt[0])[:], a=v4(Bt[0])[:], b=b63: e.tensor_tensor(out=o.ap, in0=a.ap, in1=b.ap.broadcast_to([128, 4, 128]), op=ALU.subtract),
                      reads=[v4(Bt[0])[:]], writes=[v4(Dt[0])[:]])
                tt(Dt[1][:], SG[1][:], Bt[1][:], ALU.subtract)
                c64 = v4(Dt[1])[:, :, 64:65]
                P.add("dve", lambda e, o=v4(Bt[1])[:], a=v4(Dt[1])[:], b=c64: e.tensor_tensor(out=o.ap, in0=a.ap, in1=b.ap.broadcast_to([128, 4, 128]), op=ALU.subtract),
                      reads=[v4(Dt[1])[:]], writes=[v4(Bt[1])[:]])
                Dd = [Dt[0], Bt[1]]
                for d_ in range(2):
                    act(SG[d_][:], Dd[d_][:], AF.Exp)
                for d_ in range(2):
                    stt(qk[("q", d_)][:, gs], q32[:], hscale, SG[d_][:], ALU.mult, ALU.mult)
                for d_ in range(2):
                    act(SG[d_][:], Dd[d_][:], AF.Exp, scale=-1.0)
                for d_ in range(2):
                    tt(qk[("k", d_)][:, gs], Kt[d_][:], SG[d_][:], ALU.mult)
            for g in range(NG):
                bv = bank3(g, 4, 128)
                for t4 in range(4):
                    t = g * 4 + t4
                    for k in range(8):
                        mm(bv[:, t4, :], xnT[:, k, t * 128:(t + 1) * 128], wi[:, k, :], k == 0, k == 7)
                cpy(Vh[:, g * 4:(g + 1) * 4, :], bv[:], eng="dve")
            for d_ in range(2):
                kk = qk[("k", d_)]
                for half in range(2):
                    for i in range(8):
                        c = half * 8 + i
                        tr(psb[:, i, :], kk[:, c * 128:(c + 1) * 128], ident[:])
                    cpy(kTt[d_][:, half * 8:(half + 1) * 8, :], psb[:], eng="act")
                memset(St[d_][:], 0.0)
            for n_ in range(NT):
                for d_ in range(2):
                    c = n_ if d_ == 0 else NT - 1 - n_
                    ie_r, ie_tr = (1, 2) if d_ == 0 else (2, 1)
                    ub = bank(2 + 2 * (n_ % 2) + d_, 128)
                    mm(ub[:], kTt[d_][:, c, :], Vh[:, c, :], True, True)
                    act(Sbf[d_][:, c, :], St[d_][:], AF.Identity, scale=Esc[d_][:, ie_r, c:c + 1])
                    act(utmp[d_][:], ub[:], AF.Identity, scale=Esc[d_][:, ie_tr, c:c + 1])
                    stt(St[d_][:], St[d_][:], Esc[d_][:, 0, c:c + 1], utmp[d_][:], ALU.mult, ALU.add)
            kf, qf, kb, qb = qk[("k", 0)], qk[("q", 0)], qk[("k", 1)], qk[("q", 1)]

            def Amm(c):
                c0 = c * 128
                ab = bank3(6 + c % 2, 4, 128)
                mm(ab[0:64, 0, :], kf[:, c0:c0 + 64], qf[:, c0:c0 + 128], True, True)
                mm(ab[64:128, 0, 64:128], kf[:, c0 + 64:c0 + 128], qf[:, c0 + 64:c0 + 128], True, True)
                mm(ab[0:64, 1, 0:64], kb[:, c0:c0 + 64], qb[:, c0:c0 + 64], True, True)
                mm(ab[64:128, 1, :], kb[:, c0 + 64:c0 + 128], qb[:, c0:c0 + 128], True, True)
                for d_ in range(2):
                    cpred(AT[(c % 2) * 2 + d_][:], (maskf if d_ == 0 else maskb)[:], ab[:, d_, :])

            Amm(0)
            for c in range(NT):
                g = c // 4
                c4_ = c % 4
                gs = slice(g * 512, (g + 1) * 512)
                cs = slice(c * 128, (c + 1) * 128)
                if c + 1 < NT:
                    Amm(c + 1)
                ob = bank3(g % 2, 4, 128)
                mm(ob[:, c4_, :], Vh[:, c, :], AT[(c % 2) * 2 + 0][:], True, False)
                mm(ob[:, c4_, :], Vh[:, c, :], AT[(c % 2) * 2 + 1][:], False, False)
                mm(ob[:, c4_, :], Sbf[0][:, c, :], qf[:, cs], False, False)
                mm(ob[:, c4_, :], Sbf[1][:, c, :], qb[:, cs], False, True)
                if c4_ == 3:
                    obf = bank(g % 2)
                    act(sqb[:], obf[:], AF.Square)
                    mm(bank(2 + g % 2)[:], onesb[:], sqb[:], True, True)
                    act(rstd[:], bank(2 + g % 2)[:], AF.Ln, scale=1.0 / 128, bias=epsb[:, 0:1])
                    act(rstd[:], rstd[:], AF.Exp, scale=-0.5)
                    tt(tq[:], obf[:], rstd[:], ALU.mult)
                    stt(hoT[:, h, gs], tq[:], prm[:, P_HNW + l:P_HNW + l + 1], gateT[:, gs], ALU.mult, ALU.mult)
        sgt = q32
        for c in range(8):
            who = arena.load(dwho[l].rearrange("(h p) n -> p h n", p=128)[:, :, c * 128:(c + 1) * 128], 8, 128)
            wga = arena.load(wcols(wl, C_GA + 1024 + c * 128, 128), 8, 128)
            for g in range(NG):
                gs = slice(g * 512, (g + 1) * 512)
                yb_ = bank(g % 2)
                for hh in range(8):
                    mm(yb_[:], who[:, hh, :], hoT[:, hh, gs], hh == 0, hh == 7)
                gb = bank(2 + g % 2)
                for k in range(8):
                    mm(gb[:], wga[:, k, :], xnT[:, k, gs], k == 0, k == 7)
                act(sgt[:], gb[:], AF.Sigmoid)
                tt(tq[:], yb_[:], sgt[:], ALU.mult)
                tt(merged[:, c, gs], merged[:, c, gs], tq[:], ALU.add)
        if debug_stage == "hgrn" and l == 0:
            dump_merged(); break

        pool.reset()
        Ap = pool([NT, D], BF16)
        Bp = pool([NT, D], BF16)
        Zc = pool([8, 256], BF16)
        Zs = pool([8, 256], BF16)
        fuT = pool([2, 256], BF16)
        cgt = pool([2, 256], BF16)
        sgt2 = pool([2, 256], BF16)
        wfn = pool([8, D], BF16)
        arena.set(POOL_B + pool.cur, POOL_SZ - pool.cur)
        dma("pool", cgt[:], None, in_ap=dcg[0].rearrange("(c p) n -> p c n", p=128))
        dma("pool", sgt2[:], None, in_ap=dcg[1].rearrange("(c p) n -> p c n", p=128))
        dma("pool", wfn[:], None, in_ap=dwfn[l].rearrange("(c p) n -> p c n", p=128))
        Zc2 = [Zc, Zc]
        Zs2 = [Zs, Zs]
        fu2 = [fuT, pool([2, 256], BF16)]
        arena.set(POOL_B + pool.cur, POOL_SZ - pool.cur)
        units = [(tg, G) for tg in range(8) for G in range(4)]

        def FU(i):
            tg, G = units[i]
            tgs = slice(tg * 256, (tg + 1) * 256)
            wfu = arena.load(wcols(wl, C_FU + G * 256, 256), 8, 256)
            for cc in range(2):
                b_ = bank((i % 2) * 2 + cc, 256)
                for k in range(8):
                    mm(b_[:], wfu[:, k, cc * 128:(cc + 1) * 128], xnT[:, k, tgs], k == 0, k == 7)
                cpy(fu2[i % 2][:, cc, :], b_[:], eng="act")

        def ZZ(i):
            tg, G = units[i]
            for (tab, Z, bb) in ((cgt, Zc2[tg % 2], 4), (sgt2, Zs2[tg % 2], 6)):
                for c2 in range(2):
                    b_ = bank(bb + c2, 256)
                    for cc in range(2):
                        mm(b_[:], tab[:, cc, c2 * 128:(c2 + 1) * 128], fu2[i % 2][:, cc, :], cc == 0, cc == 1)
                    cpy(Z[:, G * 2 + c2, :], b_[:])

        def AB(tg):
            n_ = 0
            for t2 in range(2):
                t = tg * 2 + t2
                for (Z, dstp) in ((Zc2[tg % 2], Ap), (Zs2[tg % 2], Bp)):
                    for half in range(2):
                        b_ = bank(4 + n_ % 4)
                        n_ += 1
                        for c_ in range(8):
                            mm(b_[:], Z[:, c_, t2 * 128:(t2 + 1) * 128], wfn[:, c_, half * 512:(half + 1) * 512], c_ == 0, c_ == 7)
                        cpy(dstp[:, t, half * 512:(half + 1) * 512], b_[:], eng="act" if half == 0 else "dve")

        FU(0)
        for i in range(len(units)):
            if i + 1 < len(units):
                FU(i + 1)
            ZZ(i)
            if units[i][1] == 3:
                AB(units[i][0])
        pool.cur = 2 * NT * D * 2
        ysb = pool([4, 512], F32)
        ydm = [[pool([512], F32) for _ in range(2)] for _ in range(2)]
        sgd = [pool([512], F32) for _ in range(2)]
        sgm = [pool([512], F32) for _ in range(2)]
        tmr = [pool([512], F32) for _ in range(2)]
        arena.set(POOL_B + pool.cur, POOL_SZ - pool.cur)
        for kg in range(2):
            k0 = kg * 512
            ks = slice(k0, k0 + 512)
            if kg == 0:
                mlo, mhi, nm = 1537, 2048, 511
            else:
                mlo, mhi, nm = 1025, 1537, 512
            for half in range(2):
                for n in range(NT):
                    ct = arena.load(ddft[0, n * 128:(n + 1) * 128, ks].rearrange("p (c n) -> p c n", c=1), 1, 512)
                    st_ = arena.load(ddft[1, n * 128:(n + 1) * 128, ks].rearrange("p (c n) -> p c n", c=1), 1, 512)
                    for c4 in range(4):
                        c = half * 4 + c4
                        mm(bank(c4)[:], Ap[:, n, c * 128:(c + 1) * 128], ct[:, 0, :], n == 0, n == NT - 1)
                        mm(bank(4 + c4)[:], Bp[:, n, c * 128:(c + 1) * 128], st_[:, 0, :], n == 0, n == NT - 1)
                for c4 in range(4):
                    cpy(ysb[:, c4, :], bank(4 + c4)[:], eng="act")
                for c4 in range(4):
                    c = half * 4 + c4
                    i = c4 % 2
                    wga = arena.load(wcols(wl, C_GA + 2048 + c * 128, 128), 8, 128)
                    yd, ym = ydm[i]
                    tt(yd[:], bank(c4)[:], ysb[:, c4, :], ALU.add)
                    tt(ym[:], bank(c4)[:], ysb[:, c4, :], ALU.subtract)
                    gb = bank(4 + i * 2)
                    for k in range(8):
                        mm(gb[:], wga[:, k, :], xnT[:, k, ks], k == 0, k == 7)
                    act(sgd[i][:], gb[:], AF.Sigmoid)
                    gm = bank(5 + i * 2, nm)
                    for k in range(8):
                        mm(gm[:], wga[:, k, :], xnT[:, k, mlo:mhi], k == 0, k == 7)
                    act(sgm[i][:, 0:nm], gm[:], AF.Sigmoid)
                    tt(yd[:], yd[:], sgd[i][:], ALU.mult)
                    tt(merged[:, c, ks], merged[:, c, ks], yd[:], ALU.add)
                    rev = ym.ap[:, 511:0:-1] if kg == 0 else ym.ap[:, ::-1]
                    P.add("dve", lambda e, o=tmr[i][:, 0:nm], r=rev, g_=sgm[i][:, 0:nm]: e.tensor_tensor(out=o.ap, in0=r, in1=g_.ap, op=ALU.mult),
                          reads=[ym[:], sgm[i][:, 0:nm]], writes=[tmr[i][:, 0:nm]])
                    tt(merged[:, c, mlo:mhi], merged[:, c, mlo:mhi], tmr[i][:, 0:nm], ALU.add)
        c32 = [pool([NT, 32], BF16) for _ in range(2)]
        sg256 = pool([256], F32)
        t256 = pool([256], F32)
        for tb in range(2):
            dma("pool", c32[tb][:], None, in_ap=ddft[tb, :, S // 2:S // 2 + 32].rearrange("(n p) k -> p n k", p=128))
        b0 = bank3(0, 8, 32)
        b1 = bank3(1, 8, 32)
        for c in range(8):
            for n in range(NT):
                mm(b0[:, c, :], Ap[:, n, c * 128:(c + 1) * 128], c32[0][:, n, :], n == 0, False)
                mm(b0[:, c, :], Bp[:, n, c * 128:(c + 1) * 128], c32[1][:, n, :], False, n == NT - 1)
        for c in range(8):
            wga = arena.load(wcols(wl, C_GA + 2048 + c * 128, 128), 8, 128)
            for k in range(8):
                mm(b1[:, c, :], wga[:, k, :], xnT[:, k, S // 2:S // 2 + 32], k == 0, k == 7)
        act(sg256[:], bank(1, 256)[:], AF.Sigmoid)
        tt(t256[:], bank(0, 256)[:], sg256[:], ALU.mult)
        t83 = t256.ap.rearrange("p (c k) -> p c k", c=8)[:, :, 0:1]
        P.add("dve", lambda e: e.tensor_tensor(out=merged.ap[:, :, S // 2:S // 2 + 1], in0=merged.ap[:, :, S // 2:S // 2 + 1], in1=t83, op=ALU.add),
              reads=[merged[:, :, S // 2:S // 2 + 1], t256[:]], writes=[merged[:, :, S // 2:S // 2 + 1]])
        if debug_stage == "fft" and l == 0:
            dump_merged(); break

        pool.reset()
        nwb = pool([D], F32)
        dma("sp", nwb[:], None, in_ap=dnw[l * 4 + 1: l * 4 + 2, :].partition_broadcast(128))
        xb = [pool([D], F32) for _ in range(4)]
        yb = [pool([D], F32) for _ in range(4)]
        sc = pool([4, 8], F32)
        mb = [pool([8, 128], BF16) for _ in range(2)]
        wout = pool([8, D], BF16)
        dma("pool", wout[:], None, in_ap=dwout[l].rearrange("(c p) n -> p c n", p=128))
        def WO(t):
            cpy(mb[t % 2][:], merged[:, :, t * 128:(t + 1) * 128], eng="act")
            for hh in range(2):
                for c in range(8):
                    mm(bank((t % 2) * 2 + hh)[:], mb[t % 2][:, c, :], wout[:, c, hh * 512:(hh + 1) * 512], c == 0, c == 7)

        WO(0)
        for t in range(NT):
            if t + 1 < NT:
                WO(t + 1)
            residual_tile(t, [bank((t % 2) * 2)[:], bank((t % 2) * 2 + 1)[:]], nwb, xb, yb, sc, src_a, vxs)
        if debug_stage == "x1" and l == 0:
            break

        norm_phase(l, 2, vxs)
        ACT_B = MRG_B
        actT = sview(ACT_B, [22, S], BF16)
        fb = ACT_B + 22 * S * 2
        ffp = Bump(fb, BIGB - fb)
        wdn = ffp([22, D], BF16)
        hraw = [ffp([S + 2], F32) for _ in range(2)]
        hc = [ffp([S], F32) for _ in range(2)]
        for i in range(2):
            memset(hraw[i][:, 0:1], 0.0)
            memset(hraw[i][:, S + 1:S + 2], 0.0)
        arena.set(fb + ffp.cur, BIGB - fb - ffp.cur)
        dma("pool", wdn[:], None, in_ap=dwdn[l].rearrange("(f p) n -> p f n", p=128))
        wupl = dwup[l]
        for f in range(22):
            wgu = [arena.load(wcols(wupl, f * 128, 128), 8, 128), arena.load(wcols(wupl, DFF + f * 128, 128), 8, 128)]
            for i in range(2):
                for g in range(NG):
                    gs = slice(g * 512, (g + 1) * 512)
                    b_ = bank(i * 4 + g)
                    for k in range(8):
                        mm(b_[:], wgu[i][:, k, :], xnT[:, k, gs], k == 0, k == 7)
                ch = i * 22 + f
                cw = lambda j: prm[:, P_CW + (l * 3 + j) * 44 + ch: P_CW + (l * 3 + j) * 44 + ch + 1]
                cb_ = prm[:, P_CB + l * 44 + ch: P_CB + l * 44 + ch + 1]
                for g in range(NG):
                    gs = slice(g * 512, (g + 1) * 512)
                    b_ = bank(i * 4 + g)
                    cpy(hraw[i][:, 1 + g * 512: 1 + (g + 1) * 512], b_[:], eng="act")
                    act(hc[i][:, gs], b_[:], AF.Identity, scale=cw(1), bias=cb_)
                stt(hc[i][:], hraw[i][:, 0:S], cw(0), hc[i][:], ALU.mult, ALU.add)
                stt(hc[i][:], hraw[i][:, 2:S + 2], cw(2), hc[i][:], ALU.mult, ALU.add)
            act(hc[0][:], hc[0][:], AF.Gelu_apprx_tanh)
            tt(actT[:, f, :], hc[0][:], hc[1][:], ALU.mult)
        rp = Bump(fb + 22 * D * 2, BIGB - fb - 22 * D * 2)
        nwb = rp([D], F32)
        dma("sp", nwb[:], None, in_ap=dnw[l * 4 + 3: l * 4 + 4, :].partition_broadcast(128))
        xb = [rp([D], F32) for _ in range(3)]
        yb = [rp([D], F32) for _ in range(3)]
        sc = rp([3, 8], F32)
        last = (l == nlayers - 1)
        dst = vout if last else vxs
        def DN(t):
            for hh in range(2):
                for f in range(22):
                    mm(bank((t % 2) * 2 + hh)[:], actT[:, f, t * 128:(t + 1) * 128], wdn[:, f, hh * 512:(hh + 1) * 512], f == 0, f == 21)

        DN(0)
        for t in range(NT):
            if t + 1 < NT:
                DN(t + 1)
            iid = residual_tile(t, [bank((t % 2) * 2)[:], bank((t % 2) * 2 + 1)[:]], nwb, xb, yb, sc, vxs, dst)
            if last:
                out_dmas.append(iid)
    if out_dmas:
        P.add("sp", lambda e: e.nop(), extra_deps=out_dmas)

    P.finalize()
    sems_eng = {e: [es.enter_context(nc.semaphore("s_%s_%d" % (e, i))) for i in range(P.n_ms[e] // EPOCH + 1)] for e in ENGS}
    sems_dma = {k: es.enter_context(nc.semaphore("d_%s" % k)) for k in P.dma_count}
    block = es.enter_context(nc.Block())
    P.emit(sems_eng, sems_dma, block)
    es.close()
    return nc, P


def _constants():
    cst = np.zeros((128, 512), np.float32)
    cst[:, 0:128] = np.eye(128, dtype=np.float32)
    cst[:, 128:256] = 1.0
    s_ = np.arange(128)[:, None]; t_ = np.arange(128)[None, :]
    cst[:, 256:384] = (s_ <= t_).astype(np.float32)
    cst[:, 384:512] = (s_ >= t_).astype(np.float32)
    gam = 1.0 - 2.0 ** (-5.0 - np.arange(4, dtype=np.float64))
    m_ = np.arange(128)[:, None]; c_ = np.arange(3968)[None, :]
    rmask = np.stack([(256.0 ** -0.5) * g ** np.abs(c_ - 1920 - m_) for g in gam]).astype(np.float32)
    i_ = np.arange(256, dtype=np.float64)
    ang = 2 * np.pi * np.outer(i_, i_) / 256.0
    cg = np.stack([np.cos(ang), np.sin(ang)]).astype(np.float32) / 16.0
    n_ = np.arange(S, dtype=np.int64)
    nk = (np.outer(n_, n_) % S).astype(np.float64)
    ang = 2 * np.pi * nk / S
    dft = np.stack([np.cos(ang), -np.sin(ang)]).astype(np.float32) / np.float32(np.sqrt(S))
    dcol = np.ascontiguousarray(dft[:, :, S // 2].reshape(2, NT, 128).transpose(2, 0, 1))
    return cst, rmask, cg, dft, dcol


_CACHE = {}


def kernel(x, positions, norm_w, w_in, hgrn_lb_logits, hgrn_norm_w, w_ret_o, w_hgrn_o, w_fnet, w_out,
           w_up, conv_w, conv_b, w_down):
    if "nc" not in _CACHE:
        _CACHE["nc"] = build_program()[0]
        _CACHE["const"] = _constants()
    nc = _CACHE["nc"]
    cst, rmask, cg, dft, dcol = _CACHE["const"]
    f32 = np.float32
    prm = np.zeros((128, NPRM), f32)
    prm[:, P_LBL:P_LBL + 32] = np.asarray(hgrn_lb_logits, f32).reshape(2, DEPTH, 8, 128).transpose(3, 0, 1, 2).reshape(128, 32)
    prm[:, P_HNW:P_HNW + 2] = np.asarray(hgrn_norm_w, f32).T
    prm[:, P_CW:P_CW + 264] = np.asarray(conv_w, f32).reshape(DEPTH, 3, 44, 128).transpose(3, 0, 1, 2).reshape(128, 264)
    prm[:, P_CB:P_CB + 88] = np.asarray(conv_b, f32).reshape(DEPTH, 44, 128).transpose(2, 0, 1).reshape(128, 88)
    prm[:, P_IF] = (10000.0 ** (-np.arange(128, dtype=np.float32) / np.float32(128))).astype(f32)
    shared = dict(norm_w=np.ascontiguousarray(np.asarray(norm_w, f32).reshape(DEPTH * 4, D)),
                  w_in=np.asarray(w_in, f32), w_ret_o=np.asarray(w_ret_o, f32), w_hgrn_o=np.asarray(w_hgrn_o, f32),
                  w_fnet=np.asarray(w_fnet, f32), w_out=np.asarray(w_out, f32), w_up=np.asarray(w_up, f32),
                  w_down=np.asarray(w_down, f32), prm=prm, rmask=rmask, cg=cg, dft=dft, cst=cst, dcol=dcol)
    xs = np.asarray(x, f32)
    pos = np.asarray(positions, np.int32)
    in_maps = [dict(shared, x=np.ascontiguousarray(xs[b]), pos=np.ascontiguousarray(pos[b:b + 1])) for b in range(8)]
    res = run_bass_kernel_spmd(nc, in_maps, core_ids=list(range(8)))
    return np.stack([np.asarray(r["out"], f32) for r in res.results], axis=0)
```
